# Optimizing a Trainium2 kernel written in Bass

```python
import jax, jax.numpy as jnp
from jax import lax
import numpy as np

D_MODEL = 1024
BATCH = 8
SEQ = 2048
DEPTH = 1
DEC_BATCH = 32
DEC_SEQ = 32
PAST_LEN = 2048

CHUNK = 64
D_MIX = D_MODEL
RET_WIDTH = D_MIX // 2
RET_HEADS = 4
RET_DK = RET_WIDTH // RET_HEADS
RET_DV = RET_WIDTH // RET_HEADS
RET_ROPE_BASE = 10000.0
ATT_WIDTH = D_MIX - RET_WIDTH
ATT_HEADS = 8
ATT_DH = ATT_WIDTH // ATT_HEADS
ATT_ROT = ATT_DH // 4
IDX_HEADS = 8
IDX_DH = 64
IDX_ROT = IDX_DH // 4
TOPK_MAX = 256
ROPE_THETA = 500000.0
Q_BLOCK = 128
D_FF = 2816
LN_EPS = 1e-5
ALPHA = (2.0 * DEPTH) ** 0.25
BETA = (8.0 * DEPTH) ** -0.25
NEG = -1e30
IN_SIZES = (RET_WIDTH, RET_WIDTH, RET_HEADS * RET_DV, RET_HEADS * RET_DV,
            ATT_WIDTH, ATT_WIDTH, ATT_WIDTH, IDX_HEADS * IDX_DH, IDX_DH, IDX_HEADS)
IN_BETA = (1.0, 1.0, BETA, 1.0, 1.0, 1.0, BETA, 1.0, 1.0, 1.0)
D_IN = sum(IN_SIZES)

kernel_name = "hybrid_retention_dsa_streaming_encoder_step"

F32 = jnp.float32


def layer_norm(x, g, b):
    xf = x.astype(F32)
    mu = jnp.mean(xf, -1, keepdims=True)
    var = jnp.mean(jnp.square(xf - mu), -1, keepdims=True)
    return ((xf - mu) * lax.rsqrt(var + LN_EPS)).astype(x.dtype) * g + b


def head_norm(x):
    xf = x.astype(F32)
    mu = jnp.mean(xf, -1, keepdims=True)
    var = jnp.mean(jnp.square(xf - mu), -1, keepdims=True)
    return (xf - mu) * lax.rsqrt(var + LN_EPS)


def rope_angles(pos, rot_dim, base):
    inv = 1.0 / (base ** (jnp.arange(0, rot_dim, 2, dtype=F32) / rot_dim))
    ang = pos.astype(F32)[:, None] * inv[None, :]
    return jnp.cos(ang), jnp.sin(ang)


def apply_rope(x, cos, sin):
    r = 2 * cos.shape[-1]
    if x.ndim == 4:
        cos, sin = cos[:, None, :], sin[:, None, :]
    xf = x.astype(F32)
    x1, x2, xp = xf[..., : r // 2], xf[..., r // 2: r], xf[..., r:]
    return jnp.concatenate([x1 * cos - x2 * sin, x2 * cos + x1 * sin, xp], -1).astype(x.dtype)


def modulate(x, shift, scale):
    return x * (1.0 + scale[:, None, :]) + shift[:, None, :]


def post_norm(x, out, gate, weight, g, b):
    return layer_norm(ALPHA * x + weight * gate[:, None, :] * out, g, b)


def swiglu(h, w_gate, w_up, w_down):
    a = jnp.einsum('btd,df->btf', h, w_gate)
    u = jnp.einsum('btd,df->btf', h, w_up)
    return jnp.einsum('btf,fd->btd', jax.nn.silu(a) * u, w_down)


def mixer_inputs(h, pos, w_in):
    B, T, _ = h.shape
    z = jnp.einsum('btd,de->bte', h, w_in)
    split_points = [int(s) for s in np.cumsum(IN_SIZES)[:-1]]
    rq, rk, rv, rg, aq, ak, av, iq, ik, iw = jnp.split(z, split_points, axis=-1)
    rcos, rsin = rope_angles(pos, RET_DK, RET_ROPE_BASE)
    acos, asin = rope_angles(pos, ATT_ROT, ROPE_THETA)
    icos, isin = rope_angles(pos, IDX_ROT, ROPE_THETA)
    rq = apply_rope(rq.reshape(B, T, RET_HEADS, RET_DK), rcos, rsin)
    rk = apply_rope(rk.reshape(B, T, RET_HEADS, RET_DK), rcos, rsin) * (RET_DK ** -0.5)
    rv = rv.reshape(B, T, RET_HEADS, RET_DV)
    rg = rg.reshape(B, T, RET_HEADS, RET_DV)
    aq = apply_rope(aq.reshape(B, T, ATT_HEADS, ATT_DH), acos, asin)
    ak = apply_rope(ak.reshape(B, T, ATT_HEADS, ATT_DH), acos, asin)
    av = av.reshape(B, T, ATT_HEADS, ATT_DH)
    iq = apply_rope(iq.reshape(B, T, IDX_HEADS, IDX_DH), icos, isin)
    ik = apply_rope(ik, icos, isin)
    iw = iw * (IDX_HEADS ** -0.5)
    return rq, rk, rv, rg, aq, ak, av, iq, ik, iw


def retention_log_decay():
    return jnp.log(1.0 - 2.0 ** (-5.0 - jnp.arange(RET_HEADS, dtype=F32)))


def retention_chunk(S, q, k, v):
    lg = retention_log_decay()
    C = q.shape[1]
    n = jnp.arange(C, dtype=F32)
    rel = n[:, None] - n[None, :]
    D = jnp.where(rel[None] >= 0, jnp.exp(lg[:, None, None] * jnp.maximum(rel, 0.0)[None]), 0.0)
    qf, kf, vf = q.astype(F32), k.astype(F32), v.astype(F32)
    scores = jnp.einsum('bnhd,bmhd->bhnm', qf, kf) * D[None]
    inner = jnp.einsum('bhnm,bmhe->bnhe', scores, vf)
    cross = jnp.einsum('bnhd,bhde->bnhe', qf, S) * jnp.exp(lg[None, :] * (n[:, None] + 1.0))[None, :, :, None]
    decay_k = jnp.exp(lg[None, :] * (C - 1.0 - n)[:, None])
    S_new = jnp.exp(lg * C)[None, :, None, None] * S + jnp.einsum('bmhd,mh,bmhe->bhde', kf, decay_k, vf)
    return S_new, inner + cross


def retention_prompt(q, k, v):
    B, T, H, dk = q.shape
    nc = T // CHUNK

    def to_chunks(a):
        return a.reshape(B, nc, CHUNK, *a.shape[2:]).swapaxes(0, 1)

    S0 = jnp.zeros((B, H, dk, RET_DV), F32)
    S, o = lax.scan(lambda s, qkv: retention_chunk(s, *qkv), S0, (to_chunks(q), to_chunks(k), to_chunks(v)))
    return S, o.swapaxes(0, 1).reshape(B, T, H, RET_DV)


def dsa_attend(q, iq, iw, qpos, k, v, ik, topk):
    L = k.shape[1]
    s = jnp.einsum('bqhd,bsd->bqhs', iq.astype(F32), ik.astype(F32)) * (IDX_DH ** -0.5)
    index_score = jnp.einsum('bqh,bqhs->bqs', iw.astype(F32), jax.nn.relu(s))
    limit = (qpos // CHUNK + 1) * CHUNK
    admissible = jnp.arange(L, dtype=jnp.int32)[None, :] < limit[:, None]
    index_score = jnp.where(admissible[None], index_score, NEG)
    _, idx = lax.top_k(index_score, topk)
    valid = idx < limit[None, :, None]
    kg = jax.vmap(lambda kk, ii: kk[ii])(k, idx)
    vg = jax.vmap(lambda vv, ii: vv[ii])(v, idx)
    logits = jnp.einsum('bqhd,bqkhd->bqhk', q.astype(F32), kg.astype(F32)) * (ATT_DH ** -0.5)
    logits = jnp.where(valid[:, :, None, :], logits, NEG)
    p = jax.nn.softmax(logits, axis=-1)
    return jnp.einsum('bqhk,bqkhd->bqhd', p, vg.astype(F32)).astype(q.dtype)


def dsa_prompt(q, k, v, iq, ik, iw):
    B, T = q.shape[:2]
    nb = T // Q_BLOCK
    topk = min(TOPK_MAX, T // 4)

    def blocks(a):
        return a.reshape(B, nb, Q_BLOCK, *a.shape[2:]).swapaxes(0, 1)

    qpos = jnp.arange(T, dtype=jnp.int32).reshape(nb, Q_BLOCK)
    o = lax.map(lambda a: dsa_attend(a[0], a[1], a[2], a[3], k, v, ik, topk),
                (blocks(q), blocks(iq), blocks(iw), qpos))
    return o.swapaxes(0, 1).reshape(B, T, ATT_HEADS, ATT_DH)


def encoder_layer(x, c, w_cond, b_cond, f1_gate, f1_up, f1_down, ln1_g, ln1_b, w_in, w_out, ln2_g, ln2_b,
                  f2_gate, f2_up, f2_down, ln3_g, ln3_b, past):
    B, T, _ = x.shape
    mod = jnp.einsum('bd,de->be', jax.nn.silu(c), w_cond) + b_cond
    sh1, sc1, gt1, sh2, sc2, gt2, sh3, sc3, gt3 = jnp.split(mod, 9, axis=-1)
    x = post_norm(x, swiglu(modulate(x, sh1, sc1), f1_gate, f1_up, f1_down), 1.0 + gt1, 0.5, ln1_g, ln1_b)
    h = modulate(x, sh2, sc2)
    if past is None:
        pos = jnp.arange(T, dtype=jnp.int32)
    else:
        past_k, past_v, past_ik, ret_s0 = past
        pos = past_k.shape[1] + jnp.arange(T, dtype=jnp.int32)
    rq, rk, rv, rg, aq, ak, av, iq, ik, iw = mixer_inputs(h, pos, w_in)
    if past is None:
        ret_s, o_ret = retention_prompt(rq, rk, rv)
        o_att = dsa_prompt(aq, ak, av, iq, ik, iw)
    else:
        ret_s, o_ret = retention_chunk(ret_s0.astype(F32), rq, rk, rv)
        k_all = jnp.concatenate([past_k, ak], axis=1)
        v_all = jnp.concatenate([past_v, av], axis=1)
        ik_all = jnp.concatenate([past_ik, ik], axis=1)
        L = k_all.shape[1]
        o_att = dsa_attend(aq, iq, iw, pos, k_all, v_all, ik_all, min(TOPK_MAX, L // 4))
    o_ret = (head_norm(o_ret) * jax.nn.silu(rg.astype(F32))).astype(x.dtype).reshape(B, T, RET_WIDTH)
    mixed = jnp.concatenate([o_ret, o_att.reshape(B, T, ATT_WIDTH)], axis=-1)
    x = post_norm(x, jnp.einsum('btm,md->btd', mixed, w_out), 1.0 + gt2, 1.0, ln2_g, ln2_b)
    x = post_norm(x, swiglu(modulate(x, sh3, sc3), f2_gate, f2_up, f2_down), 1.0 + gt3, 0.5, ln3_g, ln3_b)
    return x, (ak, av, ik, ret_s.astype(x.dtype))


def setup_inputs(seed: int = 0) -> dict:
    key = jax.random.key(seed)
    ks = jax.random.split(key, 32)
    nrm = jax.random.normal
    d_in_scale = jnp.asarray(np.concatenate([np.full((s,), b, np.float32) for s, b in zip(IN_SIZES, IN_BETA)]))
    return {
        'x_prompt': nrm(ks[0], (BATCH, SEQ, D_MODEL), F32),
        'x_sample': nrm(ks[1], (DEC_BATCH, DEC_SEQ, D_MODEL), F32),
        'c_prompt': nrm(ks[2], (BATCH, D_MODEL), F32),
        'c_sample': nrm(ks[3], (DEC_BATCH, D_MODEL), F32),
        'cache_k': nrm(ks[4], (DEPTH, DEC_BATCH, PAST_LEN, ATT_HEADS, ATT_DH), F32),
        'cache_v': nrm(ks[5], (DEPTH, DEC_BATCH, PAST_LEN, ATT_HEADS, ATT_DH), F32),
        'cache_idx_k': nrm(ks[6], (DEPTH, DEC_BATCH, PAST_LEN, IDX_DH), F32),
        'state_ret': nrm(ks[7], (DEPTH, DEC_BATCH, RET_HEADS, RET_DK, RET_DV), F32),
        'w_cond': nrm(ks[8], (DEPTH, D_MODEL, 9 * D_MODEL), F32) * (0.1 * D_MODEL ** -0.5),
        'b_cond': nrm(ks[9], (DEPTH, 9 * D_MODEL), F32) * 0.01,
        'ffn1_w_gate': nrm(ks[10], (DEPTH, D_MODEL, D_FF), F32) * D_MODEL ** -0.5,
        'ffn1_w_up': nrm(ks[11], (DEPTH, D_MODEL, D_FF), F32) * D_MODEL ** -0.5,
        'ffn1_w_down': nrm(ks[12], (DEPTH, D_FF, D_MODEL), F32) * (BETA * D_FF ** -0.5),
        'ln1_g': 1.0 + 0.02 * nrm(ks[13], (DEPTH, D_MODEL), F32),
        'ln1_b': 0.02 * nrm(ks[14], (DEPTH, D_MODEL), F32),
        'w_in': nrm(ks[15], (DEPTH, D_MODEL, D_IN), F32) * (D_MODEL ** -0.5) * d_in_scale,
        'w_out': nrm(ks[16], (DEPTH, D_MIX, D_MODEL), F32) * (BETA * D_MIX ** -0.5),
        'ln2_g': 1.0 + 0.02 * nrm(ks[17], (DEPTH, D_MODEL), F32),
        'ln2_b': 0.02 * nrm(ks[18], (DEPTH, D_MODEL), F32),
        'ffn2_w_gate': nrm(ks[19], (DEPTH, D_MODEL, D_FF), F32) * D_MODEL ** -0.5,
        'ffn2_w_up': nrm(ks[20], (DEPTH, D_MODEL, D_FF), F32) * D_MODEL ** -0.5,
        'ffn2_w_down': nrm(ks[21], (DEPTH, D_FF, D_MODEL), F32) * (BETA * D_FF ** -0.5),
        'ln3_g': 1.0 + 0.02 * nrm(ks[22], (DEPTH, D_MODEL), F32),
        'ln3_b': 0.02 * nrm(ks[23], (DEPTH, D_MODEL), F32),
    }


def reference(x_prompt, x_sample, c_prompt, c_sample, cache_k, cache_v, cache_idx_k, state_ret,
              w_cond, b_cond, ffn1_w_gate, ffn1_w_up, ffn1_w_down, ln1_g, ln1_b, w_in, w_out, ln2_g, ln2_b,
              ffn2_w_gate, ffn2_w_up, ffn2_w_down, ln3_g, ln3_b):
    y_prompt, y_sample = x_prompt, x_sample
    st_p, st_s = [], []
    for l in range(DEPTH):
        params = (w_cond[l], b_cond[l], ffn1_w_gate[l], ffn1_w_up[l], ffn1_w_down[l], ln1_g[l], ln1_b[l],
                  w_in[l], w_out[l], ln2_g[l], ln2_b[l], ffn2_w_gate[l], ffn2_w_up[l], ffn2_w_down[l],
                  ln3_g[l], ln3_b[l])
        y_prompt, sp = encoder_layer(y_prompt, c_prompt, *params, past=None)
        y_sample, ss = encoder_layer(y_sample, c_sample, *params,
                                     past=(cache_k[l], cache_v[l], cache_idx_k[l], state_ret[l]))
        st_p.append(sp)
        st_s.append(ss)
    new_k_prompt = jnp.stack([s[0] for s in st_p])
    new_v_prompt = jnp.stack([s[1] for s in st_p])
    new_idx_k_prompt = jnp.stack([s[2] for s in st_p])
    state_ret_prompt = jnp.stack([s[3] for s in st_p])
    new_k_sample = jnp.stack([s[0] for s in st_s])
    new_v_sample = jnp.stack([s[1] for s in st_s])
    new_idx_k_sample = jnp.stack([s[2] for s in st_s])
    state_ret_sample = jnp.stack([s[3] for s in st_s])
    return (y_prompt, y_sample, new_k_prompt, new_v_prompt, new_idx_k_prompt, state_ret_prompt,
            new_k_sample, new_v_sample, new_idx_k_sample, state_ret_sample)
```

```python
import contextlib
import math
import numpy as np
import concourse.bass as bass
import concourse.mybir as mybir
from concourse.bass_utils import run_bass_kernel_spmd

F32 = mybir.dt.float32
BF16 = mybir.dt.bfloat16
AF = mybir.ActivationFunctionType
ALU = mybir.AluOpType

NT = 17
D = 1024
DFF = 2816
NFC = 22
DIN = 4168
ALPHA = 2.0 ** 0.25
LN_EPS = 1e-5
NEG = -1.0e30
NIT = 18
TOPK = 256
RET_G = [1.0 - 2.0 ** (-5.0 - h) for h in range(4)]
STOP_AFTER = None


class Ev:
    __slots__ = ("sem", "val", "key")

    def __init__(self, sem, val, key):
        self.sem, self.val, self.key = sem, val, key


class Res:
    __slots__ = ("name", "w", "rs")

    def __init__(self, name):
        self.name = name
        self.w = None
        self.rs = {}


class K:
    def __init__(self, nc, es):
        self.nc = nc
        self.eng = {"pe": nc.tensor, "act": nc.scalar, "dve": nc.vector, "pool": nc.gpsimd, "sp": nc.sync}
        self.sem = {}
        self.cnt = {}
        for e in ("pe", "act", "dve", "pool"):
            self.sem[e] = es.enter_context(nc.semaphore("sem_" + e))
            self.cnt[e] = 0
        self.waited = {e: {} for e in self.eng}
        self.ring = {}
        for q, depth in (("sp", 8), ("pool", 6), ("act", 4)):
            sems = [es.enter_context(nc.semaphore("dq_%s_%d" % (q, i))) for i in range(depth)]
            self.ring[q] = {"sems": sems, "k": 0, "tgt": [0] * depth}
        self.pending_pe = False

    def _wait(self, e, ev):
        if ev is None:
            return
        if e == "pe" and ev.key == "pe":
            return
        if self.waited[e].get(ev.key, 0) >= ev.val:
            return
        self.eng[e].wait_ge(ev.sem, ev.val)
        self.waited[e][ev.key] = ev.val

    def _deps(self, e, reads, writes):
        for r in reads:
            self._wait(e, r.w)
        for w in writes:
            self._wait(e, w.w)
            for ev in w.rs.values():
                self._wait(e, ev)

    def _mark(self, ev, reads, writes):
        for r in reads:
            old = r.rs.get(ev.key)
            if old is None or old.val < ev.val:
                r.rs[ev.key] = ev
        for w in writes:
            w.w = ev
            w.rs = {}

    def op(self, e, fn, reads=(), writes=(), signal=True):
        self._deps(e, reads, writes)
        ins = fn(self.eng[e])
        if signal:
            self.cnt[e] += 1
            ins.then_inc(self.sem[e], 1)
            ev = Ev(self.sem[e], self.cnt[e], e)
            if e == "pe":
                self.pending_pe = False
        else:
            assert e == "pe"
            ev = Ev(self.sem[e], self.cnt[e] + 1, e)
            self.pending_pe = True
        self._mark(ev, reads, writes)
        return ev

    def dma(self, q, out, in_, reads=(), writes=()):
        rg = self.ring[q]
        d = len(rg["sems"])
        slot = rg["k"] % d
        sem = rg["sems"][slot]
        key = "dq_%s_%d" % (q, slot)
        if rg["tgt"][slot] > 0:
            self._wait(q, Ev(sem, rg["tgt"][slot], key))
        self._deps(q, reads, writes)
        rg["tgt"][slot] += 16
        rg["k"] += 1
        self.eng[q].dma_start(out=out, in_=in_).then_inc(sem, 16)
        ev = Ev(sem, rg["tgt"][slot], key)
        self._mark(ev, reads, writes)
        return ev

    def barrier(self, engines=("pe", "act", "dve", "pool", "sp")):
        assert not self.pending_pe
        for e in engines:
            for p in ("pe", "act", "dve", "pool"):
                if self.cnt[p] > 0 and self.waited[e].get(p, 0) < self.cnt[p]:
                    self.eng[e].wait_ge(self.sem[p], self.cnt[p])
                    self.waited[e][p] = self.cnt[p]
            for q, rg in self.ring.items():
                for slot, sem in enumerate(rg["sems"]):
                    key = "dq_%s_%d" % (q, slot)
                    if rg["tgt"][slot] > 0 and self.waited[e].get(key, 0) < rg["tgt"][slot]:
                        self.eng[e].wait_ge(sem, rg["tgt"][slot])
                        self.waited[e][key] = rg["tgt"][slot]


def _consts():
    c = {}
    c["ident_f"] = np.eye(128, dtype=np.float32)
    pos = np.zeros((NT, 128), np.float32)
    for t in range(16):
        pos[t] = 128 * t + np.arange(128)
    pos[16] = 2048 + (np.arange(128) % 32)
    inv_r = (1.0 / (np.float32(10000.0) ** (np.arange(0, 128, 2, dtype=np.float32) / np.float32(128)))).astype(np.float32)
    ang = (pos[:, :, None] * inv_r[None, None, :]).astype(np.float32)
    c["rcos"] = np.cos(ang).astype(np.float32).transpose(1, 0, 2).copy()
    c["rsin"] = np.sin(ang).astype(np.float32).transpose(1, 0, 2).copy()
    inv_a = (1.0 / (np.float32(500000.0) ** (np.arange(0, 16, 2, dtype=np.float32) / np.float32(16)))).astype(np.float32)
    ang = (pos[:, :, None] * inv_a[None, None, :]).astype(np.float32)
    c["acos"] = np.cos(ang).astype(np.float32).transpose(1, 0, 2).copy()
    c["asin"] = np.sin(ang).astype(np.float32).transpose(1, 0, 2).copy()
    dec = np.zeros((128, 2, 8), np.float64)
    for kind in range(2):
        n = np.arange(128) if kind == 0 else (np.arange(128) % 32)
        for h in range(4):
            g = RET_G[h]
            dec[:, kind, h] = g ** (n + 1.0)
            dec[:, kind, 4 + h] = (g ** (-(n + 1.0))) * (128.0 ** -0.5)
    c["dec"] = dec.astype(np.float32)
    gc = np.zeros((128, 2, 4), np.float64)
    for h in range(4):
        gc[:, 0, h] = RET_G[h] ** 128.0
        gc[:, 1, h] = RET_G[h] ** 32.0
    c["gc"] = gc.astype(np.float32)
    m = np.arange(128)
    cm = (m[:, None] <= m[None, :]).astype(np.float32)
    cms = cm * ((m[:, None] // 32) == (m[None, :] // 32)).astype(np.float32)
    c["cmask"] = np.stack([cm, cms], axis=1).copy()
    rowm = np.zeros((128, 4), np.float32)
    for b in range(4):
        rowm[32 * b:32 * b + 32, b] = 1.0
    c["rowm"] = rowm
    sel = np.zeros((5, 2, 128), np.float32)
    sel[0, 0, :] = 1.0
    for b in range(4):
        sel[1 + b, 1, 32 * b:32 * b + 32] = 1.0
    c["sel"] = sel
    c["pow2"] = np.tile((2.0 ** -(np.arange(NIT + 2) + 1.0)).astype(np.float32)[None, :], (128, 1)).copy()
    return c


CONST_SHAPES = {
    "ident_f": [128, 128], "rcos": [128, NT, 64], "rsin": [128, NT, 64], "acos": [128, NT, 8], "asin": [128, NT, 8],
    "dec": [128, 2, 8], "gc": [128, 2, 4], "cmask": [128, 2, 128], "rowm": [128, 4], "sel": [5, 2, 128],
    "pow2": [128, NIT + 2],
}

IN_SHAPES = {
    "xin": [NT * 128, D], "c5": [5, D],
    "ck": [4, 2048, 512], "cv": [4, 2048, 512], "cik": [4, 2048, 64], "sret": [4, 4, 128, 128],
    "w_cond": [D, 9 * D], "b_cond": [72, 128],
    "f1g": [D, DFF], "f1u": [D, DFF], "f1d": [DFF, D], "ln1g": [1, D], "ln1b": [1, D],
    "w_in": [D, DIN], "w_out": [D, D], "ln2g": [1, D], "ln2b": [1, D],
    "f2g": [D, DFF], "f2u": [D, DFF], "f2d": [DFF, D], "ln3g": [1, D], "ln3b": [1, D],
}
OUT_SHAPES = {
    "y": [NT * 128, D], "newk": [NT * 128, 512], "newv": [NT * 128, 512], "newik": [NT * 128, 64],
    "stp": [4, 128, 128], "sts": [4, 4, 128, 128],
}


def build_program():
    nc = bass.Bass("TRN2", target_bir_lowering=False)
    A = {}
    for n, s in IN_SHAPES.items():
        A[n] = nc.dram_tensor(n, s, F32, kind="ExternalInput").ap()
    for n, s in CONST_SHAPES.items():
        A[n] = nc.dram_tensor("k_" + n, s, F32, kind="ExternalInput").ap()
    for n, s in OUT_SHAPES.items():
        A[n] = nc.dram_tensor(n, s, F32, kind="ExternalOutput").ap()
    if STOP_AFTER == "dsa":
        A["dbg"] = nc.dram_tensor("dbg", [NT * 128, 512], F32, kind="ExternalOutput").ap()
        A["dbgI"] = nc.dram_tensor("dbgI", [2, 128, 2080], F32, kind="ExternalOutput").ap()
        A["dbgM"] = nc.dram_tensor("dbgM", [2, 128, 2080], F32, kind="ExternalOutput").ap()

    with contextlib.ExitStack() as es:
        k = K(nc, es)
        _emit(nc, k, A, es)
    return nc


def _emit(nc, k, A, es0):
    uid = [0]

    def sb(es, name, shape, dt=F32):
        uid[0] += 1
        return es.enter_context(nc.sbuf_tensor("s%d_%s" % (uid[0], name), shape, dt))

    def ps(es, name, shape, dt=F32):
        uid[0] += 1
        return es.enter_context(nc.psum_tensor("p%d_%s" % (uid[0], name), shape, dt))

    def bc(ap, axis, shape):
        return ap.unsqueeze(axis).broadcast_to(shape)

    identf = sb(es0, "identf", [128, 128]); identf_r = Res("identf")
    identb = sb(es0, "identb", [128, 128], BF16); identb_r = Res("identb")
    MODT = sb(es0, "MODT", [128, 72, 5]); MODT_r = Res("MODT")
    SEL = sb(es0, "SEL", [5, 2, 128]); SEL_r = Res("SEL")
    ROWM = sb(es0, "ROWM", [128, 4]); ROWM_r = Res("ROWM")
    epsc = sb(es0, "epsc", [128, 1]); epsc_r = Res("epsc")
    XS = nc.dram_tensor("xs_scratch", [NT * 128, D], F32).ap()
    XS_r = [Res("XS%d" % t) for t in range(NT)]
    OATS = nc.dram_tensor("oat_scratch", [NT, 128, 512], BF16).ap()
    OATS_r = [Res("OATS%d" % t) for t in range(NT)]

    k.dma("sp", identf[:], A["ident_f"], writes=[identf_r])
    k.op("act", lambda e: e.copy(out=identb[:], in_=identf[:]), reads=[identf_r], writes=[identb_r])
    k.dma("sp", SEL[:], A["sel"], writes=[SEL_r])
    k.dma("sp", ROWM[:], A["rowm"], writes=[ROWM_r])
    k.op("dve", lambda e: e.memset(epsc[:], LN_EPS), writes=[epsc_r])

    def seq_cols(t):
        if t < 16:
            return [(0, 0, 128)]
        return [(1 + b, 32 * b, 32 * b + 32) for b in range(4)]

    def load_stage_consts(P, gidx, lng, lnb, gbps, gbps_r, GROW, GROW_r):
        GB, GB_r, LNG, LNG_r, LNB, LNB_r = P["GB"], P["GB_r"], P["LNG"], P["LNG_r"], P["LNB"], P["LNB_r"]
        k.dma("sp", LNG[:], A[lng].partition_broadcast(128), writes=[LNG_r])
        k.dma("sp", LNB[:], A[lnb].partition_broadcast(128), writes=[LNB_r])
        jg = (2, 5, 8)[gidx]
        for half in range(2):
            for q in range(4):
                c = half * 4 + q
                k.op("pe", lambda e, c=c, q=q, half=half: e.transpose(
                    gbps[half][0:5, q * 128:(q + 1) * 128], MODT[:, jg * 8 + c, :], identf[:]),
                    reads=[MODT_r, identf_r], writes=[gbps_r[half]], signal=(q == 3))
            k.op("act", lambda e, half=half: e.copy(out=GROW[:, half * 512:(half + 1) * 512], in_=gbps[half][0:5, :]),
                 reads=[gbps_r[half]], writes=[GROW_r])
        i = 0
        for kind in range(2):
            for half in range(2):
                g = gbps[i % 2]; gr = gbps_r[i % 2]; i += 1
                k.op("pe", lambda e, g=g, kind=kind, half=half: e.matmul(
                    g[:, :], lhsT=SEL[:, kind, :], rhs=GROW[:, half * 512:(half + 1) * 512], start=True, stop=True),
                    reads=[SEL_r, GROW_r], writes=[gr])
                k.op("act", lambda e, g=g, kind=kind, half=half: e.copy(out=GB[:, kind, half * 512:(half + 1) * 512], in_=g[:, :]),
                     reads=[gr], writes=[GB_r])

    def make_xmT(t, src_fn, src_r, jsh, jsc, tp, tp_r, dst_fn, dst_r):
        for half in range(2):
            p_ = tp[half]; pr = tp_r[half]
            for q in range(4):
                kc = half * 4 + q
                k.op("pe", lambda e, kc=kc, q=q, p_=p_: e.transpose(p_[:, q, :], src_fn(kc), identf[:]),
                     reads=[src_r, identf_r], writes=[pr], signal=(q == 3))
            for q in range(4):
                kc = half * 4 + q
                for (s, c0, c1) in seq_cols(t):
                    k.op("act", lambda e, kc=kc, q=q, s=s, c0=c0, c1=c1, p_=p_: e.activation(
                        out=dst_fn(kc, c0, c1), in_=p_[:, q, c0:c1], func=AF.Identity,
                        scale=MODT[:, jsc * 8 + kc, s:s + 1], bias=MODT[:, jsh * 8 + kc, s:s + 1]),
                        reads=[pr, MODT_r], writes=[dst_r])

    def post_norm_ln(P, t, yps, yps_r, T, T_r, st, st_r, mv, mv_r):
        X, XR = P["X"], P["XR"]
        GB, GB_r, LNG, LNG_r, LNB, LNB_r = P["GB"], P["GB_r"], P["LNG"], P["LNG_r"], P["LNB"], P["LNB_r"]
        kind = 0 if t < 16 else 1
        for half in range(2):
            k.op("dve", lambda e, half=half: e.tensor_tensor(
                out=T[:, half * 512:(half + 1) * 512], in0=yps[half][:, :], in1=GB[:, kind, half * 512:(half + 1) * 512],
                op=ALU.mult), reads=[yps_r[half], GB_r], writes=[T_r])
        k.op("dve", lambda e: e.scalar_tensor_tensor(out=T[:], in0=X[:, t, :], scalar=ALPHA, in1=T[:],
                                                     op0=ALU.mult, op1=ALU.add), reads=[XR[t], T_r], writes=[T_r])
        for half in range(2):
            k.op("dve", lambda e, half=half: e.bn_stats(out=st[:, half, :], in_=T[:, half * 512:(half + 1) * 512]),
                 reads=[T_r], writes=[st_r])
        k.op("dve", lambda e: e.bn_aggr(out=mv[:, 0:2], in_=st[:].rearrange("p a b -> p (a b)")), reads=[st_r], writes=[mv_r])
        k.op("act", lambda e: e.activation(out=mv[:, 2:3], in_=mv[:, 1:2], func=AF.Sqrt, bias=epsc[:, 0:1], scale=1.0),
             reads=[mv_r, epsc_r], writes=[mv_r])
        k.op("dve", lambda e: e.reciprocal(out=mv[:, 2:3], in_=mv[:, 2:3]), reads=[mv_r], writes=[mv_r])
        k.op("dve", lambda e: e.tensor_scalar(out=mv[:, 3:4], in0=mv[:, 0:1], scalar1=mv[:, 2:3], scalar2=-1.0,
                                              op0=ALU.mult, op1=ALU.mult), reads=[mv_r], writes=[mv_r])
        k.op("act", lambda e: e.activation(out=T[:], in_=T[:], func=AF.Identity, scale=mv[:, 2:3], bias=mv[:, 3:4]),
             reads=[T_r, mv_r], writes=[T_r])
        k.op("pool", lambda e: e.tensor_tensor(out=T[:], in0=T[:], in1=LNG[:], op=ALU.mult), reads=[T_r, LNG_r], writes=[T_r])
        k.op("pool", lambda e: e.tensor_tensor(out=X[:, t, :], in0=T[:], in1=LNB[:], op=ALU.add),
             reads=[T_r, LNB_r], writes=[XR[t]])

    def stage_ln_tiles(es, P):
        P["GB"] = sb(es, "GB", [128, 2, D]); P["GB_r"] = Res("GB")
        P["LNG"] = sb(es, "LNG", [128, D]); P["LNG_r"] = Res("LNG")
        P["LNB"] = sb(es, "LNB", [128, D]); P["LNB_r"] = Res("LNB")

    def cond_stage():
        with contextlib.ExitStack() as es:
            c5 = sb(es, "c5", [5, D]); c5_r = Res("c5")
            sc = sb(es, "sc", [5, D], BF16); sc_r = Res("sc")
            scT = sb(es, "scT", [128, 8, 5], BF16); scT_r = Res("scT")
            bcn = sb(es, "bc", [72, 128]); bc_r = Res("bc")
            bT = sb(es, "bT", [128, 72]); bT_r = Res("bT")
            WC = [sb(es, "WC%d" % i, [128, 8, D], BF16) for i in range(2)]
            WC_r = [Res("WC%d" % i) for i in range(2)]
            tps = ps(es, "tps", [128, 8, 8], BF16); tps_r = Res("tps")
            bps = ps(es, "bps", [128, 72]); bps_r = Res("bps")
            mps = ps(es, "mps", [128, 72, 5]); mps_r = Res("mps")
            k.dma("sp", c5[:], A["c5"], writes=[c5_r])
            k.dma("sp", bcn[:], A["b_cond"], writes=[bc_r])
            wc_v = A["w_cond"].rearrange("(kc p) n -> p kc n", p=128)
            for j in range(2):
                k.dma("pool", WC[j][:], wc_v[:, :, j * D:(j + 1) * D], writes=[WC_r[j]])
            k.op("act", lambda e: e.activation(out=sc[:], in_=c5[:], func=AF.Silu), reads=[c5_r], writes=[sc_r])
            for kc in range(8):
                k.op("pe", lambda e, kc=kc: e.transpose(tps[:, kc, 0:5], sc[:, kc * 128:(kc + 1) * 128], identb[0:5, 0:5]),
                     reads=[sc_r, identb_r], writes=[tps_r])
            k.op("act", lambda e: e.copy(out=scT[:], in_=tps[:, :, 0:5]), reads=[tps_r], writes=[scT_r])
            k.op("pe", lambda e: e.transpose(bps[:], bcn[:], identf[0:72, 0:72]), reads=[bc_r, identf_r], writes=[bps_r])
            k.op("act", lambda e: e.copy(out=bT[:], in_=bps[:]), reads=[bps_r], writes=[bT_r])
            for j in range(9):
                w = WC[j % 2]; wr = WC_r[j % 2]
                for c in range(8):
                    for kc in range(8):
                        k.op("pe", lambda e, c=c, kc=kc, w=w, j=j: e.matmul(
                            mps[:, j * 8 + c, :], lhsT=w[:, kc, c * 128:(c + 1) * 128], rhs=scT[:, kc, :],
                            start=(kc == 0), stop=(kc == 7)),
                            reads=[wr, scT_r], writes=[mps_r], signal=(kc == 7))
                if j + 2 < 9:
                    k.dma("pool", w[:], wc_v[:, :, (j + 2) * D:(j + 3) * D], writes=[wr])
            k.op("dve", lambda e: e.tensor_tensor(out=MODT[:], in0=mps[:], in1=bc(bT[:], 2, [128, 72, 5]), op=ALU.add),
                 reads=[mps_r, bT_r], writes=[MODT_r])
            for j in (1, 4, 7):
                k.op("dve", lambda e, j=j: e.tensor_scalar_add(out=MODT[:, j * 8:(j + 1) * 8, :], in0=MODT[:, j * 8:(j + 1) * 8, :],
                                                               scalar1=1.0), reads=[MODT_r], writes=[MODT_r])
            for j in (2, 5, 8):
                wgt = 1.0 if j == 5 else 0.5
                k.op("dve", lambda e, j=j, wgt=wgt: e.tensor_scalar(
                    out=MODT[:, j * 8:(j + 1) * 8, :], in0=MODT[:, j * 8:(j + 1) * 8, :], scalar1=1.0, scalar2=wgt,
                    op0=ALU.add, op1=ALU.mult), reads=[MODT_r], writes=[MODT_r])
            k.barrier()

    def ffn_stage(P, wg, wu, wd, jsh, jsc, gidx, lng, lnb, final, spill):
        X, XR = P["X"], P["XR"]
        with contextlib.ExitStack() as es:
            stage_ln_tiles(es, P)
            blocks = [list(range(0, 6)), list(range(6, 12)), list(range(12, 17))]
            WD = sb(es, "WD", [128, NFC, D], BF16); WD_r = [Res("WD%d" % i) for i in range(4)]
            WG = [sb(es, "WG%d" % i, [128, 8, 256], BF16) for i in range(2)]
            WU = [sb(es, "WU%d" % i, [128, 8, 256], BF16) for i in range(2)]
            WGU_r = [Res("WGU%d" % i) for i in range(2)]
            xmT = sb(es, "xmT", [128, 8, 768], BF16); xmT_r = Res("xmT")
            H = sb(es, "H", [128, NFC, 768], BF16); H_r = [Res("H%d" % c) for c in range(NFC)]
            S = [sb(es, "S%d" % i, [128, 512]) for i in range(2)]; S_r = [Res("S%d" % i) for i in range(2)]
            T = [sb(es, "T%d" % i, [128, D]) for i in range(2)]; T_r = [Res("T%d" % i) for i in range(2)]
            st = [sb(es, "st%d" % i, [128, 2, 6]) for i in range(2)]; st_r = [Res("st%d" % i) for i in range(2)]
            mv = [sb(es, "mv%d" % i, [128, 4]) for i in range(2)]; mv_r = [Res("mv%d" % i) for i in range(2)]
            tp = [ps(es, "tp%d" % i, [128, 4, 128]) for i in range(2)]; tp_r = [Res("tp%d" % i) for i in range(2)]
            pA = [ps(es, "pA%d" % i, [128, 512]) for i in range(2)]; pA_r = [Res("pA%d" % i) for i in range(2)]
            pB = [ps(es, "pB%d" % i, [128, 512]) for i in range(2)]; pB_r = [Res("pB%d" % i) for i in range(2)]
            pY = [ps(es, "pY%d" % i, [128, 512]) for i in range(2)]; pY_r = [Res("pY%d" % i) for i in range(2)]

            load_stage_consts(P, gidx, lng, lnb, pY, pY_r, T[0][0:5, :], T_r[0])
            wd_v = A[wd].rearrange("(c p) n -> p c n", p=128)
            wdq = [(0, 6), (6, 12), (12, 17), (17, 22)]
            wg_v = A[wg].rearrange("(kc p) n -> p kc n", p=128)
            wu_v = A[wu].rearrange("(kc p) n -> p kc n", p=128)
            groups = [(g * 2, 2) for g in range(11)]
            gcount = 0
            wd_loaded = False

            def load_group(gi_, slot):
                c0, n = groups[gi_]
                k.dma("pool", WG[slot][:, :, 0:n * 128], wg_v[:, :, c0 * 128:(c0 + n) * 128], writes=[WGU_r[slot]])
                k.dma("pool", WU[slot][:, :, 0:n * 128], wu_v[:, :, c0 * 128:(c0 + n) * 128], writes=[WGU_r[slot]])

            seqg = [(bi, gi_) for bi in range(len(blocks)) for gi_ in range(len(groups))]
            load_group(seqg[0][1], 0)
            load_group(seqg[1][1], 1)
            si = 0
            mm = 0
            ti = 0
            for bi, tiles in enumerate(blocks):
                ntok = 128 * len(tiles)
                if bi == 0:
                    for li, t in enumerate(tiles):
                        make_xmT(t, lambda kc, t=t: X[:, t, kc * 128:(kc + 1) * 128], XR[t], jsh, jsc, tp, tp_r,
                                 lambda kc, c0, c1, li=li: xmT[:, kc, li * 128 + c0:li * 128 + c1], xmT_r)
                subs = [(s0, min(512, ntok - s0)) for s0 in range(0, ntok, 512)]
                for gi_ in range(len(groups)):
                    slot = gcount % 2
                    c0, n = groups[gi_]
                    for cc in range(n):
                        c = c0 + cc
                        for (s0, sn) in subs:
                            a = pA[mm % 2]; ar = pA_r[mm % 2]; b_ = pB[mm % 2]; br = pB_r[mm % 2]; mm += 1
                            for kc in range(8):
                                k.op("pe", lambda e, kc=kc, cc=cc, a=a, s0=s0, sn=sn, slot=slot: e.matmul(
                                    a[:, 0:sn], lhsT=WG[slot][:, kc, cc * 128:(cc + 1) * 128], rhs=xmT[:, kc, s0:s0 + sn],
                                    start=(kc == 0), stop=(kc == 7)),
                                    reads=[WGU_r[slot], xmT_r], writes=[ar], signal=(kc == 7))
                            for kc in range(8):
                                k.op("pe", lambda e, kc=kc, cc=cc, b_=b_, s0=s0, sn=sn, slot=slot: e.matmul(
                                    b_[:, 0:sn], lhsT=WU[slot][:, kc, cc * 128:(cc + 1) * 128], rhs=xmT[:, kc, s0:s0 + sn],
                                    start=(kc == 0), stop=(kc == 7)),
                                    reads=[WGU_r[slot], xmT_r], writes=[br], signal=(kc == 7))
                            s_ = S[si % 2]; sr = S_r[si % 2]; si += 1
                            k.op("act", lambda e, a=a, s_=s_, sn=sn: e.activation(out=s_[:, 0:sn], in_=a[:, 0:sn], func=AF.Silu),
                                 reads=[ar], writes=[sr])
                            k.op("dve", lambda e, b_=b_, s_=s_, c=c, s0=s0, sn=sn: e.tensor_tensor(
                                out=H[:, c, s0:s0 + sn], in0=s_[:, 0:sn], in1=b_[:, 0:sn], op=ALU.mult),
                                reads=[sr, br], writes=[H_r[c]])
                    gcount += 1
                    if gcount + 1 < len(seqg):
                        load_group(seqg[gcount + 1][1], slot)
                    if not wd_loaded and gcount == 2:
                        for qi, (q0, q1) in enumerate(wdq):
                            k.dma("pool", WD[:, q0:q1, :], wd_v[:, q0:q1, :], writes=[WD_r[qi]])
                        wd_loaded = True
                for li, t in enumerate(tiles):
                    for half in range(2):
                        for c in range(NFC):
                            qi = [i for i, (q0, q1) in enumerate(wdq) if q0 <= c < q1][0]
                            k.op("pe", lambda e, c=c, half=half, li=li: e.matmul(
                                pY[half][:, :], lhsT=H[:, c, li * 128:(li + 1) * 128], rhs=WD[:, c, half * 512:(half + 1) * 512],
                                start=(c == 0), stop=(c == NFC - 1)),
                                reads=[H_r[c], WD_r[qi]], writes=[pY_r[half]], signal=(c == NFC - 1))
                    if bi + 1 < len(blocks) and li < len(blocks[bi + 1]):
                        tn = blocks[bi + 1][li]
                        make_xmT(tn, lambda kc, tn=tn: X[:, tn, kc * 128:(kc + 1) * 128], XR[tn], jsh, jsc, tp, tp_r,
                                 lambda kc, c0, c1, li=li: xmT[:, kc, li * 128 + c0:li * 128 + c1], xmT_r)
                    post_norm_ln(P, t, pY, pY_r, T[ti % 2], T_r[ti % 2], st[ti % 2], st_r[ti % 2], mv[ti % 2], mv_r[ti % 2])
                    if final:
                        k.dma("sp", A["y"][t * 128:(t + 1) * 128, :], X[:, t, :], reads=[XR[t]])
                    if spill:
                        k.dma("sp", XS[t * 128:(t + 1) * 128, :], X[:, t, :], reads=[XR[t]], writes=[XS_r[t]])
                    ti += 1
            k.barrier()

    def dsa_stage():
        with contextlib.ExitStack() as es:
            C_AQ, C_AK, C_IQ, C_IK, C_IK2, C_IW, C_AV = 0, 512, 1024, 1536, 1600, 1664, 1672
            WA = sb(es, "WA", [128, 8, 2184], BF16); WA_rs = [Res("WA%d" % i) for i in range(5)]
            wi_v = A["w_in"].rearrange("(kc p) n -> p kc n", p=128)
            for (dst, src, n, gi_) in ((C_AQ, 2048, 512, 0), (C_AK, 2560, 512, 1), (C_IQ, 3584, 512, 2), (C_IK, 4096, 64, 3),
                                       (C_IK2, 4096, 64, 3), (C_IW, 4160, 8, 3), (C_AV, 3072, 512, 4)):
                k.dma("pool", WA[:, :, dst:dst + n], wi_v[:, :, src:src + n], writes=[WA_rs[gi_]])
            ACOS = sb(es, "ACOS", [128, NT, 8]); ASIN = sb(es, "ASIN", [128, NT, 8]); AC_r = Res("AC")
            k.dma("sp", ACOS[:], A["acos"], writes=[AC_r])
            k.dma("sp", ASIN[:], A["asin"], writes=[AC_r])
            POW2 = sb(es, "POW2", [128, NIT + 2]); POW2_r = Res("POW2")
            k.dma("sp", POW2[:], A["pow2"], writes=[POW2_r])
            kT = sb(es, "kT", [128, 4, 2048], BF16); kT_r = [Res("kT%d" % j) for j in range(16)]
            Vaug = sb(es, "Vaug", [128, 16, 8, 65], BF16); V_r = [Res("V%d" % j) for j in range(16)]
            ikT2 = sb(es, "ikT2", [128, 5, 2048], BF16); ik_r = [[Res("ik%d_%d" % (b, j)) for j in range(16)] for b in range(5)]
            akTn = sb(es, "akTn", [128, 4, 128], BF16); akTn_r = Res("akTn")
            ikT2n = sb(es, "ikT2n", [128, 128], BF16); ikT2n_r = Res("ikT2n")
            Vn = sb(es, "Vn", [128, 8, 65], BF16); Vn_r = Res("Vn")
            iqTz = [sb(es, "iqTz%d" % b, [128, 4, 128], BF16) for b in range(4)]; iqTz_r = Res("iqTz")
            Mnew = sb(es, "Mnew", [128, 128], BF16); Mnew_r = Res("Mnew")
            MTn = sb(es, "MTn", [128, 128], BF16); MTn_r = Res("MTn")
            CI = sb(es, "CI", [128, 16, 64]); CI_r = Res("CI")
            CIb = sb(es, "CIb", [128, 16, 2, 64], BF16); CIb_r = Res("CIb")
            KS = [sb(es, "KS%d" % i, [128, 512]) for i in range(2)]; KS_r = [Res("KS%d" % i) for i in range(2)]
            VS = [sb(es, "VS%d" % i, [128, 512]) for i in range(2)]; VS_r = [Res("VS%d" % i) for i in range(2)]
            PTz = [[sb(es, "PTz%d_%d" % (p_, r), [128, 4, 128], BF16) for r in range(2)] for p_ in range(2)]
            PTz_r = [[Res("PTz%d_%d" % (p_, r)) for r in range(2)] for p_ in range(2)]
            XT = [sb(es, "XT%d" % i, [128, D]) for i in range(2)]; XT_r = [Res("XT%d" % i) for i in range(2)]
            XM = [sb(es, "XM%d" % i, [128, 8, 128], BF16) for i in range(2)]; XM_r = [Res("XM%d" % i) for i in range(2)]
            ZA = sb(es, "ZA", [128, 26, 64]); ZA_r = Res("ZA")
            ZAb = sb(es, "ZAb", [128, 26, 64], BF16); ZAb_r = Res("ZAb")
            RT = [sb(es, "RT%d" % i, [128, 26, 8]) for i in range(4)]; RT_r = [Res("RT%d" % i) for i in range(4)]
            ZV = sb(es, "ZV", [128, 512]); ZV_r = Res("ZV")
            IW = [sb(es, "IW%d" % i, [128, 8]) for i in range(2)]; IW_r = [Res("IW%d" % i) for i in range(2)]
            DIAG = sb(es, "DIAG", [128, 8, 128], BF16); DIAG_r = Res("DIAG")
            qT = [sb(es, "qT%d" % i, [128, 4, 128], BF16) for i in range(4)]; qT_r = [Res("qT%d" % i) for i in range(4)]
            iqT = [sb(es, "iqT%d" % i, [128, 4, 128], BF16) for i in range(2)]; iqT_r = [Res("iqT%d" % i) for i in range(2)]
            R = [sb(es, "R%d" % i, [128, 512], BF16) for i in range(4)]; R_r = [Res("R%d" % i) for i in range(4)]
            I = [sb(es, "I%d" % i, [128, 2080]) for i in range(2)]; I_r = [Res("I%d" % i) for i in range(2)]
            M = [sb(es, "M%d" % i, [128, 2080], BF16) for i in range(2)]; M_r = [Res("M%d" % i) for i in range(2)]
            MT = sb(es, "MT", [128, 16, 128], BF16); MT_r = Res("MT")
            BS = sb(es, "BS", [128, 8]); BS_r = Res("BS")
            HK = sb(es, "HK", [128, NIT + 2]); HK_r = Res("HK")
            E = [[sb(es, "E%d_%d" % (p_, r), [128, 4, 128], BF16) for r in range(2)] for p_ in range(2)]
            E_r = [[Res("E%d_%d" % (p_, r)) for r in range(2)] for p_ in range(2)]
            PT = [[sb(es, "PT%d_%d" % (p_, r), [128, 4, 128], BF16) for r in range(2)] for p_ in range(2)]
            PT_r = [[Res("PT%d_%d" % (p_, r)) for r in range(2)] for p_ in range(2)]
            RD = sb(es, "RD", [128, 2, 4]); RD_r = Res("RD")
            OATT = sb(es, "OATT", [128, 4, 2, 64], BF16); OATT_r = Res("OATT")
            OT = [sb(es, "OT%d" % i, [128, 4, 128], BF16) for i in range(2)]; OT_r = [Res("OT%d" % i) for i in range(2)]
            G0 = ps(es, "G0", [128, 512]); G0_r = Res("G0")
            G1 = ps(es, "G1", [128, 512]); G1_r = Res("G1")
            TB = ps(es, "TB", [128, 2, 4, 128], BF16); TB_r = [Res("TB0")] * 2
            TF = TB[:].rearrange("p a b c -> p (a b c)").bitcast(F32).rearrange("p (a b) -> p a b", b=128); TF_r = TB_r[0]
            SB2 = ps(es, "SB2", [128, 512]); SB2_r = Res("SB2")
            L = [ps(es, "L%d" % i, [128, 4, 128]) for i in range(2)]; L_r = [Res("L%d" % i) for i in range(2)]
            O = [ps(es, "O%d" % i, [128, 4, 65]) for i in range(2)]; O_r = [Res("O%d" % i) for i in range(2)]
            TFf = TB[:].rearrange("p a b c -> p (a b c)").bitcast(F32)
            G1v = G1[:, :].rearrange("p (a b) -> p a b", b=128)
            cnt = {"tb": 0, "r": 0, "e": 0, "ks": 0, "vs": 0}

            k.op("pool", lambda e: e.memset(Vaug[:, :, :, 64:65], 1.0), writes=V_r)
            k.op("pool", lambda e: e.memset(Vn[:, :, 64:65], 1.0), writes=[Vn_r])
            for b in range(4):
                k.op("pool", lambda e, b=b: e.memset(iqTz[b][:], 0.0), writes=[iqTz_r])

            def transposes_bf(src_fn, nblk, src_rs, dst_ap, dst_rs, evac):
                hb = cnt["tb"] % 2; cnt["tb"] += 1
                for q in range(nblk):
                    k.op("pe", lambda e, q=q, hb=hb: e.transpose(TB[:, hb, q, :], src_fn(q), identb[:]),
                         reads=list(src_rs) + [identb_r], writes=[TB_r[hb]], signal=(q == nblk - 1))
                src = TB[:, hb, 0:nblk, :] if nblk > 1 else TB[:, hb, 0, :]
                if evac == "act":
                    k.op("act", lambda e: e.copy(out=dst_ap, in_=src), reads=[TB_r[hb]], writes=list(dst_rs))
                else:
                    k.op("dve", lambda e: e.tensor_copy(out=dst_ap, in_=src), reads=[TB_r[hb]], writes=list(dst_rs))

            def S1x(t):
                smp = (t == 16)
                iq_ = iqT[t % 2]; iq_r = iqT_r[t % 2]; iw_ = IW[t % 2]; iw_r = IW_r[t % 2]
                xt = XT[t % 2]; xt_r = XT_r[t % 2]
                k.dma("sp", xt[:], XS[t * 128:(t + 1) * 128, :], reads=[XS_r[t]], writes=[xt_r])
                xm = XM[t % 2]; xm_r = XM_r[t % 2]
                make_xmT(t, lambda kc: xt[:, kc * 128:(kc + 1) * 128], xt_r, 3, 4, [TF, TF], [TF_r, TF_r],
                         lambda kc, c0, c1: xm[:, kc, c0:c1], xm_r)
                yield
                for gi, (c0, n) in enumerate(((C_AQ, 512), (C_AK, 512), (C_IQ, 512), (C_IK, 136), (C_AV, 512))):
                    g, gr = TFf, TF_r
                    for kc in range(8):
                        k.op("pe", lambda e, kc=kc, g=g, c0=c0, n=n: e.matmul(
                            g[:, 0:n], lhsT=xm[:, kc, :], rhs=WA[:, kc, c0:c0 + n], start=(kc == 0), stop=(kc == 7)),
                            reads=[xm_r, WA_rs[gi]], writes=[gr], signal=(kc == 7))
                    if gi < 3:
                        k.op("act", lambda e, g=g, gi=gi: e.copy(
                            out=ZA[:, 8 * gi:8 * gi + 8, :], in_=g[:, 0:512].rearrange("p (h d) -> p h d", d=64)),
                            reads=[gr], writes=[ZA_r])
                    elif gi == 3:
                        k.op("act", lambda e, g=g: e.copy(
                            out=ZA[:, 24:26, :], in_=g[:, 0:128].rearrange("p (h d) -> p h d", d=64)),
                            reads=[gr], writes=[ZA_r])
                        k.op("act", lambda e, g=g: e.mul(out=iw_[:], in_=g[:, 128:136], mul=8.0 ** -0.5),
                             reads=[gr], writes=[iw_r])
                    else:
                        k.op("act", lambda e, g=g: e.copy(out=ZV[:], in_=g[:, 0:512]), reads=[gr], writes=[ZV_r])
                    yield
                k.dma("sp", A["newv"][t * 128:(t + 1) * 128, :], ZV[:], reads=[ZV_r])
                if not smp:
                    k.op("pool", lambda e: e.tensor_copy(out=Vaug[:, t, :, 0:64], in_=ZV[:].rearrange("p (h d) -> p h d", d=64)),
                         reads=[ZV_r], writes=[V_r[t]])
                else:
                    k.op("pool", lambda e: e.tensor_copy(out=Vn[:, :, 0:64], in_=ZV[:].rearrange("p (h d) -> p h d", d=64)),
                         reads=[ZV_r], writes=[Vn_r])
                cosb = bc(ACOS[:, t, :], 1, [128, 26, 8]); sinb = bc(ASIN[:, t, :], 1, [128, 26, 8])
                x1 = ZA[:, :, 0:8]; x2 = ZA[:, :, 8:16]
                k.op("pool", lambda e: e.tensor_tensor(out=RT[0][:], in0=x1, in1=cosb, op=ALU.mult), reads=[ZA_r, AC_r], writes=[RT_r[0]])
                k.op("pool", lambda e: e.tensor_tensor(out=RT[1][:], in0=x2, in1=sinb, op=ALU.mult), reads=[ZA_r, AC_r], writes=[RT_r[1]])
                k.op("pool", lambda e: e.tensor_tensor(out=RT[2][:], in0=x2, in1=cosb, op=ALU.mult), reads=[ZA_r, AC_r], writes=[RT_r[2]])
                k.op("pool", lambda e: e.tensor_tensor(out=RT[3][:], in0=x1, in1=sinb, op=ALU.mult), reads=[ZA_r, AC_r], writes=[RT_r[3]])
                k.op("pool", lambda e: e.tensor_tensor(out=x1, in0=RT[0][:], in1=RT[1][:], op=ALU.subtract),
                     reads=[RT_r[0], RT_r[1]], writes=[ZA_r])
                k.op("pool", lambda e: e.tensor_tensor(out=x2, in0=RT[2][:], in1=RT[3][:], op=ALU.add),
                     reads=[RT_r[2], RT_r[3]], writes=[ZA_r])
                k.dma("sp", A["newk"][t * 128:(t + 1) * 128, :], ZA[:, 8:16, :].rearrange("p h d -> p (h d)"), reads=[ZA_r])
                k.dma("sp", A["newik"][t * 128:(t + 1) * 128, :], ZA[:, 24, :], reads=[ZA_r])
                yield
                k.op("pool", lambda e: e.tensor_copy(out=ZAb[:], in_=ZA[:]), reads=[ZA_r], writes=[ZAb_r])
                yield
                zb = ZAb[:].rearrange("p h d -> p (h d)")
                q_ = qT[t % 4]
                transposes_bf(lambda q: zb[:, q * 128:(q + 1) * 128], 4, [ZAb_r], q_[:], [qT_r[t % 4]], "act")
                if not smp:
                    transposes_bf(lambda q: zb[:, 512 + q * 128:512 + (q + 1) * 128], 4, [ZAb_r],
                                  kT[:, :, t * 128:(t + 1) * 128], [kT_r[t]], "act")
                else:
                    transposes_bf(lambda q: zb[:, 512 + q * 128:512 + (q + 1) * 128], 4, [ZAb_r], akTn[:], [akTn_r], "act")
                transposes_bf(lambda q: zb[:, 1024 + q * 128:1024 + (q + 1) * 128], 4, [ZAb_r], iq_[:], [iq_r], "act")
                if not smp:
                    transposes_bf(lambda q: zb[:, 1536:1664], 1, [ZAb_r], ikT2[:, 0, t * 128:(t + 1) * 128], [ik_r[0][t]], "act")
                else:
                    transposes_bf(lambda q: zb[:, 1536:1664], 1, [ZAb_r], ikT2n[:], [ikT2n_r], "act")
                    for b in range(4):
                        k.op("pool", lambda e, b=b: e.tensor_copy(out=iqTz[b][:, :, 32 * b:32 * b + 32], in_=iq_[:, :, 32 * b:32 * b + 32]),
                             reads=[iq_r], writes=[iqTz_r])
                    for b in range(4):
                        k.dma("sp", CI[:], A["cik"][b].rearrange("(kt p) d -> p kt d", p=128), writes=[CI_r])
                        k.op("pool", lambda e: e.tensor_copy(out=CIb[:, :, 0, :], in_=CI[:]), reads=[CI_r], writes=[CIb_r])
                        k.op("pool", lambda e: e.tensor_copy(out=CIb[:, :, 1, :], in_=CI[:]), reads=[CI_r], writes=[CIb_r])
                        for k4 in range(4):
                            transposes_bf(lambda q, k4=k4: CIb[:, k4 * 4 + q, :, :].rearrange("p a d -> p (a d)"), 4, [CIb_r],
                                          ikT2[:, b + 1, k4 * 512:(k4 + 1) * 512].rearrange("p (q n) -> p q n", n=128),
                                          [ik_r[b + 1][k4 * 4 + q] for q in range(4)], "act")
                yield

            def S1s(t):
                smp = (t == 16)
                I_ = I[t % 2]; Ir = I_r[t % 2]
                iq_ = iqT[t % 2]; iq_r = iqT_r[t % 2]; iw_ = IW[t % 2]; iw_r = IW_r[t % 2]
                k.op("pool", lambda e: e.tensor_tensor(out=DIAG[:], in0=bc(identb[:], 1, [128, 8, 128]),
                                                      in1=bc(iw_[:], 2, [128, 8, 128]), op=ALU.mult),
                     reads=[identb_r, iw_r], writes=[DIAG_r])
                nk = 2080 if smp else 128 * (t + 1)
                chunks = [(c0, min(512, 2048 - c0) if smp else min(512, nk - c0)) for c0 in range(0, 2048 if smp else nk, 512)]
                if smp:
                    chunks.append((2048, 32))
                pend = None
                for (c0, n) in chunks:
                    for hp in range(4):
                        items = []
                        for h in (2 * hp, 2 * hp + 1):
                            par = h % 2
                            g, gr = ((G0, G0_r), (SB2, SB2_r))[h % 2]
                            rows = slice(64 * par, 64 * par + 64)
                            if not smp:
                                k.op("pe", lambda e, g=g, h=h, c0=c0, n=n, rows=rows: e.matmul(
                                    g[:, 0:n], lhsT=iq_[rows, h // 2, :], rhs=ikT2[rows, 0, c0:c0 + n], start=True, stop=True),
                                    reads=[iq_r] + [ik_r[0][j] for j in range(c0 // 128, (c0 + n) // 128)], writes=[gr])
                            else:
                                for b in range(4):
                                    if c0 < 2048:
                                        rhs = ikT2[rows, b + 1, c0:c0 + n]
                                        rr = [ik_r[b + 1][j] for j in range(c0 // 128, (c0 + n) // 128)]
                                    else:
                                        rhs = ikT2n[rows, 32 * b:32 * b + 32]
                                        rr = [ikT2n_r]
                                    k.op("pe", lambda e, g=g, h=h, b=b, n=n, rows=rows, rhs=rhs: e.matmul(
                                        g[:, 0:n], lhsT=iqTz[b][rows, h // 2, :], rhs=rhs, start=(b == 0), stop=(b == 3)),
                                        reads=[iqTz_r] + rr, writes=[gr], signal=(b == 3))
                            items.append((h, g, gr))
                        nxt = []
                        for (h, g, gr) in items:
                            r_ = R[cnt["r"] % 4]; rr_ = R_r[cnt["r"] % 4]; cnt["r"] += 1
                            k.op("act", lambda e, g=g, r_=r_, n=n: e.activation(out=r_[:, 0:n], in_=g[:, 0:n], func=AF.Relu, scale=0.125),
                                 reads=[gr], writes=[rr_])
                            nxt.append((h, r_, rr_))
                        if pend is not None:
                            pend()
                        yield
                        def pend(nxt=nxt, n=n, c0=c0):
                            for (h, r_, rr_) in nxt:
                                k.op("pe", lambda e, h=h, r_=r_: e.matmul(
                                    G1[:, 0:n], lhsT=DIAG[:, h, :], rhs=r_[:, 0:n], start=(h == 0), stop=(h == 7)),
                                    reads=[DIAG_r, rr_], writes=[G1_r], signal=(h == 7))
                                if h == 7:
                                    k.op("dve", lambda e: e.tensor_copy(out=I_[:, c0:c0 + n], in_=G1[:, 0:n]), reads=[G1_r], writes=[Ir])
                pend()
                yield

            def S1b(t):
                smp = (t == 16)
                nk = 2080 if smp else 128 * (t + 1)
                m_ = M[t % 2]; m_r = M_r[t % 2]
                I_ = I[t % 2]; Ir = I_r[t % 2]
                if t >= 2:
                    k.op("dve", lambda e: e.tensor_reduce(out=BS[:, 0:1], in_=I_[:, 0:nk], axis=mybir.AxisListType.X, op=ALU.max),
                         reads=[Ir], writes=[BS_r])
                    k.op("dve", lambda e: e.tensor_reduce(out=BS[:, 1:2], in_=I_[:, 0:nk], axis=mybir.AxisListType.X, op=ALU.min),
                         reads=[Ir], writes=[BS_r])
                    if not smp:
                        k.op("pool", lambda e: e.memset(I_[0:64, nk - 64:nk], NEG), reads=[BS_r], writes=[Ir])
                    k.op("dve", lambda e: e.tensor_tensor(out=BS[:, 2:3], in0=BS[:, 0:1], in1=BS[:, 1:2], op=ALU.subtract),
                         reads=[BS_r], writes=[BS_r])
                    k.op("dve", lambda e: e.tensor_scalar(out=HK[:], in0=POW2[:], scalar1=BS[:, 2:3], scalar2=None, op0=ALU.mult),
                         reads=[POW2_r, BS_r], writes=[HK_r])
                    k.op("dve", lambda e: e.tensor_tensor(out=BS[:, 3:4], in0=BS[:, 1:2], in1=HK[:, 0:1], op=ALU.add),
                         reads=[BS_r, HK_r], writes=[BS_r])
                    yield
                    for kk in range(NIT):
                        k.op("dve", lambda e: e.tensor_scalar(out=m_[:, 0:nk], in0=I_[:, 0:nk], scalar1=BS[:, 3:4], scalar2=None,
                                                              op0=ALU.is_gt, op1=ALU.add, accum_out=BS[:, 4:5]),
                             reads=[Ir, BS_r], writes=[m_r, BS_r])
                        k.op("dve", lambda e, kk=kk: e.tensor_scalar(out=BS[:, 5:6], in0=BS[:, 4:5], scalar1=TOPK - 0.5,
                                                                     scalar2=HK[:, kk:kk + 1], op0=ALU.is_gt, op1=ALU.mult),
                             reads=[BS_r, HK_r], writes=[BS_r])
                        sub = kk + 1 if kk < NIT - 1 else kk
                        dst = BS[:, 3:4] if kk < NIT - 1 else BS[:, 6:7]
                        k.op("dve", lambda e, sub=sub, dst=dst: e.scalar_tensor_tensor(
                            out=dst, in0=BS[:, 5:6], scalar=HK[:, sub:sub + 1], in1=BS[:, 3:4], op0=ALU.subtract, op1=ALU.add),
                            reads=[BS_r, HK_r], writes=[BS_r])
                        yield
                    k.op("dve", lambda e: e.tensor_scalar(out=m_[:, 0:nk], in0=I_[:, 0:nk], scalar1=BS[:, 6:7], scalar2=None,
                                                          op0=ALU.is_gt), reads=[Ir, BS_r], writes=[m_r])
                else:
                    k.op("pool", lambda e: e.memset(I_[0:64, nk - 64:nk], NEG), writes=[Ir])
                    k.op("dve", lambda e: e.tensor_scalar(out=m_[:, 0:nk], in0=I_[:, 0:nk], scalar1=-1.0e29, scalar2=None,
                                                          op0=ALU.is_gt), reads=[Ir], writes=[m_r])
                if STOP_AFTER == "dsa" and t in (3, 16):
                    sl = 0 if t == 3 else 1
                    k.dma("sp", A["dbgI"][sl, :, 0:nk], I_[:, 0:nk], reads=[Ir])
                    k.dma("pool", A["dbgM"][sl, :, 0:nk], m_[:, 0:nk], reads=[m_r])
                yield

            def attend_step(b, m_src, kt_fn, kt_rs, v_fn, v_rs, q_, q_r, first, last, par=None, lb=None, lb_r=None):
                cs = slice(0, 128) if b is None else slice(32 * b, 32 * b + 32)
                ncol = 128 if b is None else 32
                st_ = {}

                def front():
                    for i in range(4):
                        for par_ in range(2):
                            rows = slice(64 * par_, 64 * par_ + 64)
                            k.op("pe", lambda e, i=i, par_=par_, rows=rows: e.matmul(
                                L[par_][:, i, 0:ncol], lhsT=kt_fn(rows, i), rhs=q_[rows, i, cs], start=True, stop=True),
                                reads=list(kt_rs) + [q_r], writes=[L_r[par_]], signal=(i == 3))
                    ps_ = []
                    for par_ in range(2):
                        ei = cnt["e%d" % par_] % 2; cnt["e%d" % par_] += 1
                        e_ = E[par_][ei]; er = E_r[par_][ei]
                        k.op("act", lambda e, par_=par_, e_=e_: e.activation(out=e_[:, :, 0:ncol], in_=L[par_][:, :, 0:ncol],
                                                                            func=AF.Exp, scale=0.125),
                             reads=[L_r[par_]], writes=[er])
                        if b is None:
                            p_ = PT[par_][ei]; pr = PT_r[par_][ei]
                        else:
                            p_ = PTz[par_][ei]; pr = PTz_r[par_][ei]
                        k.op("pool", lambda e, e_=e_, p_=p_: e.tensor_tensor(
                            out=p_[:, :, cs], in0=e_[:, :, 0:ncol], in1=bc(m_src, 1, [128, 4, ncol]), op=ALU.mult),
                            reads=[er, MT_r, MTn_r], writes=[pr])
                        ps_.append((p_, pr))
                    st_["p"] = ps_

                def pv():
                    for par_ in range(2):
                        p_, pr = st_["p"][par_]
                        for i in range(4):
                            h = 2 * i + par_
                            k.op("pe", lambda e, i=i, h=h, par_=par_, p_=p_: e.matmul(
                                O[par_][:, i, :], lhsT=p_[:, i, :], rhs=v_fn(h), start=(first and i == 0), stop=last,
                                skip_group_check=True),
                                reads=[pr] + list(v_rs), writes=[O_r[par_]], signal=(i == 3))
                return [front, pv, None]

            def run_steps(steps):
                prev = None
                for st_ in steps:
                    st_[0]()
                    if prev is not None:
                        prev[1]()
                        if prev[2] is not None:
                            prev[2]()
                    prev = st_
                    yield
                if prev is not None:
                    prev[1]()
                    if prev[2] is not None:
                        prev[2]()
                yield

            def finalize(t):
                for par in range(2):
                    k.op("dve", lambda e, par=par: e.reciprocal(out=RD[:, par, :], in_=O[par][:, :, 64]), reads=[O_r[par]], writes=[RD_r])
                    k.op("dve", lambda e, par=par: e.tensor_tensor(
                        out=OATT[:, :, par, :], in0=O[par][:, :, 0:64], in1=bc(RD[:, par, :], 2, [128, 4, 64]), op=ALU.mult),
                        reads=[O_r[par], RD_r], writes=[OATT_r])
                of = OATT[:].rearrange("p a b d -> p (a b d)")
                o_ = OT[t % 2]; o_r = OT_r[t % 2]
                transposes_bf(lambda q: of[:, q * 128:(q + 1) * 128], 4, [OATT_r], o_[:], [o_r], "act")
                k.dma("sp", OATS[t], o_[:].rearrange("p a b -> p (a b)"), reads=[o_r], writes=[OATS_r[t]])
                if STOP_AFTER == "dsa":
                    k.dma("pool", A["dbg"][t * 128:(t + 1) * 128, :], of, reads=[OATT_r])

            def S2(t):
                m_ = M[t % 2]; m_r = M_r[t % 2]
                q_ = qT[t % 4]; q_r = qT_r[t % 4]
                if t < 16:
                    for j0 in range(0, t + 1, 4):
                        nb = min(4, t + 1 - j0)
                        transposes_bf(lambda q, j0=j0: m_[:, (j0 + q) * 128:(j0 + q + 1) * 128], nb, [m_r],
                                      MT[:, j0:j0 + nb, :] if nb > 1 else MT[:, j0, :], [MT_r], "act")
                    yield
                    steps = []
                    for j in range(t + 1):
                        steps.append(attend_step(None, MT[:, j, :], lambda rows, i, j=j: kT[rows, i, j * 128:(j + 1) * 128], [kT_r[j]],
                                                 lambda h, j=j: Vaug[:, j, h, :], [V_r[j]], q_, q_r, (j == 0), (j == t)))
                    yield from run_steps(steps)
                    finalize(t)
                    yield
                    return
                for j0 in range(0, 16, 4):
                    transposes_bf(lambda q, j0=j0: m_[:, (j0 + q) * 128:(j0 + q + 1) * 128], 4, [m_r], MT[:, j0:j0 + 4, :], [MT_r], "act")
                for b in range(4):
                    k.op("pool", lambda e, b=b: e.tensor_scalar(out=Mnew[:, 32 * b:32 * b + 32], in0=m_[:, 2048:2080],
                                                                scalar1=ROWM[:, b:b + 1], scalar2=None, op0=ALU.mult),
                         reads=[m_r, ROWM_r], writes=[Mnew_r])
                transposes_bf(lambda q: Mnew[:], 1, [Mnew_r], MTn[:], [MTn_r], "act")
                def prep(b, kt):
                    ks = KS[cnt["ks"] % 2]; ksr = KS_r[cnt["ks"] % 2]; cnt["ks"] += 1
                    k.dma("sp", ks[:], A["ck"][b, kt * 128:(kt + 1) * 128, :], writes=[ksr])
                    for q in range(4):
                        k.op("pe", lambda e, q=q, ks=ks: e.transpose(TF[:, q, :], ks[:, q * 128:(q + 1) * 128], identf[:]),
                             reads=[ksr, identf_r], writes=[TF_r], signal=(q == 3))
                    k.op("dve", lambda e: e.tensor_copy(out=kT[:, :, kt * 128:(kt + 1) * 128], in_=TF[:]), reads=[TF_r], writes=[kT_r[kt]])
                    vs = VS[cnt["vs"] % 2]; vsr = VS_r[cnt["vs"] % 2]; cnt["vs"] += 1
                    k.dma("sp", vs[:], A["cv"][b, kt * 128:(kt + 1) * 128, :], writes=[vsr])
                    k.op("act", lambda e: e.copy(out=Vaug[:, kt, :, 0:64], in_=vs[:].rearrange("p (h d) -> p h d", d=64)),
                         reads=[vsr], writes=[V_r[kt]])

                for kt in range(16):
                    prep(0, kt)
                    yield
                for b in range(4):
                    for par in range(2):
                        for r in range(2):
                            k.op("pool", lambda e, par=par, r=r: e.memset(PTz[par][r][:], 0.0), writes=[PTz_r[par][r]])
                    steps = []
                    for j in range(17):
                        if j < 16:
                            stp = attend_step(b, MT[:, j, 32 * b:32 * b + 32],
                                              lambda rows, i, j=j: kT[rows, i, j * 128:(j + 1) * 128], [kT_r[j]],
                                              lambda h, j=j: Vaug[:, j, h, :], [V_r[j]], q_, q_r, (b == 0 and j == 0), False)
                            if b < 3:
                                stp[2] = (lambda b=b, j=j: prep(b + 1, j))
                        else:
                            stp = attend_step(b, MTn[:, 32 * b:32 * b + 32], lambda rows, i: akTn[rows, i, :], [akTn_r],
                                              lambda h: Vn[:, h, :], [Vn_r], q_, q_r, False, (b == 3))
                        steps.append(stp)
                    yield from run_steps(steps)
                finalize(t)
                yield

            def interleave(g1, n1, g2, n2):
                a1 = a2 = 0
                d1 = d2 = False
                while not (d1 and d2):
                    take1 = (not d1) and (d2 or a1 * n2 <= a2 * n1)
                    if take1:
                        try:
                            next(g1); a1 += 1
                        except StopIteration:
                            d1 = True
                    else:
                        try:
                            next(g2); a2 += 1
                        except StopIteration:
                            d2 = True

            cnt["e0"] = 0; cnt["e1"] = 0

            def interleave_n(gens):
                gens = [[g, n, 0, False] for (g, n) in gens]
                while any(not x[3] for x in gens):
                    live = [x for x in gens if not x[3]]
                    x = min(live, key=lambda x: x[2] / float(x[1]))
                    try:
                        next(x[0]); x[2] += 1
                    except StopIteration:
                        x[3] = True

            def n_s1s(t):
                return 2 + 4 * (5 if t == 16 else (t + 4) // 4)

            for r in range(NT + 3):
                gens = []
                if r < NT:
                    gens.append((S1x(r), 14))
                if 0 <= r - 1 < NT:
                    gens.append((S1s(r - 1), n_s1s(r - 1)))
                if 0 <= r - 2 < NT:
                    gens.append((S1b(r - 2), NIT + 3))
                if 0 <= r - 3 < NT:
                    t2 = r - 3
                    gens.append((S2(t2), (t2 + 5) if t2 < 16 else 90))
                interleave_n(gens)
            k.barrier()

    def ro_stage(P):
        X, XR = P["X"], P["XR"]
        with contextlib.ExitStack() as es:
            stage_ln_tiles(es, P)
            WR = sb(es, "WR", [128, 8, 2048], BF16); WR_r = [Res("WR%d" % i) for i in range(4)]
            WO = sb(es, "WO", [128, 8, D], BF16); WO_r = Res("WO")
            wi_v = A["w_in"].rearrange("(kc p) n -> p kc n", p=128)
            for i in range(4):
                k.dma("pool", WR[:, :, i * 512:(i + 1) * 512], wi_v[:, :, i * 512:(i + 1) * 512], writes=[WR_r[i]])
            k.dma("pool", WO[:], A["w_out"].rearrange("(kc p) n -> p kc n", p=128), writes=[WO_r])
            RCS = [sb(es, "RCS%d" % i, [128, 2, 64]) for i in range(2)]; RCS_r = [Res("RCS%d" % i) for i in range(2)]
            DEC = sb(es, "DEC", [128, 2, 8]); GC = sb(es, "GC", [128, 2, 4]); CMASK = sb(es, "CMASK", [128, 2, 128]); CN_r = Res("CN")
            k.dma("sp", DEC[:], A["dec"], writes=[CN_r])
            k.dma("sp", GC[:], A["gc"], writes=[CN_r])
            k.dma("sp", CMASK[:], A["cmask"], writes=[CN_r])
            S32 = sb(es, "S32", [128, 4, 128]); S32_r = Res("S32")
            Sbf = sb(es, "Sbf", [128, 4, 128], BF16); Sbf_r = Res("Sbf")
            S0 = [sb(es, "S0_%d" % b, [128, 4, 128]) for b in range(4)]; S0_r = [Res("S0_%d" % b) for b in range(4)]
            S0b = [sb(es, "S0b_%d" % b, [128, 4, 128], BF16) for b in range(4)]; S0b_r = [Res("S0b_%d" % b) for b in range(4)]
            QTz = [sb(es, "QTz%d" % b, [128, 4, 128], BF16) for b in range(4)]; QTz_r = Res("QTz")
            KMb = sb(es, "KMb", [128, 4, 128], BF16); KMb_r = Res("KMb")
            XM = [sb(es, "XMr%d" % i, [128, 8, 128], BF16) for i in range(2)]; XM_r = [Res("XMr%d" % i) for i in range(2)]
            ZR = sb(es, "ZR", [128, 8, 128]); ZR_r = Res("ZR")
            ZRb = [sb(es, "ZRb%d" % i, [128, 8, 128], BF16) for i in range(2)]; ZRb_r = [Res("ZRb%d" % i) for i in range(2)]
            RT = [sb(es, "RTr%d" % i, [128, 8, 64]) for i in range(4)]; RT_r = [Res("RTr%d" % i) for i in range(4)]
            Vb = [sb(es, "Vb%d" % i, [128, 4, 128], BF16) for i in range(2)]; Vb_r = [Res("Vb%d" % i) for i in range(2)]
            SG = [sb(es, "SG%d" % i, [128, 512]) for i in range(2)]; SG_r = [Res("SG%d" % i) for i in range(2)]
            QKT = [sb(es, "QKT%d" % i, [128, 8, 128], BF16) for i in range(2)]; QKT_r = [Res("QKT%d" % i) for i in range(2)]
            ST = sb(es, "ST", [128, 4, 128], BF16); ST_r = Res("ST")
            hst = sb(es, "hst", [128, 4, 6]); hmv = sb(es, "hmv", [128, 4, 2]); hrs = sb(es, "hrs", [128, 4]); hs_r = Res("hs")
            ON = sb(es, "ON", [128, 4, 128]); ON_r = Res("ON")
            ORb = [sb(es, "ORb%d" % i, [128, 512], BF16) for i in range(2)]; ORb_r = [Res("ORb%d" % i) for i in range(2)]
            ORT = [sb(es, "ORT%d" % i, [128, 4, 128], BF16) for i in range(2)]; ORT_r = [Res("ORT%d" % i) for i in range(2)]
            OTl = [sb(es, "OTl%d" % i, [128, 4, 128], BF16) for i in range(2)]; OTl_r = [Res("OTl%d" % i) for i in range(2)]
            T = [sb(es, "Tr%d" % i, [128, D]) for i in range(2)]; T_r = [Res("Tr%d" % i) for i in range(2)]
            st = [sb(es, "str%d" % i, [128, 2, 6]) for i in range(2)]; st_r = [Res("str%d" % i) for i in range(2)]
            mv = [sb(es, "mvr%d" % i, [128, 4]) for i in range(2)]; mv_r = [Res("mvr%d" % i) for i in range(2)]
            G = [ps(es, "Gr%d" % i, [128, 512]) for i in range(2)]; G_r = [Res("Gr%d" % i) for i in range(2)]
            TF = ps(es, "TFr", [128, 4, 128]); TF_r = Res("TFr")
            TBa = ps(es, "TBr0", [128, 1, 4, 128], BF16); TBb = ps(es, "TBr1", [128, 1, 4, 128], BF16)
            TBs = [TBa, TBb]; TB_r = [Res("TBr0"), Res("TBr1")]
            SC = ps(es, "SC", [128, 4, 128]); SC_r = Res("SC")
            OP = ps(es, "OP", [128, 4, 128]); OP_r = Res("OP")
            UP = ps(es, "UP", [128, 4, 128]); UP_r = Res("UP")
            cnt = {"tb": 0, "g": 0}

            load_stage_consts(P, 1, "ln2g", "ln2b", G, G_r, T[0][0:5, :], T_r[0])
            for b in range(4):
                k.dma("sp", S0[b][:], A["sret"][b].rearrange("h k v -> k h v"), writes=[S0_r[b]])
                k.op("pool", lambda e, b=b: e.tensor_copy(out=S0b[b][:], in_=S0[b][:]), reads=[S0_r[b]], writes=[S0b_r[b]])
                k.op("pool", lambda e, b=b: e.memset(QTz[b][:], 0.0), writes=[QTz_r])

            def transposes_bf(src_fn, nblk, src_rs, dst_ap, dst_rs, evac):
                hb = cnt["tb"] % 2; cnt["tb"] += 1
                for q in range(nblk):
                    k.op("pe", lambda e, q=q, hb=hb: e.transpose(TBs[hb][:, 0, q, :], src_fn(q), identb[:]),
                         reads=list(src_rs) + [identb_r], writes=[TB_r[hb]], signal=(q == nblk - 1))
                if evac == "act":
                    k.op("act", lambda e: e.copy(out=dst_ap, in_=TBs[hb][:, 0, 0:nblk, :]), reads=[TB_r[hb]], writes=list(dst_rs))
                else:
                    k.op("dve", lambda e: e.tensor_copy(out=dst_ap, in_=TBs[hb][:, 0, 0:nblk, :]), reads=[TB_r[hb]], writes=list(dst_rs))

            def RA1(t):
                xm = XM[t % 2]; xm_r = XM_r[t % 2]
                rcs = RCS[t % 2]; rcs_r = RCS_r[t % 2]
                k.dma("sp", rcs[:, 0, :], A["rcos"][:, t, :], writes=[rcs_r])
                k.dma("sp", rcs[:, 1, :], A["rsin"][:, t, :], writes=[rcs_r])
                make_xmT(t, lambda kc, t=t: X[:, t, kc * 128:(kc + 1) * 128], XR[t], 3, 4, [TF, TF], [TF_r, TF_r],
                         lambda kc, c0, c1: xm[:, kc, c0:c1], xm_r)

            def RA(t):
                smp = (t == 16); kind = 1 if smp else 0
                xm = XM[t % 2]; xm_r = XM_r[t % 2]
                rcs = RCS[t % 2]; rcs_r = RCS_r[t % 2]
                zrb = ZRb[t % 2]; zrb_r = ZRb_r[t % 2]; vb = Vb[t % 2]; vb_r = Vb_r[t % 2]
                sg = SG[t % 2]; sg_r = SG_r[t % 2]; qkt = QKT[t % 2]; qkt_r = QKT_r[t % 2]
                for gi in range(4):
                    if gi % 2 == 0:
                        g = TF[:].rearrange("p a b -> p (a b)"); gr = TF_r
                    else:
                        g = UP[:].rearrange("p a b -> p (a b)"); gr = UP_r
                    for kc in range(8):
                        k.op("pe", lambda e, kc=kc, g=g, gi=gi: e.matmul(
                            g[:, :], lhsT=xm[:, kc, :], rhs=WR[:, kc, gi * 512:(gi + 1) * 512], start=(kc == 0), stop=(kc == 7)),
                            reads=[xm_r, WR_r[gi]], writes=[gr], signal=(kc == 7))
                    gv = g[:, :].rearrange("p (h d) -> p h d", d=128)
                    if gi < 2:
                        k.op("act", lambda e, gv=gv, gi=gi: e.copy(out=ZR[:, 4 * gi:4 * gi + 4, :], in_=gv), reads=[gr], writes=[ZR_r])
                    elif gi == 2:
                        k.op("act", lambda e, gv=gv: e.copy(out=vb[:], in_=gv), reads=[gr], writes=[vb_r])
                    else:
                        k.op("act", lambda e, g=g: e.activation(out=sg[:], in_=g[:, :], func=AF.Silu), reads=[gr], writes=[sg_r])
                    yield
                cosb = bc(rcs[:, 0, :], 1, [128, 8, 64]); sinb = bc(rcs[:, 1, :], 1, [128, 8, 64])
                x1 = ZR[:, :, 0:64]; x2 = ZR[:, :, 64:128]
                k.op("dve", lambda e: e.tensor_tensor(out=RT[0][:], in0=x1, in1=cosb, op=ALU.mult), reads=[ZR_r, rcs_r], writes=[RT_r[0]])
                k.op("dve", lambda e: e.tensor_tensor(out=RT[1][:], in0=x2, in1=sinb, op=ALU.mult), reads=[ZR_r, rcs_r], writes=[RT_r[1]])
                k.op("pool", lambda e: e.tensor_tensor(out=RT[2][:], in0=x2, in1=cosb, op=ALU.mult), reads=[ZR_r, rcs_r], writes=[RT_r[2]])
                k.op("pool", lambda e: e.tensor_tensor(out=RT[3][:], in0=x1, in1=sinb, op=ALU.mult), reads=[ZR_r, rcs_r], writes=[RT_r[3]])
                k.op("dve", lambda e: e.tensor_tensor(out=x1, in0=RT[0][:], in1=RT[1][:], op=ALU.subtract),
                     reads=[RT_r[0], RT_r[1]], writes=[ZR_r])
                k.op("pool", lambda e: e.tensor_tensor(out=x2, in0=RT[2][:], in1=RT[3][:], op=ALU.add),
                     reads=[RT_r[2], RT_r[3]], writes=[ZR_r])
                k.op("dve", lambda e: e.tensor_tensor(out=zrb[:], in0=ZR[:], in1=bc(DEC[:, kind, :], 2, [128, 8, 128]), op=ALU.mult),
                     reads=[ZR_r, CN_r], writes=[zrb_r])
                yield
                transposes_bf(lambda q: zrb[:, q, :], 4, [zrb_r], qkt[:, 0:4, :], [qkt_r], "act")
                yield
                transposes_bf(lambda q: zrb[:, 4 + q, :], 4, [zrb_r], qkt[:, 4:8, :], [qkt_r], "act")
                yield

            def RB(t):
                smp = (t == 16); kind = 1 if smp else 0
                zrb = ZRb[t % 2]; zrb_r = ZRb_r[t % 2]; vb = Vb[t % 2]; vb_r = Vb_r[t % 2]
                sg = SG[t % 2]; sg_r = SG_r[t % 2]; qkt = QKT[t % 2]; qkt_r = QKT_r[t % 2]
                for h in range(4):
                    k.op("pe", lambda e, h=h: e.matmul(SC[:, h, :], lhsT=qkt[:, 4 + h, :], rhs=qkt[:, h, :], start=True, stop=True),
                         reads=[qkt_r], writes=[SC_r], signal=(h == 3))
                k.op("dve", lambda e: e.tensor_tensor(out=ST[:], in0=SC[:], in1=bc(CMASK[:, kind, :], 1, [128, 4, 128]), op=ALU.mult),
                     reads=[SC_r, CN_r], writes=[ST_r])
                if smp:
                    for b in range(4):
                        k.op("pool", lambda e, b=b: e.tensor_copy(out=QTz[b][:, :, 32 * b:32 * b + 32], in_=qkt[:, 0:4, 32 * b:32 * b + 32]),
                             reads=[qkt_r], writes=[QTz_r])
                yield
                for h in range(4):
                    cross = smp or t > 0
                    k.op("pe", lambda e, h=h, cross=cross: e.matmul(OP[:, h, :], lhsT=ST[:, h, :], rhs=vb[:, h, :], start=True, stop=(not cross)),
                         reads=[ST_r, vb_r], writes=[OP_r], signal=(not cross and h == 3))
                    if smp:
                        for b in range(4):
                            k.op("pe", lambda e, h=h, b=b: e.matmul(OP[:, h, :], lhsT=QTz[b][:, h, :], rhs=S0b[b][:, h, :],
                                                                    start=False, stop=(b == 3)),
                                 reads=[QTz_r, S0b_r[b]], writes=[OP_r], signal=(b == 3 and h == 3))
                    elif t > 0:
                        k.op("pe", lambda e, h=h: e.matmul(OP[:, h, :], lhsT=qkt[:, h, :], rhs=Sbf[:, h, :], start=False, stop=True),
                             reads=[qkt_r, Sbf_r], writes=[OP_r], signal=(h == 3))
                yield
                gcb = bc(GC[:, kind, :], 2, [128, 4, 128])
                if not smp:
                    for h in range(4):
                        k.op("pe", lambda e, h=h: e.matmul(UP[:, h, :], lhsT=zrb[:, 4 + h, :], rhs=vb[:, h, :], start=True, stop=True),
                             reads=[zrb_r, vb_r], writes=[UP_r], signal=(h == 3))
                    if t == 0:
                        k.op("dve", lambda e: e.tensor_tensor(out=S32[:], in0=UP[:], in1=gcb, op=ALU.mult), reads=[UP_r, CN_r], writes=[S32_r])
                    else:
                        k.op("dve", lambda e: e.tensor_tensor(out=S32[:], in0=S32[:], in1=UP[:], op=ALU.add), reads=[UP_r, S32_r], writes=[S32_r])
                        k.op("dve", lambda e: e.tensor_tensor(out=S32[:], in0=S32[:], in1=gcb, op=ALU.mult), reads=[S32_r, CN_r], writes=[S32_r])
                    if t < 15:
                        k.op("pool", lambda e: e.tensor_copy(out=Sbf[:], in_=S32[:]), reads=[S32_r], writes=[Sbf_r])
                    else:
                        k.dma("sp", A["stp"].rearrange("h k v -> k h v"), S32[:], reads=[S32_r])
                else:
                    for b in range(4):
                        k.op("pool", lambda e, b=b: e.tensor_scalar(out=KMb[:], in0=zrb[:, 4:8, :], scalar1=ROWM[:, b:b + 1], scalar2=None,
                                                                    op0=ALU.mult), reads=[zrb_r, ROWM_r], writes=[KMb_r])
                        for h in range(4):
                            k.op("pe", lambda e, h=h: e.matmul(UP[:, h, :], lhsT=KMb[:, h, :], rhs=vb[:, h, :], start=True, stop=True),
                                 reads=[KMb_r, vb_r], writes=[UP_r], signal=(h == 3))
                        k.op("dve", lambda e, b=b: e.tensor_tensor(out=S0[b][:], in0=S0[b][:], in1=UP[:], op=ALU.add),
                             reads=[UP_r, S0_r[b], S0b_r[b]], writes=[S0_r[b]])
                        k.op("dve", lambda e, b=b: e.tensor_tensor(out=S0[b][:], in0=S0[b][:], in1=gcb, op=ALU.mult),
                             reads=[S0_r[b], CN_r], writes=[S0_r[b]])
                        k.dma("sp", A["sts"][b].rearrange("h k v -> k h v"), S0[b][:], reads=[S0_r[b]])
                yield
                for h in range(4):
                    k.op("dve", lambda e, h=h: e.bn_stats(out=hst[:, h, :], in_=OP[:, h, :]), reads=[OP_r], writes=[hs_r])
                for h in range(4):
                    k.op("dve", lambda e, h=h: e.bn_aggr(out=hmv[:, h, :], in_=hst[:, h, :]), reads=[hs_r], writes=[hs_r])
                k.op("act", lambda e: e.activation(out=hrs[:], in_=hmv[:, :, 1], func=AF.Sqrt, bias=epsc[:, 0:1], scale=1.0),
                     reads=[hs_r, epsc_r], writes=[hs_r])
                k.op("dve", lambda e: e.reciprocal(out=hrs[:], in_=hrs[:]), reads=[hs_r], writes=[hs_r])
                k.op("dve", lambda e: e.tensor_tensor(out=ON[:], in0=OP[:], in1=bc(hmv[:, :, 0], 2, [128, 4, 128]), op=ALU.subtract),
                     reads=[OP_r, hs_r], writes=[ON_r])
                yield
                k.op("pool", lambda e: e.tensor_tensor(out=ON[:], in0=ON[:], in1=bc(hrs[:], 2, [128, 4, 128]), op=ALU.mult),
                     reads=[ON_r, hs_r], writes=[ON_r])
                orb = ORb[t % 2]; orb_r = ORb_r[t % 2]
                k.op("pool", lambda e: e.tensor_tensor(out=orb[:], in0=ON[:].rearrange("p h d -> p (h d)"), in1=sg[:], op=ALU.mult),
                     reads=[ON_r, sg_r], writes=[orb_r])
                yield

            def RC(t):
                ort = ORT[t % 2]; ort_r = ORT_r[t % 2]
                ot = OTl[t % 2]; ot_r = OTl_r[t % 2]
                k.dma("sp", ot[:].rearrange("p a b -> p (a b)"), OATS[t], reads=[OATS_r[t]], writes=[ot_r])
                orb = ORb[t % 2]; orb_r = ORb_r[t % 2]
                transposes_bf(lambda q: orb[:, q * 128:(q + 1) * 128], 4, [orb_r], ort[:], [ort_r], "act")
                yield
                ys = []
                for half in range(2):
                    g = G[half]; gr = G_r[half]
                    ys.append((g, gr))
                    for c in range(8):
                        lhs = ort[:, c, :] if c < 4 else ot[:, c - 4, :]
                        k.op("pe", lambda e, c=c, g=g, half=half, lhs=lhs: e.matmul(
                            g[:, :], lhsT=lhs, rhs=WO[:, c, half * 512:(half + 1) * 512], start=(c == 0), stop=(c == 7)),
                            reads=[ort_r, ot_r, WO_r], writes=[gr], signal=(c == 7))
                    yield
                post_norm_ln(P, t, [ys[0][0], ys[1][0]], [ys[0][1], ys[1][1]], T[t % 2], T_r[t % 2], st[t % 2], st_r[t % 2],
                             mv[t % 2], mv_r[t % 2])
                if STOP_AFTER == "ro":
                    k.dma("sp", A["y"][t * 128:(t + 1) * 128, :], X[:, t, :], reads=[XR[t]])
                yield

            def interleave_n(gens):
                gens = [[g, n, 0, False] for (g, n) in gens]
                while any(not x[3] for x in gens):
                    live = [x for x in gens if not x[3]]
                    x = min(live, key=lambda x: x[2] / float(x[1]))
                    try:
                        next(x[0]); x[2] += 1
                    except StopIteration:
                        x[3] = True

            for t in range(min(3, NT)):
                P["reload"](t)
            RA1(0)
            for r in range(NT + 2):
                gens = []
                if r + 3 < NT:
                    P["reload"](r + 3)
                if r + 1 < NT:
                    RA1(r + 1)
                if r < NT:
                    gens.append((RA(r), 8))
                if 0 <= r - 1 < NT:
                    gens.append((RB(r - 1), 6))
                if 0 <= r - 2 < NT:
                    gens.append((RC(r - 2), 5))
                interleave_n(gens)
            k.barrier()

    P = {}
    with contextlib.ExitStack() as esA:
        X = sb(esA, "X", [128, NT, D])
        P["X"] = X
        P["XR"] = [Res("X%d" % t) for t in range(NT)]
        xin_v = A["xin"].rearrange("(t p) d -> p t d", p=128)
        for t in range(NT):
            k.dma("sp", X[:, t, :], xin_v[:, t, :], writes=[P["XR"][t]])
        cond_stage()
        ffn_stage(P, "f1g", "f1u", "f1d", 0, 1, 0, "ln1g", "ln1b", final=(STOP_AFTER == "ffn1"), spill=True)
        k.barrier()
    if STOP_AFTER == "ffn1":
        return
    dsa_stage()
    if STOP_AFTER == "dsa":
        return
    with contextlib.ExitStack() as esC:
        X = sb(esC, "X2", [128, NT, D])
        P["X"] = X
        P["XR"] = [Res("X2_%d" % t) for t in range(NT)]
        P["reload"] = lambda t: k.dma("sp", P["X"][:, t, :], XS[t * 128:(t + 1) * 128, :], reads=[XS_r[t]], writes=[P["XR"][t]])
        ro_stage(P)
        if STOP_AFTER != "ro":
            ffn_stage(P, "f2g", "f2u", "f2d", 6, 7, 2, "ln3g", "ln3b", final=True, spill=False)
        k.barrier()


_PROGRAM = None
_LAST = None


def kernel(x_prompt, x_sample, c_prompt, c_sample, cache_k, cache_v, cache_idx_k, state_ret,
           w_cond, b_cond, ffn1_w_gate, ffn1_w_up, ffn1_w_down, ln1_g, ln1_b, w_in, w_out, ln2_g, ln2_b,
           ffn2_w_gate, ffn2_w_up, ffn2_w_down, ln3_g, ln3_b):
    global _PROGRAM
    f = lambda a: np.ascontiguousarray(np.asarray(a, dtype=np.float32))
    x_prompt, x_sample, c_prompt, c_sample = f(x_prompt), f(x_sample), f(c_prompt), f(c_sample)
    cache_k, cache_v, cache_idx_k, state_ret = f(cache_k), f(cache_v), f(cache_idx_k), f(state_ret)
    consts = _consts()
    shared = {
        "w_cond": f(w_cond)[0], "b_cond": f(b_cond)[0].reshape(72, 128),
        "f1g": f(ffn1_w_gate)[0], "f1u": f(ffn1_w_up)[0], "f1d": f(ffn1_w_down)[0],
        "ln1g": f(ln1_g)[0].reshape(1, D), "ln1b": f(ln1_b)[0].reshape(1, D),
        "w_in": f(w_in)[0], "w_out": f(w_out)[0],
        "ln2g": f(ln2_g)[0].reshape(1, D), "ln2b": f(ln2_b)[0].reshape(1, D),
        "f2g": f(ffn2_w_gate)[0], "f2u": f(ffn2_w_up)[0], "f2d": f(ffn2_w_down)[0],
        "ln3g": f(ln3_g)[0].reshape(1, D), "ln3b": f(ln3_b)[0].reshape(1, D),
    }
    for n, v in consts.items():
        shared["k_" + n] = v
    in_maps = []
    for i in range(8):
        m = dict(shared)
        m["xin"] = np.concatenate([x_prompt[i], x_sample[4 * i:4 * i + 4].reshape(128, D)], axis=0)
        m["c5"] = np.concatenate([c_prompt[i:i + 1], c_sample[4 * i:4 * i + 4]], axis=0)
        m["ck"] = cache_k[0, 4 * i:4 * i + 4].reshape(4, 2048, 512)
        m["cv"] = cache_v[0, 4 * i:4 * i + 4].reshape(4, 2048, 512)
        m["cik"] = cache_idx_k[0, 4 * i:4 * i + 4]
        m["sret"] = state_ret[0, 4 * i:4 * i + 4]
        in_maps.append(m)
    if _PROGRAM is None:
        _PROGRAM = build_program()
    res = run_bass_kernel_spmd(_PROGRAM, in_maps, core_ids=list(range(8)))
    R = res.results
    global _LAST
    _LAST = R
    y = np.stack([r["y"] for r in R])
    nk = np.stack([r["newk"] for r in R])
    nv = np.stack([r["newv"] for r in R])
    nik = np.stack([r["newik"] for r in R])
    stp = np.stack([r["stp"] for r in R])
    sts = np.stack([r["sts"] for r in R])
    y_prompt = y[:, :2048].copy()
    y_sample = y[:, 2048:].reshape(32, 32, D).copy()
    new_k_prompt = nk[:, :2048].reshape(1, 8, 2048, 8, 64).copy()
    new_v_prompt = nv[:, :2048].reshape(1, 8, 2048, 8, 64).copy()
    new_idx_k_prompt = nik[:, :2048].reshape(1, 8, 2048, 64).copy()
    state_ret_prompt = stp.reshape(1, 8, 4, 128, 128).copy()
    new_k_sample = nk[:, 2048:].reshape(1, 32, 32, 8, 64).copy()
    new_v_sample = nv[:, 2048:].reshape(1, 32, 32, 8, 64).copy()
    new_idx_k_sample = nik[:, 2048:].reshape(1, 32, 32, 64).copy()
    state_ret_sample = sts.reshape(1, 32, 4, 128, 128).copy()
    return (y_prompt, y_sample, new_k_prompt, new_v_prompt, new_idx_k_prompt, state_ret_prompt,
            new_k_sample, new_v_sample, new_idx_k_sample, state_ret_sample)
```

```python
import contextlib
import math
import numpy as np
import concourse.bass as bass
import concourse.mybir as mybir
from concourse.bass_utils import run_bass_kernel_spmd

F32 = mybir.dt.float32
BF16 = mybir.dt.bfloat16
AF = mybir.ActivationFunctionType
ALU = mybir.AluOpType

NT = 17
D = 1024
DFF = 2816
NFC = 22
DIN = 4168
ALPHA = 2.0 ** 0.25
LN_EPS = 1e-5
NEG = -1.0e30
NIT = 18
TOPK = 256
RET_G = [1.0 - 2.0 ** (-5.0 - h) for h in range(4)]
STOP_AFTER = None


class Ev:
    __slots__ = ("sem", "val", "key")

    def __init__(self, sem, val, key):
        self.sem, self.val, self.key = sem, val, key


class Res:
    __slots__ = ("name", "w", "rs")

    def __init__(self, name):
        self.name = name
        self.w = None
        self.rs = {}


class K:
    def __init__(self, nc, es):
        self.nc = nc
        self.eng = {"pe": nc.tensor, "act": nc.scalar, "dve": nc.vector, "pool": nc.gpsimd, "sp": nc.sync}
        self.sem = {}
        self.cnt = {}
        for e in ("pe", "act", "dve", "pool"):
            self.sem[e] = es.enter_context(nc.semaphore("sem_" + e))
            self.cnt[e] = 0
        self.waited = {e: {} for e in self.eng}
        self.ring = {}
        for q, depth in (("sp", 8), ("pool", 6), ("act", 4)):
            sems = [es.enter_context(nc.semaphore("dq_%s_%d" % (q, i))) for i in range(depth)]
            self.ring[q] = {"sems": sems, "k": 0, "tgt": [0] * depth}
        self.pending_pe = False

    def _wait(self, e, ev):
        if ev is None:
            return
        if e == "pe" and ev.key == "pe":
            return
        if self.waited[e].get(ev.key, 0) >= ev.val:
            return
        self.eng[e].wait_ge(ev.sem, ev.val)
        self.waited[e][ev.key] = ev.val

    def _deps(self, e, reads, writes):
        for r in reads:
            self._wait(e, r.w)
        for w in writes:
            self._wait(e, w.w)
            for ev in w.rs.values():
                self._wait(e, ev)

    def _mark(self, ev, reads, writes):
        for r in reads:
            old = r.rs.get(ev.key)
            if old is None or old.val < ev.val:
                r.rs[ev.key] = ev
        for w in writes:
            w.w = ev
            w.rs = {}

    def op(self, e, fn, reads=(), writes=(), signal=True):
        self._deps(e, reads, writes)
        ins = fn(self.eng[e])
        if signal:
            self.cnt[e] += 1
            ins.then_inc(self.sem[e], 1)
            ev = Ev(self.sem[e], self.cnt[e], e)
            if e == "pe":
                self.pending_pe = False
        else:
            assert e == "pe"
            ev = Ev(self.sem[e], self.cnt[e] + 1, e)
            self.pending_pe = True
        self._mark(ev, reads, writes)
        return ev

    def dma(self, q, out, in_, reads=(), writes=()):
        rg = self.ring[q]
        d = len(rg["sems"])
        slot = rg["k"] % d
        sem = rg["sems"][slot]
        key = "dq_%s_%d" % (q, slot)
        if rg["tgt"][slot] > 0:
            self._wait(q, Ev(sem, rg["tgt"][slot], key))
        self._deps(q, reads, writes)
        rg["tgt"][slot] += 16
        rg["k"] += 1
        self.eng[q].dma_start(out=out, in_=in_).then_inc(sem, 16)
        ev = Ev(sem, rg["tgt"][slot], key)
        self._mark(ev, reads, writes)
        return ev

    def barrier(self, engines=("pe", "act", "dve", "pool", "sp")):
        assert not self.pending_pe
        for e in engines:
            for p in ("pe", "act", "dve", "pool"):
                if self.cnt[p] > 0 and self.waited[e].get(p, 0) < self.cnt[p]:
                    self.eng[e].wait_ge(self.sem[p], self.cnt[p])
                    self.waited[e][p] = self.cnt[p]
            for q, rg in self.ring.items():
                for slot, sem in enumerate(rg["sems"]):
                    key = "dq_%s_%d" % (q, slot)
                    if rg["tgt"][slot] > 0 and self.waited[e].get(key, 0) < rg["tgt"][slot]:
                        self.eng[e].wait_ge(sem, rg["tgt"][slot])
                        self.waited[e][key] = rg["tgt"][slot]


def _consts():
    c = {}
    c["ident_f"] = np.eye(128, dtype=np.float32)
    pos = np.zeros((NT, 128), np.float32)
    for t in range(16):
        pos[t] = 128 * t + np.arange(128)
    pos[16] = 2048 + (np.arange(128) % 32)
    inv_r = (1.0 / (np.float32(10000.0) ** (np.arange(0, 128, 2, dtype=np.float32) / np.float32(128)))).astype(np.float32)
    ang = (pos[:, :, None] * inv_r[None, None, :]).astype(np.float32)
    c["rcos"] = np.cos(ang).astype(np.float32).transpose(1, 0, 2).copy()
    c["rsin"] = np.sin(ang).astype(np.float32).transpose(1, 0, 2).copy()
    inv_a = (1.0 / (np.float32(500000.0) ** (np.arange(0, 16, 2, dtype=np.float32) / np.float32(16)))).astype(np.float32)
    ang = (pos[:, :, None] * inv_a[None, None, :]).astype(np.float32)
    c["acos"] = np.cos(ang).astype(np.float32).transpose(1, 0, 2).copy()
    c["asin"] = np.sin(ang).astype(np.float32).transpose(1, 0, 2).copy()
    dec = np.zeros((128, 2, 8), np.float64)
    for kind in range(2):
        n = np.arange(128) if kind == 0 else (np.arange(128) % 32)
        for h in range(4):
            g = RET_G[h]
            dec[:, kind, h] = g ** (n + 1.0)
            dec[:, kind, 4 + h] = (g ** (-(n + 1.0))) * (128.0 ** -0.5)
    c["dec"] = dec.astype(np.float32)
    gc = np.zeros((128, 2, 4), np.float64)
    for h in range(4):
        gc[:, 0, h] = RET_G[h] ** 128.0
        gc[:, 1, h] = RET_G[h] ** 32.0
    c["gc"] = gc.astype(np.float32)
    m = np.arange(128)
    cm = (m[:, None] <= m[None, :]).astype(np.float32)
    cms = cm * ((m[:, None] // 32) == (m[None, :] // 32)).astype(np.float32)
    c["cmask"] = np.stack([cm, cms], axis=1).copy()
    rowm = np.zeros((128, 4), np.float32)
    for b in range(4):
        rowm[32 * b:32 * b + 32, b] = 1.0
    c["rowm"] = rowm
    sel = np.zeros((5, 2, 128), np.float32)
    sel[0, 0, :] = 1.0
    for b in range(4):
        sel[1 + b, 1, 32 * b:32 * b + 32] = 1.0
    c["sel"] = sel
    c["pow2"] = np.tile((2.0 ** -(np.arange(NIT + 2) + 1.0)).astype(np.float32)[None, :], (128, 1)).copy()
    return c


CONST_SHAPES = {
    "ident_f": [128, 128], "rcos": [128, NT, 64], "rsin": [128, NT, 64], "acos": [128, NT, 8], "asin": [128, NT, 8],
    "dec": [128, 2, 8], "gc": [128, 2, 4], "cmask": [128, 2, 128], "rowm": [128, 4], "sel": [5, 2, 128],
    "pow2": [128, NIT + 2],
}

IN_SHAPES = {
    "xin": [NT * 128, D], "c5": [5, D],
    "ck": [4, 2048, 512], "cv": [4, 2048, 512], "cik": [4, 2048, 64], "sret": [4, 4, 128, 128],
    "w_cond": [D, 9 * D], "b_cond": [72, 128],
    "f1g": [D, DFF], "f1u": [D, DFF], "f1d": [DFF, D], "ln1g": [1, D], "ln1b": [1, D],
    "w_in": [D, DIN], "w_out": [D, D], "ln2g": [1, D], "ln2b": [1, D],
    "f2g": [D, DFF], "f2u": [D, DFF], "f2d": [DFF, D], "ln3g": [1, D], "ln3b": [1, D],
}
OUT_SHAPES = {
    "y": [NT * 128, D], "newk": [NT * 128, 512], "newv": [NT * 128, 512], "newik": [NT * 128, 64],
    "stp": [4, 128, 128], "sts": [4, 4, 128, 128],
}


def build_program():
    nc = bass.Bass("TRN2", target_bir_lowering=False)
    A = {}
    for n, s in IN_SHAPES.items():
        A[n] = nc.dram_tensor(n, s, F32, kind="ExternalInput").ap()
    for n, s in CONST_SHAPES.items():
        A[n] = nc.dram_tensor("k_" + n, s, F32, kind="ExternalInput").ap()
    for n, s in OUT_SHAPES.items():
        A[n] = nc.dram_tensor(n, s, F32, kind="ExternalOutput").ap()
    if STOP_AFTER == "dsa":
        A["dbg"] = nc.dram_tensor("dbg", [NT * 128, 512], F32, kind="ExternalOutput").ap()
        A["dbgI"] = nc.dram_tensor("dbgI", [2, 128, 2080], F32, kind="ExternalOutput").ap()
        A["dbgM"] = nc.dram_tensor("dbgM", [2, 128, 2080], F32, kind="ExternalOutput").ap()

    with contextlib.ExitStack() as es:
        k = K(nc, es)
        _emit(nc, k, A, es)
    return nc


def _emit(nc, k, A, es0):
    uid = [0]

    def sb(es, name, shape, dt=F32):
        uid[0] += 1
        return es.enter_context(nc.sbuf_tensor("s%d_%s" % (uid[0], name), shape, dt))

    def ps(es, name, shape, dt=F32):
        uid[0] += 1
        return es.enter_context(nc.psum_tensor("p%d_%s" % (uid[0], name), shape, dt))

    def bc(ap, axis, shape):
        return ap.unsqueeze(axis).broadcast_to(shape)

    identf = sb(es0, "identf", [128, 128]); identf_r = Res("identf")
    identb = sb(es0, "identb", [128, 128], BF16); identb_r = Res("identb")
    MODT = sb(es0, "MODT", [128, 72, 5]); MODT_r = Res("MODT")
    SEL = sb(es0, "SEL", [5, 2, 128]); SEL_r = Res("SEL")
    ROWM = sb(es0, "ROWM", [128, 4]); ROWM_r = Res("ROWM")
    epsc = sb(es0, "epsc", [128, 1]); epsc_r = Res("epsc")
    XS = nc.dram_tensor("xs_scratch", [NT * 128, D], F32).ap()
    XS_r = [Res("XS%d" % t) for t in range(NT)]
    OATS = nc.dram_tensor("oat_scratch", [NT, 128, 512], BF16).ap()
    OATS_r = [Res("OATS%d" % t) for t in range(NT)]

    k.dma("sp", identf[:], A["ident_f"], writes=[identf_r])
    k.op("act", lambda e: e.copy(out=identb[:], in_=identf[:]), reads=[identf_r], writes=[identb_r])
    k.dma("sp", SEL[:], A["sel"], writes=[SEL_r])
    k.dma("sp", ROWM[:], A["rowm"], writes=[ROWM_r])
    k.op("dve", lambda e: e.memset(epsc[:], LN_EPS), writes=[epsc_r])

    def seq_cols(t):
        if t < 16:
            return [(0, 0, 128)]
        return [(1 + b, 32 * b, 32 * b + 32) for b in range(4)]

    def load_stage_consts(P, gidx, lng, lnb, gbps, gbps_r, GROW, GROW_r):
        GB, GB_r, LNG, LNG_r, LNB, LNB_r = P["GB"], P["GB_r"], P["LNG"], P["LNG_r"], P["LNB"], P["LNB_r"]
        k.dma("sp", LNG[:], A[lng].partition_broadcast(128), writes=[LNG_r])
        k.dma("sp", LNB[:], A[lnb].partition_broadcast(128), writes=[LNB_r])
        jg = (2, 5, 8)[gidx]
        for half in range(2):
            for q in range(4):
                c = half * 4 + q
                k.op("pe", lambda e, c=c, q=q, half=half: e.transpose(
                    gbps[half][0:5, q * 128:(q + 1) * 128], MODT[:, jg * 8 + c, :], identf[:]),
                    reads=[MODT_r, identf_r], writes=[gbps_r[half]], signal=(q == 3))
            k.op("act", lambda e, half=half: e.copy(out=GROW[:, half * 512:(half + 1) * 512], in_=gbps[half][0:5, :]),
                 reads=[gbps_r[half]], writes=[GROW_r])
        i = 0
        for kind in range(2):
            for half in range(2):
                g = gbps[i % 2]; gr = gbps_r[i % 2]; i += 1
                k.op("pe", lambda e, g=g, kind=kind, half=half: e.matmul(
                    g[:, :], lhsT=SEL[:, kind, :], rhs=GROW[:, half * 512:(half + 1) * 512], start=True, stop=True),
                    reads=[SEL_r, GROW_r], writes=[gr])
                k.op("act", lambda e, g=g, kind=kind, half=half: e.copy(out=GB[:, kind, half * 512:(half + 1) * 512], in_=g[:, :]),
                     reads=[gr], writes=[GB_r])

    def make_xmT(t, src_fn, src_r, jsh, jsc, tp, tp_r, dst_fn, dst_r):
        for half in range(2):
            p_ = tp[half]; pr = tp_r[half]
            for q in range(4):
                kc = half * 4 + q
                k.op("pe", lambda e, kc=kc, q=q, p_=p_: e.transpose(p_[:, q, :], src_fn(kc), identf[:]),
                     reads=[src_r, identf_r], writes=[pr], signal=(q == 3))
            for q in range(4):
                kc = half * 4 + q
                for (s, c0, c1) in seq_cols(t):
                    k.op("act", lambda e, kc=kc, q=q, s=s, c0=c0, c1=c1, p_=p_: e.activation(
                        out=dst_fn(kc, c0, c1), in_=p_[:, q, c0:c1], func=AF.Identity,
                        scale=MODT[:, jsc * 8 + kc, s:s + 1], bias=MODT[:, jsh * 8 + kc, s:s + 1]),
                        reads=[pr, MODT_r], writes=[dst_r])

    def post_norm_ln(P, t, yps, yps_r, T, T_r, st, st_r, mv, mv_r):
        X, XR = P["X"], P["XR"]
        GB, GB_r, LNG, LNG_r, LNB, LNB_r = P["GB"], P["GB_r"], P["LNG"], P["LNG_r"], P["LNB"], P["LNB_r"]
        kind = 0 if t < 16 else 1
        for half in range(2):
            k.op("dve", lambda e, half=half: e.tensor_tensor(
                out=T[:, half * 512:(half + 1) * 512], in0=yps[half][:, :], in1=GB[:, kind, half * 512:(half + 1) * 512],
                op=ALU.mult), reads=[yps_r[half], GB_r], writes=[T_r])
        k.op("dve", lambda e: e.scalar_tensor_tensor(out=T[:], in0=X[:, t, :], scalar=ALPHA, in1=T[:],
                                                     op0=ALU.mult, op1=ALU.add), reads=[XR[t], T_r], writes=[T_r])
        for half in range(2):
            k.op("dve", lambda e, half=half: e.bn_stats(out=st[:, half, :], in_=T[:, half * 512:(half + 1) * 512]),
                 reads=[T_r], writes=[st_r])
        k.op("dve", lambda e: e.bn_aggr(out=mv[:, 0:2], in_=st[:].rearrange("p a b -> p (a b)")), reads=[st_r], writes=[mv_r])
        k.op("act", lambda e: e.activation(out=mv[:, 2:3], in_=mv[:, 1:2], func=AF.Sqrt, bias=epsc[:, 0:1], scale=1.0),
             reads=[mv_r, epsc_r], writes=[mv_r])
        k.op("dve", lambda e: e.reciprocal(out=mv[:, 2:3], in_=mv[:, 2:3]), reads=[mv_r], writes=[mv_r])
        k.op("dve", lambda e: e.tensor_scalar(out=mv[:, 3:4], in0=mv[:, 0:1], scalar1=mv[:, 2:3], scalar2=-1.0,
                                              op0=ALU.mult, op1=ALU.mult), reads=[mv_r], writes=[mv_r])
        k.op("act", lambda e: e.activation(out=T[:], in_=T[:], func=AF.Identity, scale=mv[:, 2:3], bias=mv[:, 3:4]),
             reads=[T_r, mv_r], writes=[T_r])
        k.op("pool", lambda e: e.tensor_tensor(out=T[:], in0=T[:], in1=LNG[:], op=ALU.mult), reads=[T_r, LNG_r], writes=[T_r])
        k.op("pool", lambda e: e.tensor_tensor(out=X[:, t, :], in0=T[:], in1=LNB[:], op=ALU.add),
             reads=[T_r, LNB_r], writes=[XR[t]])

    def stage_ln_tiles(es, P):
        P["GB"] = sb(es, "GB", [128, 2, D]); P["GB_r"] = Res("GB")
        P["LNG"] = sb(es, "LNG", [128, D]); P["LNG_r"] = Res("LNG")
        P["LNB"] = sb(es, "LNB", [128, D]); P["LNB_r"] = Res("LNB")

    def cond_stage():
        with contextlib.ExitStack() as es:
            c5 = sb(es, "c5", [5, D]); c5_r = Res("c5")
            sc = sb(es, "sc", [5, D], BF16); sc_r = Res("sc")
            scT = sb(es, "scT", [128, 8, 5], BF16); scT_r = Res("scT")
            bcn = sb(es, "bc", [72, 128]); bc_r = Res("bc")
            bT = sb(es, "bT", [128, 72]); bT_r = Res("bT")
            WC = [sb(es, "WC%d" % i, [128, 8, D], BF16) for i in range(2)]
            WC_r = [Res("WC%d" % i) for i in range(2)]
            tps = ps(es, "tps", [128, 8, 8], BF16); tps_r = Res("tps")
            bps = ps(es, "bps", [128, 72]); bps_r = Res("bps")
            mps = ps(es, "mps", [128, 72, 5]); mps_r = Res("mps")
            k.dma("sp", c5[:], A["c5"], writes=[c5_r])
            k.dma("sp", bcn[:], A["b_cond"], writes=[bc_r])
            wc_v = A["w_cond"].rearrange("(kc p) n -> p kc n", p=128)
            for j in range(2):
                k.dma("pool", WC[j][:], wc_v[:, :, j * D:(j + 1) * D], writes=[WC_r[j]])
            k.op("act", lambda e: e.activation(out=sc[:], in_=c5[:], func=AF.Silu), reads=[c5_r], writes=[sc_r])
            for kc in range(8):
                k.op("pe", lambda e, kc=kc: e.transpose(tps[:, kc, 0:5], sc[:, kc * 128:(kc + 1) * 128], identb[0:5, 0:5]),
                     reads=[sc_r, identb_r], writes=[tps_r])
            k.op("act", lambda e: e.copy(out=scT[:], in_=tps[:, :, 0:5]), reads=[tps_r], writes=[scT_r])
            k.op("pe", lambda e: e.transpose(bps[:], bcn[:], identf[0:72, 0:72]), reads=[bc_r, identf_r], writes=[bps_r])
            k.op("act", lambda e: e.copy(out=bT[:], in_=bps[:]), reads=[bps_r], writes=[bT_r])
            for j in range(9):
                w = WC[j % 2]; wr = WC_r[j % 2]
                for c in range(8):
                    for kc in range(8):
                        k.op("pe", lambda e, c=c, kc=kc, w=w, j=j: e.matmul(
                            mps[:, j * 8 + c, :], lhsT=w[:, kc, c * 128:(c + 1) * 128], rhs=scT[:, kc, :],
                            start=(kc == 0), stop=(kc == 7)),
                            reads=[wr, scT_r], writes=[mps_r], signal=(kc == 7))
                if j + 2 < 9:
                    k.dma("pool", w[:], wc_v[:, :, (j + 2) * D:(j + 3) * D], writes=[wr])
            k.op("dve", lambda e: e.tensor_tensor(out=MODT[:], in0=mps[:], in1=bc(bT[:], 2, [128, 72, 5]), op=ALU.add),
                 reads=[mps_r, bT_r], writes=[MODT_r])
            for j in (1, 4, 7):
                k.op("dve", lambda e, j=j: e.tensor_scalar_add(out=MODT[:, j * 8:(j + 1) * 8, :], in0=MODT[:, j * 8:(j + 1) * 8, :],
                                                               scalar1=1.0), reads=[MODT_r], writes=[MODT_r])
            for j in (2, 5, 8):
                wgt = 1.0 if j == 5 else 0.5
                k.op("dve", lambda e, j=j, wgt=wgt: e.tensor_scalar(
                    out=MODT[:, j * 8:(j + 1) * 8, :], in0=MODT[:, j * 8:(j + 1) * 8, :], scalar1=1.0, scalar2=wgt,
                    op0=ALU.add, op1=ALU.mult), reads=[MODT_r], writes=[MODT_r])
            k.barrier()

    def ffn_stage(P, wg, wu, wd, jsh, jsc, gidx, lng, lnb, final, spill):
        X, XR = P["X"], P["XR"]
        with contextlib.ExitStack() as es:
            stage_ln_tiles(es, P)
            blocks = [list(range(0, 6)), list(range(6, 12)), list(range(12, 17))]
            WD = sb(es, "WD", [128, NFC, D], BF16); WD_r = [Res("WD%d" % i) for i in range(4)]
            WG = [sb(es, "WG%d" % i, [128, 8, 256], BF16) for i in range(2)]
            WU = [sb(es, "WU%d" % i, [128, 8, 256], BF16) for i in range(2)]
            WGU_r = [Res("WGU%d" % i) for i in range(2)]
            xmT = sb(es, "xmT", [128, 8, 768], BF16); xmT_r = Res("xmT")
            H = sb(es, "H", [128, NFC, 768], BF16); H_r = [Res("H%d" % c) for c in range(NFC)]
            S = [sb(es, "S%d" % i, [128, 512]) for i in range(2)]; S_r = [Res("S%d" % i) for i in range(2)]
            T = [sb(es, "T%d" % i, [128, D]) for i in range(2)]; T_r = [Res("T%d" % i) for i in range(2)]
            st = [sb(es, "st%d" % i, [128, 2, 6]) for i in range(2)]; st_r = [Res("st%d" % i) for i in range(2)]
            mv = [sb(es, "mv%d" % i, [128, 4]) for i in range(2)]; mv_r = [Res("mv%d" % i) for i in range(2)]
            tp = [ps(es, "tp%d" % i, [128, 4, 128]) for i in range(2)]; tp_r = [Res("tp%d" % i) for i in range(2)]
            pA = [ps(es, "pA%d" % i, [128, 512]) for i in range(2)]; pA_r = [Res("pA%d" % i) for i in range(2)]
            pB = [ps(es, "pB%d" % i, [128, 512]) for i in range(2)]; pB_r = [Res("pB%d" % i) for i in range(2)]
            pY = [ps(es, "pY%d" % i, [128, 512]) for i in range(2)]; pY_r = [Res("pY%d" % i) for i in range(2)]

            load_stage_consts(P, gidx, lng, lnb, pY, pY_r, T[0][0:5, :], T_r[0])
            wd_v = A[wd].rearrange("(c p) n -> p c n", p=128)
            wdq = [(0, 6), (6, 12), (12, 17), (17, 22)]
            wg_v = A[wg].rearrange("(kc p) n -> p kc n", p=128)
            wu_v = A[wu].rearrange("(kc p) n -> p kc n", p=128)
            groups = [(g * 2, 2) for g in range(11)]
            gcount = 0
            wd_loaded = False

            def load_group(gi_, slot):
                c0, n = groups[gi_]
                k.dma("pool", WG[slot][:, :, 0:n * 128], wg_v[:, :, c0 * 128:(c0 + n) * 128], writes=[WGU_r[slot]])
                k.dma("pool", WU[slot][:, :, 0:n * 128], wu_v[:, :, c0 * 128:(c0 + n) * 128], writes=[WGU_r[slot]])

            seqg = [(bi, gi_) for bi in range(len(blocks)) for gi_ in range(len(groups))]
            load_group(seqg[0][1], 0)
            load_group(seqg[1][1], 1)
            si = 0
            mm = 0
            ti = 0
            for bi, tiles in enumerate(blocks):
                ntok = 128 * len(tiles)
                if bi == 0:
                    for li, t in enumerate(tiles):
                        make_xmT(t, lambda kc, t=t: X[:, t, kc * 128:(kc + 1) * 128], XR[t], jsh, jsc, tp, tp_r,
                                 lambda kc, c0, c1, li=li: xmT[:, kc, li * 128 + c0:li * 128 + c1], xmT_r)
                subs = [(s0, min(512, ntok - s0)) for s0 in range(0, ntok, 512)]
                for gi_ in range(len(groups)):
                    slot = gcount % 2
                    c0, n = groups[gi_]
                    for cc in range(n):
                        c = c0 + cc
                        for (s0, sn) in subs:
                            a = pA[mm % 2]; ar = pA_r[mm % 2]; b_ = pB[mm % 2]; br = pB_r[mm % 2]; mm += 1
                            for kc in range(8):
                                k.op("pe", lambda e, kc=kc, cc=cc, a=a, s0=s0, sn=sn, slot=slot: e.matmul(
                                    a[:, 0:sn], lhsT=WG[slot][:, kc, cc * 128:(cc + 1) * 128], rhs=xmT[:, kc, s0:s0 + sn],
                                    start=(kc == 0), stop=(kc == 7)),
                                    reads=[WGU_r[slot], xmT_r], writes=[ar], signal=(kc == 7))
                            for kc in range(8):
                                k.op("pe", lambda e, kc=kc, cc=cc, b_=b_, s0=s0, sn=sn, slot=slot: e.matmul(
                                    b_[:, 0:sn], lhsT=WU[slot][:, kc, cc * 128:(cc + 1) * 128], rhs=xmT[:, kc, s0:s0 + sn],
                                    start=(kc == 0), stop=(kc == 7)),
                                    reads=[WGU_r[slot], xmT_r], writes=[br], signal=(kc == 7))
                            s_ = S[si % 2]; sr = S_r[si % 2]; si += 1
                            k.op("act", lambda e, a=a, s_=s_, sn=sn: e.activation(out=s_[:, 0:sn], in_=a[:, 0:sn], func=AF.Silu),
                                 reads=[ar], writes=[sr])
                            k.op("dve", lambda e, b_=b_, s_=s_, c=c, s0=s0, sn=sn: e.tensor_tensor(
                                out=H[:, c, s0:s0 + sn], in0=s_[:, 0:sn], in1=b_[:, 0:sn], op=ALU.mult),
                                reads=[sr, br], writes=[H_r[c]])
                    gcount += 1
                    if gcount + 1 < len(seqg):
                        load_group(seqg[gcount + 1][1], slot)
                    if not wd_loaded and gcount == 2:
                        for qi, (q0, q1) in enumerate(wdq):
                            k.dma("pool", WD[:, q0:q1, :], wd_v[:, q0:q1, :], writes=[WD_r[qi]])
                        wd_loaded = True
                for li, t in enumerate(tiles):
                    for half in range(2):
                        for c in range(NFC):
                            qi = [i for i, (q0, q1) in enumerate(wdq) if q0 <= c < q1][0]
                            k.op("pe", lambda e, c=c, half=half, li=li: e.matmul(
                                pY[half][:, :], lhsT=H[:, c, li * 128:(li + 1) * 128], rhs=WD[:, c, half * 512:(half + 1) * 512],
                                start=(c == 0), stop=(c == NFC - 1)),
                                reads=[H_r[c], WD_r[qi]], writes=[pY_r[half]], signal=(c == NFC - 1))
                    if bi + 1 < len(blocks) and li < len(blocks[bi + 1]):
                        tn = blocks[bi + 1][li]
                        make_xmT(tn, lambda kc, tn=tn: X[:, tn, kc * 128:(kc + 1) * 128], XR[tn], jsh, jsc, tp, tp_r,
                                 lambda kc, c0, c1, li=li: xmT[:, kc, li * 128 + c0:li * 128 + c1], xmT_r)
                    post_norm_ln(P, t, pY, pY_r, T[ti % 2], T_r[ti % 2], st[ti % 2], st_r[ti % 2], mv[ti % 2], mv_r[ti % 2])
                    if final:
                        k.dma("sp", A["y"][t * 128:(t + 1) * 128, :], X[:, t, :], reads=[XR[t]])
                    if spill:
                        k.dma("sp", XS[t * 128:(t + 1) * 128, :], X[:, t, :], reads=[XR[t]], writes=[XS_r[t]])
                    ti += 1
            k.barrier()

    def dsa_stage():
        with contextlib.ExitStack() as es:
            C_AQ, C_AK, C_IQ, C_IK, C_IK2, C_IW, C_AV = 0, 512, 1024, 1536, 1600, 1664, 1672
            WA = sb(es, "WA", [128, 8, 2184], BF16); WA_rs = [Res("WA%d" % i) for i in range(5)]
            wi_v = A["w_in"].rearrange("(kc p) n -> p kc n", p=128)
            for (dst, src, n, gi_) in ((C_AQ, 2048, 512, 0), (C_AK, 2560, 512, 1), (C_IQ, 3584, 512, 2), (C_IK, 4096, 64, 3),
                                       (C_IK2, 4096, 64, 3), (C_IW, 4160, 8, 3), (C_AV, 3072, 512, 4)):
                k.dma("pool", WA[:, :, dst:dst + n], wi_v[:, :, src:src + n], writes=[WA_rs[gi_]])
            ACOS = sb(es, "ACOS", [128, NT, 8]); ASIN = sb(es, "ASIN", [128, NT, 8]); AC_r = Res("AC")
            k.dma("sp", ACOS[:], A["acos"], writes=[AC_r])
            k.dma("sp", ASIN[:], A["asin"], writes=[AC_r])
            POW2 = sb(es, "POW2", [128, NIT + 2]); POW2_r = Res("POW2")
            k.dma("sp", POW2[:], A["pow2"], writes=[POW2_r])
            kT = sb(es, "kT", [128, 4, 2048], BF16); kT_r = [Res("kT%d" % j) for j in range(16)]
            Vaug = sb(es, "Vaug", [128, 16, 8, 65], BF16); V_r = [Res("V%d" % j) for j in range(16)]
            ikT2 = sb(es, "ikT2", [128, 5, 2048], BF16); ik_r = [[Res("ik%d_%d" % (b, j)) for j in range(16)] for b in range(5)]
            akTn = sb(es, "akTn", [128, 4, 128], BF16); akTn_r = Res("akTn")
            ikT2n = sb(es, "ikT2n", [128, 128], BF16); ikT2n_r = Res("ikT2n")
            Vn = sb(es, "Vn", [128, 8, 65], BF16); Vn_r = Res("Vn")
            iqTz = [sb(es, "iqTz%d" % b, [128, 4, 128], BF16) for b in range(4)]; iqTz_r = Res("iqTz")
            Mnew = sb(es, "Mnew", [128, 128], BF16); Mnew_r = Res("Mnew")
            MTn = sb(es, "MTn", [128, 128], BF16); MTn_r = Res("MTn")
            CI = sb(es, "CI", [128, 16, 64]); CI_r = Res("CI")
            CIb = sb(es, "CIb", [128, 16, 2, 64], BF16); CIb_r = Res("CIb")
            KS = [sb(es, "KS%d" % i, [128, 512]) for i in range(2)]; KS_r = [Res("KS%d" % i) for i in range(2)]
            VS = [sb(es, "VS%d" % i, [128, 512]) for i in range(2)]; VS_r = [Res("VS%d" % i) for i in range(2)]
            PTz = [[sb(es, "PTz%d_%d" % (p_, r), [128, 4, 128], BF16) for r in range(2)] for p_ in range(2)]
            PTz_r = [[Res("PTz%d_%d" % (p_, r)) for r in range(2)] for p_ in range(2)]
            XT = [sb(es, "XT%d" % i, [128, D]) for i in range(2)]; XT_r = [Res("XT%d" % i) for i in range(2)]
            XM = [sb(es, "XM%d" % i, [128, 8, 128], BF16) for i in range(2)]; XM_r = [Res("XM%d" % i) for i in range(2)]
            ZA = sb(es, "ZA", [128, 26, 64]); ZA_r = Res("ZA")
            ZAb = sb(es, "ZAb", [128, 26, 64], BF16); ZAb_r = Res("ZAb")
            RT = [sb(es, "RT%d" % i, [128, 26, 8]) for i in range(4)]; RT_r = [Res("RT%d" % i) for i in range(4)]
            ZV = sb(es, "ZV", [128, 512]); ZV_r = Res("ZV")
            IW = [sb(es, "IW%d" % i, [128, 8]) for i in range(2)]; IW_r = [Res("IW%d" % i) for i in range(2)]
            DIAG = sb(es, "DIAG", [128, 8, 128], BF16); DIAG_r = Res("DIAG")
            qT = [sb(es, "qT%d" % i, [128, 4, 128], BF16) for i in range(4)]; qT_r = [Res("qT%d" % i) for i in range(4)]
            iqT = [sb(es, "iqT%d" % i, [128, 4, 128], BF16) for i in range(2)]; iqT_r = [Res("iqT%d" % i) for i in range(2)]
            R = [sb(es, "R%d" % i, [128, 512], BF16) for i in range(4)]; R_r = [Res("R%d" % i) for i in range(4)]
            I = [sb(es, "I%d" % i, [128, 2080]) for i in range(2)]; I_r = [Res("I%d" % i) for i in range(2)]
            M = [sb(es, "M%d" % i, [128, 2080], BF16) for i in range(2)]; M_r = [Res("M%d" % i) for i in range(2)]
            MT = sb(es, "MT", [128, 16, 128], BF16); MT_r = Res("MT")
            BS = sb(es, "BS", [128, 8]); BS_r = Res("BS")
            HK = sb(es, "HK", [128, NIT + 2]); HK_r = Res("HK")
            E = [[sb(es, "E%d_%d" % (p_, r), [128, 4, 128], BF16) for r in range(2)] for p_ in range(2)]
            E_r = [[Res("E%d_%d" % (p_, r)) for r in range(2)] for p_ in range(2)]
            PT = [[sb(es, "PT%d_%d" % (p_, r), [128, 4, 128], BF16) for r in range(2)] for p_ in range(2)]
            PT_r = [[Res("PT%d_%d" % (p_, r)) for r in range(2)] for p_ in range(2)]
            RD = sb(es, "RD", [128, 2, 4]); RD_r = Res("RD")
            OATT = sb(es, "OATT", [128, 4, 2, 64], BF16); OATT_r = Res("OATT")
            OT = [sb(es, "OT%d" % i, [128, 4, 128], BF16) for i in range(2)]; OT_r = [Res("OT%d" % i) for i in range(2)]
            G0 = ps(es, "G0", [128, 512]); G0_r = Res("G0")
            G1 = ps(es, "G1", [128, 512]); G1_r = Res("G1")
            TB = ps(es, "TB", [128, 2, 4, 128], BF16); TB_r = [Res("TB0")] * 2
            TF = TB[:].rearrange("p a b c -> p (a b c)").bitcast(F32).rearrange("p (a b) -> p a b", b=128); TF_r = TB_r[0]
            SB2 = ps(es, "SB2", [128, 512]); SB2_r = Res("SB2")
            L = [ps(es, "L%d" % i, [128, 4, 128]) for i in range(2)]; L_r = [Res("L%d" % i) for i in range(2)]
            O = [ps(es, "O%d" % i, [128, 4, 65]) for i in range(2)]; O_r = [Res("O%d" % i) for i in range(2)]
            TFf = TB[:].rearrange("p a b c -> p (a b c)").bitcast(F32)
            G1v = G1[:, :].rearrange("p (a b) -> p a b", b=128)
            cnt = {"tb": 0, "r": 0, "e": 0, "ks": 0, "vs": 0}

            k.op("pool", lambda e: e.memset(Vaug[:, :, :, 64:65], 1.0), writes=V_r)
            k.op("pool", lambda e: e.memset(Vn[:, :, 64:65], 1.0), writes=[Vn_r])
            for b in range(4):
                k.op("pool", lambda e, b=b: e.memset(iqTz[b][:], 0.0), writes=[iqTz_r])

            def transposes_bf(src_fn, nblk, src_rs, dst_ap, dst_rs, evac):
                hb = cnt["tb"] % 2; cnt["tb"] += 1
                for q in range(nblk):
                    k.op("pe", lambda e, q=q, hb=hb: e.transpose(TB[:, hb, q, :], src_fn(q), identb[:]),
                         reads=list(src_rs) + [identb_r], writes=[TB_r[hb]], signal=(q == nblk - 1))
                src = TB[:, hb, 0:nblk, :] if nblk > 1 else TB[:, hb, 0, :]
                if evac == "act":
                    k.op("act", lambda e: e.copy(out=dst_ap, in_=src), reads=[TB_r[hb]], writes=list(dst_rs))
                else:
                    k.op("dve", lambda e: e.tensor_copy(out=dst_ap, in_=src), reads=[TB_r[hb]], writes=list(dst_rs))

            def S1x(t):
                smp = (t == 16)
                iq_ = iqT[t % 2]; iq_r = iqT_r[t % 2]; iw_ = IW[t % 2]; iw_r = IW_r[t % 2]
                xt = XT[t % 2]; xt_r = XT_r[t % 2]
                k.dma("sp", xt[:], XS[t * 128:(t + 1) * 128, :], reads=[XS_r[t]], writes=[xt_r])
                xm = XM[t % 2]; xm_r = XM_r[t % 2]
                make_xmT(t, lambda kc: xt[:, kc * 128:(kc + 1) * 128], xt_r, 3, 4, [TF, TF], [TF_r, TF_r],
                         lambda kc, c0, c1: xm[:, kc, c0:c1], xm_r)
                yield
                for gi, (c0, n) in enumerate(((C_AQ, 512), (C_AK, 512), (C_IQ, 512), (C_IK, 136), (C_AV, 512))):
                    g, gr = TFf, TF_r
                    for kc in range(8):
                        k.op("pe", lambda e, kc=kc, g=g, c0=c0, n=n: e.matmul(
                            g[:, 0:n], lhsT=xm[:, kc, :], rhs=WA[:, kc, c0:c0 + n], start=(kc == 0), stop=(kc == 7)),
                            reads=[xm_r, WA_rs[gi]], writes=[gr], signal=(kc == 7))
                    if gi < 3:
                        k.op("act", lambda e, g=g, gi=gi: e.copy(
                            out=ZA[:, 8 * gi:8 * gi + 8, :], in_=g[:, 0:512].rearrange("p (h d) -> p h d", d=64)),
                            reads=[gr], writes=[ZA_r])
                    elif gi == 3:
                        k.op("act", lambda e, g=g: e.copy(
                            out=ZA[:, 24:26, :], in_=g[:, 0:128].rearrange("p (h d) -> p h d", d=64)),
                            reads=[gr], writes=[ZA_r])
                        k.op("act", lambda e, g=g: e.mul(out=iw_[:], in_=g[:, 128:136], mul=8.0 ** -0.5),
                             reads=[gr], writes=[iw_r])
                    else:
                        k.op("act", lambda e, g=g: e.copy(out=ZV[:], in_=g[:, 0:512]), reads=[gr], writes=[ZV_r])
                    yield
                k.dma("sp", A["newv"][t * 128:(t + 1) * 128, :], ZV[:], reads=[ZV_r])
                if not smp:
                    k.op("pool", lambda e: e.tensor_copy(out=Vaug[:, t, :, 0:64], in_=ZV[:].rearrange("p (h d) -> p h d", d=64)),
                         reads=[ZV_r], writes=[V_r[t]])
                else:
                    k.op("pool", lambda e: e.tensor_copy(out=Vn[:, :, 0:64], in_=ZV[:].rearrange("p (h d) -> p h d", d=64)),
                         reads=[ZV_r], writes=[Vn_r])
                cosb = bc(ACOS[:, t, :], 1, [128, 26, 8]); sinb = bc(ASIN[:, t, :], 1, [128, 26, 8])
                x1 = ZA[:, :, 0:8]; x2 = ZA[:, :, 8:16]
                k.op("pool", lambda e: e.tensor_tensor(out=RT[0][:], in0=x1, in1=cosb, op=ALU.mult), reads=[ZA_r, AC_r], writes=[RT_r[0]])
                k.op("pool", lambda e: e.tensor_tensor(out=RT[1][:], in0=x2, in1=sinb, op=ALU.mult), reads=[ZA_r, AC_r], writes=[RT_r[1]])
                k.op("pool", lambda e: e.tensor_tensor(out=RT[2][:], in0=x2, in1=cosb, op=ALU.mult), reads=[ZA_r, AC_r], writes=[RT_r[2]])
                k.op("pool", lambda e: e.tensor_tensor(out=RT[3][:], in0=x1, in1=sinb, op=ALU.mult), reads=[ZA_r, AC_r], writes=[RT_r[3]])
                k.op("pool", lambda e: e.tensor_tensor(out=x1, in0=RT[0][:], in1=RT[1][:], op=ALU.subtract),
                     reads=[RT_r[0], RT_r[1]], writes=[ZA_r])
                k.op("pool", lambda e: e.tensor_tensor(out=x2, in0=RT[2][:], in1=RT[3][:], op=ALU.add),
                     reads=[RT_r[2], RT_r[3]], writes=[ZA_r])
                k.dma("sp", A["newk"][t * 128:(t + 1) * 128, :], ZA[:, 8:16, :].rearrange("p h d -> p (h d)"), reads=[ZA_r])
                k.dma("sp", A["newik"][t * 128:(t + 1) * 128, :], ZA[:, 24, :], reads=[ZA_r])
                yield
                k.op("pool", lambda e: e.tensor_copy(out=ZAb[:], in_=ZA[:]), reads=[ZA_r], writes=[ZAb_r])
                yield
                zb = ZAb[:].rearrange("p h d -> p (h d)")
                q_ = qT[t % 4]
                transposes_bf(lambda q: zb[:, q * 128:(q + 1) * 128], 4, [ZAb_r], q_[:], [qT_r[t % 4]], "act")
                if not smp:
                    transposes_bf(lambda q: zb[:, 512 + q * 128:512 + (q + 1) * 128], 4, [ZAb_r],
                                  kT[:, :, t * 128:(t + 1) * 128], [kT_r[t]], "act")
                else:
                    transposes_bf(lambda q: zb[:, 512 + q * 128:512 + (q + 1) * 128], 4, [ZAb_r], akTn[:], [akTn_r], "act")
                transposes_bf(lambda q: zb[:, 1024 + q * 128:1024 + (q + 1) * 128], 4, [ZAb_r], iq_[:], [iq_r], "act")
                if not smp:
                    transposes_bf(lambda q: zb[:, 1536:1664], 1, [ZAb_r], ikT2[:, 0, t * 128:(t + 1) * 128], [ik_r[0][t]], "act")
                else:
                    transposes_bf(lambda q: zb[:, 1536:1664], 1, [ZAb_r], ikT2n[:], [ikT2n_r], "act")
                    for b in range(4):
                        k.op("pool", lambda e, b=b: e.tensor_copy(out=iqTz[b][:, :, 32 * b:32 * b + 32], in_=iq_[:, :, 32 * b:32 * b + 32]),
                             reads=[iq_r], writes=[iqTz_r])
                    for b in range(4):
                        k.dma("sp", CI[:], A["cik"][b].rearrange("(kt p) d -> p kt d", p=128), writes=[CI_r])
                        k.op("pool", lambda e: e.tensor_copy(out=CIb[:, :, 0, :], in_=CI[:]), reads=[CI_r], writes=[CIb_r])
                        k.op("pool", lambda e: e.tensor_copy(out=CIb[:, :, 1, :], in_=CI[:]), reads=[CI_r], writes=[CIb_r])
                        for k4 in range(4):
                            transposes_bf(lambda q, k4=k4: CIb[:, k4 * 4 + q, :, :].rearrange("p a d -> p (a d)"), 4, [CIb_r],
                                          ikT2[:, b + 1, k4 * 512:(k4 + 1) * 512].rearrange("p (q n) -> p q n", n=128),
                                          [ik_r[b + 1][k4 * 4 + q] for q in range(4)], "act")
                yield

            def S1s(t):
                smp = (t == 16)
                I_ = I[t % 2]; Ir = I_r[t % 2]
                iq_ = iqT[t % 2]; iq_r = iqT_r[t % 2]; iw_ = IW[t % 2]; iw_r = IW_r[t % 2]
                k.op("pool", lambda e: e.tensor_tensor(out=DIAG[:], in0=bc(identb[:], 1, [128, 8, 128]),
                                                      in1=bc(iw_[:], 2, [128, 8, 128]), op=ALU.mult),
                     reads=[identb_r, iw_r], writes=[DIAG_r])
                nk = 2080 if smp else 128 * (t + 1)
                chunks = [(c0, min(512, 2048 - c0) if smp else min(512, nk - c0)) for c0 in range(0, 2048 if smp else nk, 512)]
                if smp:
                    chunks.append((2048, 32))
                pend = None
                for (c0, n) in chunks:
                    for hp in range(4):
                        items = []
                        for h in (2 * hp, 2 * hp + 1):
                            par = h % 2
                            g, gr = ((G0, G0_r), (SB2, SB2_r))[h % 2]
                            rows = slice(64 * par, 64 * par + 64)
                            if not smp:
                                k.op("pe", lambda e, g=g, h=h, c0=c0, n=n, rows=rows: e.matmul(
                                    g[:, 0:n], lhsT=iq_[rows, h // 2, :], rhs=ikT2[rows, 0, c0:c0 + n], start=True, stop=True),
                                    reads=[iq_r] + [ik_r[0][j] for j in range(c0 // 128, (c0 + n) // 128)], writes=[gr])
                            else:
                                for b in range(4):
                                    if c0 < 2048:
                                        rhs = ikT2[rows, b + 1, c0:c0 + n]
                                        rr = [ik_r[b + 1][j] for j in range(c0 // 128, (c0 + n) // 128)]
                                    else:
                                        rhs = ikT2n[rows, 32 * b:32 * b + 32]
                                        rr = [ikT2n_r]
                                    k.op("pe", lambda e, g=g, h=h, b=b, n=n, rows=rows, rhs=rhs: e.matmul(
                                        g[:, 0:n], lhsT=iqTz[b][rows, h // 2, :], rhs=rhs, start=(b == 0), stop=(b == 3)),
                                        reads=[iqTz_r] + rr, writes=[gr], signal=(b == 3))
                            items.append((h, g, gr))
                        nxt = []
                        for (h, g, gr) in items:
                            r_ = R[cnt["r"] % 4]; rr_ = R_r[cnt["r"] % 4]; cnt["r"] += 1
                            k.op("act", lambda e, g=g, r_=r_, n=n: e.activation(out=r_[:, 0:n], in_=g[:, 0:n], func=AF.Relu, scale=0.125),
                                 reads=[gr], writes=[rr_])
                            nxt.append((h, r_, rr_))
                        if pend is not None:
                            pend()
                        yield
                        def pend(nxt=nxt, n=n, c0=c0):
                            for (h, r_, rr_) in nxt:
                                k.op("pe", lambda e, h=h, r_=r_: e.matmul(
                                    G1[:, 0:n], lhsT=DIAG[:, h, :], rhs=r_[:, 0:n], start=(h == 0), stop=(h == 7)),
                                    reads=[DIAG_r, rr_], writes=[G1_r], signal=(h == 7))
                                if h == 7:
                                    k.op("dve", lambda e: e.tensor_copy(out=I_[:, c0:c0 + n], in_=G1[:, 0:n]), reads=[G1_r], writes=[Ir])
                pend()
                yield

            def S1b(t):
                smp = (t == 16)
                nk = 2080 if smp else 128 * (t + 1)
                m_ = M[t % 2]; m_r = M_r[t % 2]
                I_ = I[t % 2]; Ir = I_r[t % 2]
                if t >= 2:
                    k.op("dve", lambda e: e.tensor_reduce(out=BS[:, 0:1], in_=I_[:, 0:nk], axis=mybir.AxisListType.X, op=ALU.max),
                         reads=[Ir], writes=[BS_r])
                    k.op("dve", lambda e: e.tensor_reduce(out=BS[:, 1:2], in_=I_[:, 0:nk], axis=mybir.AxisListType.X, op=ALU.min),
                         reads=[Ir], writes=[BS_r])
                    if not smp:
                        k.op("pool", lambda e: e.memset(I_[0:64, nk - 64:nk], NEG), reads=[BS_r], writes=[Ir])
                    k.op("dve", lambda e: e.tensor_tensor(out=BS[:, 2:3], in0=BS[:, 0:1], in1=BS[:, 1:2], op=ALU.subtract),
                         reads=[BS_r], writes=[BS_r])
                    k.op("dve", lambda e: e.tensor_scalar(out=HK[:], in0=POW2[:], scalar1=BS[:, 2:3], scalar2=None, op0=ALU.mult),
                         reads=[POW2_r, BS_r], writes=[HK_r])
                    k.op("dve", lambda e: e.tensor_tensor(out=BS[:, 3:4], in0=BS[:, 1:2], in1=HK[:, 0:1], op=ALU.add),
                         reads=[BS_r, HK_r], writes=[BS_r])
                    yield
                    for kk in range(NIT):
                        k.op("dve", lambda e: e.tensor_scalar(out=m_[:, 0:nk], in0=I_[:, 0:nk], scalar1=BS[:, 3:4], scalar2=None,
                                                              op0=ALU.is_gt, op1=ALU.add, accum_out=BS[:, 4:5]),
                             reads=[Ir, BS_r], writes=[m_r, BS_r])
                        k.op("dve", lambda e, kk=kk: e.tensor_scalar(out=BS[:, 5:6], in0=BS[:, 4:5], scalar1=TOPK - 0.5,
                                                                     scalar2=HK[:, kk:kk + 1], op0=ALU.is_gt, op1=ALU.mult),
                             reads=[BS_r, HK_r], writes=[BS_r])
                        sub = kk + 1 if kk < NIT - 1 else kk
                        dst = BS[:, 3:4] if kk < NIT - 1 else BS[:, 6:7]
                        k.op("dve", lambda e, sub=sub, dst=dst: e.scalar_tensor_tensor(
                            out=dst, in0=BS[:, 5:6], scalar=HK[:, sub:sub + 1], in1=BS[:, 3:4], op0=ALU.subtract, op1=ALU.add),
                            reads=[BS_r, HK_r], writes=[BS_r])
                        yield
                    k.op("dve", lambda e: e.tensor_scalar(out=m_[:, 0:nk], in0=I_[:, 0:nk], scalar1=BS[:, 6:7], scalar2=None,
                                                          op0=ALU.is_gt), reads=[Ir, BS_r], writes=[m_r])
                else:
                    k.op("pool", lambda e: e.memset(I_[0:64, nk - 64:nk], NEG), writes=[Ir])
                    k.op("dve", lambda e: e.tensor_scalar(out=m_[:, 0:nk], in0=I_[:, 0:nk], scalar1=-1.0e29, scalar2=None,
                                                          op0=ALU.is_gt), reads=[Ir], writes=[m_r])
                if STOP_AFTER == "dsa" and t in (3, 16):
                    sl = 0 if t == 3 else 1
                    k.dma("sp", A["dbgI"][sl, :, 0:nk], I_[:, 0:nk], reads=[Ir])
                    k.dma("pool", A["dbgM"][sl, :, 0:nk], m_[:, 0:nk], reads=[m_r])
                yield

            def attend_step(b, m_src, kt_fn, kt_rs, v_fn, v_rs, q_, q_r, first, last, par=None, lb=None, lb_r=None):
                cs = slice(0, 128) if b is None else slice(32 * b, 32 * b + 32)
                ncol = 128 if b is None else 32
                st_ = {}

                def front():
                    for i in range(4):
                        for par_ in range(2):
                            rows = slice(64 * par_, 64 * par_ + 64)
                            k.op("pe", lambda e, i=i, par_=par_, rows=rows: e.matmul(
                                L[par_][:, i, 0:ncol], lhsT=kt_fn(rows, i), rhs=q_[rows, i, cs], start=True, stop=True),
                                reads=list(kt_rs) + [q_r], writes=[L_r[par_]], signal=(i == 3))
                    ps_ = []
                    for par_ in range(2):
                        ei = cnt["e%d" % par_] % 2; cnt["e%d" % par_] += 1
                        e_ = E[par_][ei]; er = E_r[par_][ei]
                        k.op("act", lambda e, par_=par_, e_=e_: e.activation(out=e_[:, :, 0:ncol], in_=L[par_][:, :, 0:ncol],
                                                                            func=AF.Exp, scale=0.125),
                             reads=[L_r[par_]], writes=[er])
                        if b is None:
                            p_ = PT[par_][ei]; pr = PT_r[par_][ei]
                        else:
                            p_ = PTz[par_][ei]; pr = PTz_r[par_][ei]
                        k.op("pool", lambda e, e_=e_, p_=p_: e.tensor_tensor(
                            out=p_[:, :, cs], in0=e_[:, :, 0:ncol], in1=bc(m_src, 1, [128, 4, ncol]), op=ALU.mult),
                            reads=[er, MT_r, MTn_r], writes=[pr])
                        ps_.append((p_, pr))
                    st_["p"] = ps_

                def pv():
                    for par_ in range(2):
                        p_, pr = st_["p"][par_]
                        for i in range(4):
                            h = 2 * i + par_
                            k.op("pe", lambda e, i=i, h=h, par_=par_, p_=p_: e.matmul(
                                O[par_][:, i, :], lhsT=p_[:, i, :], rhs=v_fn(h), start=(first and i == 0), stop=last,
                                skip_group_check=True),
                                reads=[pr] + list(v_rs), writes=[O_r[par_]], signal=(i == 3))
                return [front, pv, None]

            def run_steps(steps):
                prev = None
                for st_ in steps:
                    st_[0]()
                    if prev is not None:
                        prev[1]()
                        if prev[2] is not None:
                            prev[2]()
                    prev = st_
                    yield
                if prev is not None:
                    prev[1]()
                    if prev[2] is not None:
                        prev[2]()
                yield

            def finalize(t):
                for par in range(2):
                    k.op("dve", lambda e, par=par: e.reciprocal(out=RD[:, par, :], in_=O[par][:, :, 64]), reads=[O_r[par]], writes=[RD_r])
                    k.op("dve", lambda e, par=par: e.tensor_tensor(
                        out=OATT[:, :, par, :], in0=O[par][:, :, 0:64], in1=bc(RD[:, par, :], 2, [128, 4, 64]), op=ALU.mult),
                        reads=[O_r[par], RD_r], writes=[OATT_r])
                of = OATT[:].rearrange("p a b d -> p (a b d)")
                o_ = OT[t % 2]; o_r = OT_r[t % 2]
                transposes_bf(lambda q: of[:, q * 128:(q + 1) * 128], 4, [OATT_r], o_[:], [o_r], "act")
                k.dma("sp", OATS[t], o_[:].rearrange("p a b -> p (a b)"), reads=[o_r], writes=[OATS_r[t]])
                if STOP_AFTER == "dsa":
                    k.dma("pool", A["dbg"][t * 128:(t + 1) * 128, :], of, reads=[OATT_r])

            def S2(t):
                m_ = M[t % 2]; m_r = M_r[t % 2]
                q_ = qT[t % 4]; q_r = qT_r[t % 4]
                if t < 16:
                    for j0 in range(0, t + 1, 4):
                        nb = min(4, t + 1 - j0)
                        transposes_bf(lambda q, j0=j0: m_[:, (j0 + q) * 128:(j0 + q + 1) * 128], nb, [m_r],
                                      MT[:, j0:j0 + nb, :] if nb > 1 else MT[:, j0, :], [MT_r], "act")
                    yield
                    steps = []
                    for j in range(t + 1):
                        steps.append(attend_step(None, MT[:, j, :], lambda rows, i, j=j: kT[rows, i, j * 128:(j + 1) * 128], [kT_r[j]],
                                                 lambda h, j=j: Vaug[:, j, h, :], [V_r[j]], q_, q_r, (j == 0), (j == t)))
                    yield from run_steps(steps)
                    finalize(t)
                    yield
                    return
                for j0 in range(0, 16, 4):
                    transposes_bf(lambda q, j0=j0: m_[:, (j0 + q) * 128:(j0 + q + 1) * 128], 4, [m_r], MT[:, j0:j0 + 4, :], [MT_r], "act")
                for b in range(4):
                    k.op("pool", lambda e, b=b: e.tensor_scalar(out=Mnew[:, 32 * b:32 * b + 32], in0=m_[:, 2048:2080],
                                                                scalar1=ROWM[:, b:b + 1], scalar2=None, op0=ALU.mult),
                         reads=[m_r, ROWM_r], writes=[Mnew_r])
                transposes_bf(lambda q: Mnew[:], 1, [Mnew_r], MTn[:], [MTn_r], "act")
                def prep(b, kt):
                    ks = KS[cnt["ks"] % 2]; ksr = KS_r[cnt["ks"] % 2]; cnt["ks"] += 1
                    k.dma("sp", ks[:], A["ck"][b, kt * 128:(kt + 1) * 128, :], writes=[ksr])
                    for q in range(4):
                        k.op("pe", lambda e, q=q, ks=ks: e.transpose(TF[:, q, :], ks[:, q * 128:(q + 1) * 128], identf[:]),
                             reads=[ksr, identf_r], writes=[TF_r], signal=(q == 3))
                    k.op("dve", lambda e: e.tensor_copy(out=kT[:, :, kt * 128:(kt + 1) * 128], in_=TF[:]), reads=[TF_r], writes=[kT_r[kt]])
                    vs = VS[cnt["vs"] % 2]; vsr = VS_r[cnt["vs"] % 2]; cnt["vs"] += 1
                    k.dma("sp", vs[:], A["cv"][b, kt * 128:(kt + 1) * 128, :], writes=[vsr])
                    k.op("act", lambda e: e.copy(out=Vaug[:, kt, :, 0:64], in_=vs[:].rearrange("p (h d) -> p h d", d=64)),
                         reads=[vsr], writes=[V_r[kt]])

                for kt in range(16):
                    prep(0, kt)
                    yield
                for b in range(4):
                    for par in range(2):
                        for r in range(2):
                            k.op("pool", lambda e, par=par, r=r: e.memset(PTz[par][r][:], 0.0), writes=[PTz_r[par][r]])
                    steps = []
                    for j in range(17):
                        if j < 16:
                            stp = attend_step(b, MT[:, j, 32 * b:32 * b + 32],
                                              lambda rows, i, j=j: kT[rows, i, j * 128:(j + 1) * 128], [kT_r[j]],
                                              lambda h, j=j: Vaug[:, j, h, :], [V_r[j]], q_, q_r, (b == 0 and j == 0), False)
                            if b < 3:
                                stp[2] = (lambda b=b, j=j: prep(b + 1, j))
                        else:
                            stp = attend_step(b, MTn[:, 32 * b:32 * b + 32], lambda rows, i: akTn[rows, i, :], [akTn_r],
                                              lambda h: Vn[:, h, :], [Vn_r], q_, q_r, False, (b == 3))
                        steps.append(stp)
                    yield from run_steps(steps)
                finalize(t)
                yield

            def interleave(g1, n1, g2, n2):
                a1 = a2 = 0
                d1 = d2 = False
                while not (d1 and d2):
                    take1 = (not d1) and (d2 or a1 * n2 <= a2 * n1)
                    if take1:
                        try:
                            next(g1); a1 += 1
                        except StopIteration:
                            d1 = True
                    else:
                        try:
                            next(g2); a2 += 1
                        except StopIteration:
                            d2 = True

            cnt["e0"] = 0; cnt["e1"] = 0

            def interleave_n(gens):
                gens = [[g, n, 0, False] for (g, n) in gens]
                while any(not x[3] for x in gens):
                    live = [x for x in gens if not x[3]]
                    x = min(live, key=lambda x: x[2] / float(x[1]))
                    try:
                        next(x[0]); x[2] += 1
                    except StopIteration:
                        x[3] = True

            def n_s1s(t):
                return 2 + 4 * (5 if t == 16 else (t + 4) // 4)

            for r in range(NT + 3):
                gens = []
                if r < NT:
                    gens.append((S1x(r), 14))
                if 0 <= r - 1 < NT:
                    gens.append((S1s(r - 1), n_s1s(r - 1)))
                if 0 <= r - 2 < NT:
                    gens.append((S1b(r - 2), NIT + 3))
                if 0 <= r - 3 < NT:
                    t2 = r - 3
                    gens.append((S2(t2), 2 * (t2 + 5) if t2 < 16 else 90))
                interleave_n(gens)
            k.barrier()

    def ro_stage(P):
        X, XR = P["X"], P["XR"]
        with contextlib.ExitStack() as es:
            stage_ln_tiles(es, P)
            WR = sb(es, "WR", [128, 8, 2048], BF16); WR_r = [Res("WR%d" % i) for i in range(4)]
            WO = sb(es, "WO", [128, 8, D], BF16); WO_r = Res("WO")
            wi_v = A["w_in"].rearrange("(kc p) n -> p kc n", p=128)
            for i in range(4):
                k.dma("pool", WR[:, :, i * 512:(i + 1) * 512], wi_v[:, :, i * 512:(i + 1) * 512], writes=[WR_r[i]])
            k.dma("pool", WO[:], A["w_out"].rearrange("(kc p) n -> p kc n", p=128), writes=[WO_r])
            RCS = [sb(es, "RCS%d" % i, [128, 2, 64]) for i in range(2)]; RCS_r = [Res("RCS%d" % i) for i in range(2)]
            DEC = sb(es, "DEC", [128, 2, 8]); GC = sb(es, "GC", [128, 2, 4]); CMASK = sb(es, "CMASK", [128, 2, 128]); CN_r = Res("CN")
            k.dma("sp", DEC[:], A["dec"], writes=[CN_r])
            k.dma("sp", GC[:], A["gc"], writes=[CN_r])
            k.dma("sp", CMASK[:], A["cmask"], writes=[CN_r])
            S32 = sb(es, "S32", [128, 4, 128]); S32_r = Res("S32")
            Sbf = sb(es, "Sbf", [128, 4, 128], BF16); Sbf_r = Res("Sbf")
            S0 = [sb(es, "S0_%d" % b, [128, 4, 128]) for b in range(4)]; S0_r = [Res("S0_%d" % b) for b in range(4)]
            S0b = [sb(es, "S0b_%d" % b, [128, 4, 128], BF16) for b in range(4)]; S0b_r = [Res("S0b_%d" % b) for b in range(4)]
            QTz = [sb(es, "QTz%d" % b, [128, 4, 128], BF16) for b in range(4)]; QTz_r = Res("QTz")
            KMb = sb(es, "KMb", [128, 4, 128], BF16); KMb_r = Res("KMb")
            XM = [sb(es, "XMr%d" % i, [128, 8, 128], BF16) for i in range(2)]; XM_r = [Res("XMr%d" % i) for i in range(2)]
            ZR = sb(es, "ZR", [128, 8, 128]); ZR_r = Res("ZR")
            ZRb = [sb(es, "ZRb%d" % i, [128, 8, 128], BF16) for i in range(2)]; ZRb_r = [Res("ZRb%d" % i) for i in range(2)]
            RT = [sb(es, "RTr%d" % i, [128, 8, 64]) for i in range(4)]; RT_r = [Res("RTr%d" % i) for i in range(4)]
            Vb = [sb(es, "Vb%d" % i, [128, 4, 128], BF16) for i in range(2)]; Vb_r = [Res("Vb%d" % i) for i in range(2)]
            SG = [sb(es, "SG%d" % i, [128, 512]) for i in range(2)]; SG_r = [Res("SG%d" % i) for i in range(2)]
            QKT = [sb(es, "QKT%d" % i, [128, 8, 128], BF16) for i in range(2)]; QKT_r = [Res("QKT%d" % i) for i in range(2)]
            ST = sb(es, "ST", [128, 4, 128], BF16); ST_r = Res("ST")
            hst = sb(es, "hst", [128, 4, 6]); hmv = sb(es, "hmv", [128, 4, 2]); hrs = sb(es, "hrs", [128, 4]); hs_r = Res("hs")
            ON = sb(es, "ON", [128, 4, 128]); ON_r = Res("ON")
            ORb = [sb(es, "ORb%d" % i, [128, 512], BF16) for i in range(2)]; ORb_r = [Res("ORb%d" % i) for i in range(2)]
            ORT = [sb(es, "ORT%d" % i, [128, 4, 128], BF16) for i in range(2)]; ORT_r = [Res("ORT%d" % i) for i in range(2)]
            OTl = [sb(es, "OTl%d" % i, [128, 4, 128], BF16) for i in range(2)]; OTl_r = [Res("OTl%d" % i) for i in range(2)]
            T = [sb(es, "Tr%d" % i, [128, D]) for i in range(2)]; T_r = [Res("Tr%d" % i) for i in range(2)]
            st = [sb(es, "str%d" % i, [128, 2, 6]) for i in range(2)]; st_r = [Res("str%d" % i) for i in range(2)]
            mv = [sb(es, "mvr%d" % i, [128, 4]) for i in range(2)]; mv_r = [Res("mvr%d" % i) for i in range(2)]
            G = [ps(es, "Gr%d" % i, [128, 512]) for i in range(2)]; G_r = [Res("Gr%d" % i) for i in range(2)]
            TF = ps(es, "TFr", [128, 4, 128]); TF_r = Res("TFr")
            TBa = ps(es, "TBr0", [128, 1, 4, 128], BF16); TBb = ps(es, "TBr1", [128, 1, 4, 128], BF16)
            TBs = [TBa, TBb]; TB_r = [Res("TBr0"), Res("TBr1")]
            SC = ps(es, "SC", [128, 4, 128]); SC_r = Res("SC")
            OP = ps(es, "OP", [128, 4, 128]); OP_r = Res("OP")
            UP = ps(es, "UP", [128, 4, 128]); UP_r = Res("UP")
            cnt = {"tb": 0, "g": 0}

            load_stage_consts(P, 1, "ln2g", "ln2b", G, G_r, T[0][0:5, :], T_r[0])
            for b in range(4):
                k.dma("sp", S0[b][:], A["sret"][b].rearrange("h k v -> k h v"), writes=[S0_r[b]])
                k.op("pool", lambda e, b=b: e.tensor_copy(out=S0b[b][:], in_=S0[b][:]), reads=[S0_r[b]], writes=[S0b_r[b]])
                k.op("pool", lambda e, b=b: e.memset(QTz[b][:], 0.0), writes=[QTz_r])

            def transposes_bf(src_fn, nblk, src_rs, dst_ap, dst_rs, evac):
                hb = cnt["tb"] % 2; cnt["tb"] += 1
                for q in range(nblk):
                    k.op("pe", lambda e, q=q, hb=hb: e.transpose(TBs[hb][:, 0, q, :], src_fn(q), identb[:]),
                         reads=list(src_rs) + [identb_r], writes=[TB_r[hb]], signal=(q == nblk - 1))
                if evac == "act":
                    k.op("act", lambda e: e.copy(out=dst_ap, in_=TBs[hb][:, 0, 0:nblk, :]), reads=[TB_r[hb]], writes=list(dst_rs))
                else:
                    k.op("dve", lambda e: e.tensor_copy(out=dst_ap, in_=TBs[hb][:, 0, 0:nblk, :]), reads=[TB_r[hb]], writes=list(dst_rs))

            def RA1(t):
                xm = XM[t % 2]; xm_r = XM_r[t % 2]
                rcs = RCS[t % 2]; rcs_r = RCS_r[t % 2]
                k.dma("sp", rcs[:, 0, :], A["rcos"][:, t, :], writes=[rcs_r])
                k.dma("sp", rcs[:, 1, :], A["rsin"][:, t, :], writes=[rcs_r])
                make_xmT(t, lambda kc, t=t: X[:, t, kc * 128:(kc + 1) * 128], XR[t], 3, 4, [TF, TF], [TF_r, TF_r],
                         lambda kc, c0, c1: xm[:, kc, c0:c1], xm_r)

            def RA(t):
                smp = (t == 16); kind = 1 if smp else 0
                xm = XM[t % 2]; xm_r = XM_r[t % 2]
                rcs = RCS[t % 2]; rcs_r = RCS_r[t % 2]
                zrb = ZRb[t % 2]; zrb_r = ZRb_r[t % 2]; vb = Vb[t % 2]; vb_r = Vb_r[t % 2]
                sg = SG[t % 2]; sg_r = SG_r[t % 2]; qkt = QKT[t % 2]; qkt_r = QKT_r[t % 2]
                for gi in range(4):
                    if gi % 2 == 0:
                        g = TF[:].rearrange("p a b -> p (a b)"); gr = TF_r
                    else:
                        g = UP[:].rearrange("p a b -> p (a b)"); gr = UP_r
                    for kc in range(8):
                        k.op("pe", lambda e, kc=kc, g=g, gi=gi: e.matmul(
                            g[:, :], lhsT=xm[:, kc, :], rhs=WR[:, kc, gi * 512:(gi + 1) * 512], start=(kc == 0), stop=(kc == 7)),
                            reads=[xm_r, WR_r[gi]], writes=[gr], signal=(kc == 7))
                    gv = g[:, :].rearrange("p (h d) -> p h d", d=128)
                    if gi < 2:
                        k.op("act", lambda e, gv=gv, gi=gi: e.copy(out=ZR[:, 4 * gi:4 * gi + 4, :], in_=gv), reads=[gr], writes=[ZR_r])
                    elif gi == 2:
                        k.op("act", lambda e, gv=gv: e.copy(out=vb[:], in_=gv), reads=[gr], writes=[vb_r])
                    else:
                        k.op("act", lambda e, g=g: e.activation(out=sg[:], in_=g[:, :], func=AF.Silu), reads=[gr], writes=[sg_r])
                    yield
                cosb = bc(rcs[:, 0, :], 1, [128, 8, 64]); sinb = bc(rcs[:, 1, :], 1, [128, 8, 64])
                x1 = ZR[:, :, 0:64]; x2 = ZR[:, :, 64:128]
                k.op("dve", lambda e: e.tensor_tensor(out=RT[0][:], in0=x1, in1=cosb, op=ALU.mult), reads=[ZR_r, rcs_r], writes=[RT_r[0]])
                k.op("dve", lambda e: e.tensor_tensor(out=RT[1][:], in0=x2, in1=sinb, op=ALU.mult), reads=[ZR_r, rcs_r], writes=[RT_r[1]])
                k.op("pool", lambda e: e.tensor_tensor(out=RT[2][:], in0=x2, in1=cosb, op=ALU.mult), reads=[ZR_r, rcs_r], writes=[RT_r[2]])
                k.op("pool", lambda e: e.tensor_tensor(out=RT[3][:], in0=x1, in1=sinb, op=ALU.mult), reads=[ZR_r, rcs_r], writes=[RT_r[3]])
                k.op("dve", lambda e: e.tensor_tensor(out=x1, in0=RT[0][:], in1=RT[1][:], op=ALU.subtract),
                     reads=[RT_r[0], RT_r[1]], writes=[ZR_r])
                k.op("pool", lambda e: e.tensor_tensor(out=x2, in0=RT[2][:], in1=RT[3][:], op=ALU.add),
                     reads=[RT_r[2], RT_r[3]], writes=[ZR_r])
                k.op("dve", lambda e: e.tensor_tensor(out=zrb[:], in0=ZR[:], in1=bc(DEC[:, kind, :], 2, [128, 8, 128]), op=ALU.mult),
                     reads=[ZR_r, CN_r], writes=[zrb_r])
                yield
                transposes_bf(lambda q: zrb[:, q, :], 4, [zrb_r], qkt[:, 0:4, :], [qkt_r], "act")
                yield
                transposes_bf(lambda q: zrb[:, 4 + q, :], 4, [zrb_r], qkt[:, 4:8, :], [qkt_r], "act")
                yield

            def RB(t):
                smp = (t == 16); kind = 1 if smp else 0
                zrb = ZRb[t % 2]; zrb_r = ZRb_r[t % 2]; vb = Vb[t % 2]; vb_r = Vb_r[t % 2]
                sg = SG[t % 2]; sg_r = SG_r[t % 2]; qkt = QKT[t % 2]; qkt_r = QKT_r[t % 2]
                for h in range(4):
                    k.op("pe", lambda e, h=h: e.matmul(SC[:, h, :], lhsT=qkt[:, 4 + h, :], rhs=qkt[:, h, :], start=True, stop=True),
                         reads=[qkt_r], writes=[SC_r], signal=(h == 3))
                k.op("dve", lambda e: e.tensor_tensor(out=ST[:], in0=SC[:], in1=bc(CMASK[:, kind, :], 1, [128, 4, 128]), op=ALU.mult),
                     reads=[SC_r, CN_r], writes=[ST_r])
                if smp:
                    for b in range(4):
                        k.op("pool", lambda e, b=b: e.tensor_copy(out=QTz[b][:, :, 32 * b:32 * b + 32], in_=qkt[:, 0:4, 32 * b:32 * b + 32]),
                             reads=[qkt_r], writes=[QTz_r])
                yield
                for h in range(4):
                    cross = smp or t > 0
                    k.op("pe", lambda e, h=h, cross=cross: e.matmul(OP[:, h, :], lhsT=ST[:, h, :], rhs=vb[:, h, :], start=True, stop=(not cross)),
                         reads=[ST_r, vb_r], writes=[OP_r], signal=(not cross and h == 3))
                    if smp:
                        for b in range(4):
                            k.op("pe", lambda e, h=h, b=b: e.matmul(OP[:, h, :], lhsT=QTz[b][:, h, :], rhs=S0b[b][:, h, :],
                                                                    start=False, stop=(b == 3)),
                                 reads=[QTz_r, S0b_r[b]], writes=[OP_r], signal=(b == 3 and h == 3))
                    elif t > 0:
                        k.op("pe", lambda e, h=h: e.matmul(OP[:, h, :], lhsT=qkt[:, h, :], rhs=Sbf[:, h, :], start=False, stop=True),
                             reads=[qkt_r, Sbf_r], writes=[OP_r], signal=(h == 3))
                yield
                gcb = bc(GC[:, kind, :], 2, [128, 4, 128])
                if not smp:
                    for h in range(4):
                        k.op("pe", lambda e, h=h: e.matmul(UP[:, h, :], lhsT=zrb[:, 4 + h, :], rhs=vb[:, h, :], start=True, stop=True),
                             reads=[zrb_r, vb_r], writes=[UP_r], signal=(h == 3))
                    if t == 0:
                        k.op("dve", lambda e: e.tensor_tensor(out=S32[:], in0=UP[:], in1=gcb, op=ALU.mult), reads=[UP_r, CN_r], writes=[S32_r])
                    else:
                        k.op("dve", lambda e: e.tensor_tensor(out=S32[:], in0=S32[:], in1=UP[:], op=ALU.add), reads=[UP_r, S32_r], writes=[S32_r])
                        k.op("dve", lambda e: e.tensor_tensor(out=S32[:], in0=S32[:], in1=gcb, op=ALU.mult), reads=[S32_r, CN_r], writes=[S32_r])
                    if t < 15:
                        k.op("pool", lambda e: e.tensor_copy(out=Sbf[:], in_=S32[:]), reads=[S32_r], writes=[Sbf_r])
                    else:
                        k.dma("sp", A["stp"].rearrange("h k v -> k h v"), S32[:], reads=[S32_r])
                else:
                    for b in range(4):
                        k.op("pool", lambda e, b=b: e.tensor_scalar(out=KMb[:], in0=zrb[:, 4:8, :], scalar1=ROWM[:, b:b + 1], scalar2=None,
                                                                    op0=ALU.mult), reads=[zrb_r, ROWM_r], writes=[KMb_r])
                        for h in range(4):
                            k.op("pe", lambda e, h=h: e.matmul(UP[:, h, :], lhsT=KMb[:, h, :], rhs=vb[:, h, :], start=True, stop=True),
                                 reads=[KMb_r, vb_r], writes=[UP_r], signal=(h == 3))
                        k.op("dve", lambda e, b=b: e.tensor_tensor(out=S0[b][:], in0=S0[b][:], in1=UP[:], op=ALU.add),
                             reads=[UP_r, S0_r[b], S0b_r[b]], writes=[S0_r[b]])
                        k.op("dve", lambda e, b=b: e.tensor_tensor(out=S0[b][:], in0=S0[b][:], in1=gcb, op=ALU.mult),
                             reads=[S0_r[b], CN_r], writes=[S0_r[b]])
                        k.dma("sp", A["sts"][b].rearrange("h k v -> k h v"), S0[b][:], reads=[S0_r[b]])
                yield
                for h in range(4):
                    k.op("dve", lambda e, h=h: e.bn_stats(out=hst[:, h, :], in_=OP[:, h, :]), reads=[OP_r], writes=[hs_r])
                for h in range(4):
                    k.op("dve", lambda e, h=h: e.bn_aggr(out=hmv[:, h, :], in_=hst[:, h, :]), reads=[hs_r], writes=[hs_r])
                k.op("act", lambda e: e.activation(out=hrs[:], in_=hmv[:, :, 1], func=AF.Sqrt, bias=epsc[:, 0:1], scale=1.0),
                     reads=[hs_r, epsc_r], writes=[hs_r])
                k.op("dve", lambda e: e.reciprocal(out=hrs[:], in_=hrs[:]), reads=[hs_r], writes=[hs_r])
                k.op("dve", lambda e: e.tensor_tensor(out=ON[:], in0=OP[:], in1=bc(hmv[:, :, 0], 2, [128, 4, 128]), op=ALU.subtract),
                     reads=[OP_r, hs_r], writes=[ON_r])
                yield
                k.op("pool", lambda e: e.tensor_tensor(out=ON[:], in0=ON[:], in1=bc(hrs[:], 2, [128, 4, 128]), op=ALU.mult),
                     reads=[ON_r, hs_r], writes=[ON_r])
                orb = ORb[t % 2]; orb_r = ORb_r[t % 2]
                k.op("pool", lambda e: e.tensor_tensor(out=orb[:], in0=ON[:].rearrange("p h d -> p (h d)"), in1=sg[:], op=ALU.mult),
                     reads=[ON_r, sg_r], writes=[orb_r])
                yield

            def RC(t):
                ort = ORT[t % 2]; ort_r = ORT_r[t % 2]
                ot = OTl[t % 2]; ot_r = OTl_r[t % 2]
                k.dma("sp", ot[:].rearrange("p a b -> p (a b)"), OATS[t], reads=[OATS_r[t]], writes=[ot_r])
                orb = ORb[t % 2]; orb_r = ORb_r[t % 2]
                transposes_bf(lambda q: orb[:, q * 128:(q + 1) * 128], 4, [orb_r], ort[:], [ort_r], "act")
                yield
                ys = []
                for half in range(2):
                    g = G[half]; gr = G_r[half]
                    ys.append((g, gr))
                    for c in range(8):
                        lhs = ort[:, c, :] if c < 4 else ot[:, c - 4, :]
                        k.op("pe", lambda e, c=c, g=g, half=half, lhs=lhs: e.matmul(
                            g[:, :], lhsT=lhs, rhs=WO[:, c, half * 512:(half + 1) * 512], start=(c == 0), stop=(c == 7)),
                            reads=[ort_r, ot_r, WO_r], writes=[gr], signal=(c == 7))
                    yield
                post_norm_ln(P, t, [ys[0][0], ys[1][0]], [ys[0][1], ys[1][1]], T[t % 2], T_r[t % 2], st[t % 2], st_r[t % 2],
                             mv[t % 2], mv_r[t % 2])
                if STOP_AFTER == "ro":
                    k.dma("sp", A["y"][t * 128:(t + 1) * 128, :], X[:, t, :], reads=[XR[t]])
                yield

            def interleave_n(gens):
                gens = [[g, n, 0, False] for (g, n) in gens]
                while any(not x[3] for x in gens):
                    live = [x for x in gens if not x[3]]
                    x = min(live, key=lambda x: x[2] / float(x[1]))
                    try:
                        next(x[0]); x[2] += 1
                    except StopIteration:
                        x[3] = True

            for t in range(min(3, NT)):
                P["reload"](t)
            RA1(0)
            for r in range(NT + 2):
                gens = []
                if r + 3 < NT:
                    P["reload"](r + 3)
                if r + 1 < NT:
                    RA1(r + 1)
                if r < NT:
                    gens.append((RA(r), 8))
                if 0 <= r - 1 < NT:
                    gens.append((RB(r - 1), 6))
                if 0 <= r - 2 < NT:
                    gens.append((RC(r - 2), 5))
                interleave_n(gens)
            k.barrier()

    P = {}
    with contextlib.ExitStack() as esA:
        X = sb(esA, "X", [128, NT, D])
        P["X"] = X
        P["XR"] = [Res("X%d" % t) for t in range(NT)]
        xin_v = A["xin"].rearrange("(t p) d -> p t d", p=128)
        for t in range(NT):
            k.dma("sp", X[:, t, :], xin_v[:, t, :], writes=[P["XR"][t]])
        cond_stage()
        ffn_stage(P, "f1g", "f1u", "f1d", 0, 1, 0, "ln1g", "ln1b", final=(STOP_AFTER == "ffn1"), spill=True)
        k.barrier()
    if STOP_AFTER == "ffn1":
        return
    dsa_stage()
    if STOP_AFTER == "dsa":
        return
    with contextlib.ExitStack() as esC:
        X = sb(esC, "X2", [128, NT, D])
        P["X"] = X
        P["XR"] = [Res("X2_%d" % t) for t in range(NT)]
        P["reload"] = lambda t: k.dma("sp", P["X"][:, t, :], XS[t * 128:(t + 1) * 128, :], reads=[XS_r[t]], writes=[P["XR"][t]])
        ro_stage(P)
        if STOP_AFTER != "ro":
            ffn_stage(P, "f2g", "f2u", "f2d", 6, 7, 2, "ln3g", "ln3b", final=True, spill=False)
        k.barrier()


_PROGRAM = None
_LAST = None


def kernel(x_prompt, x_sample, c_prompt, c_sample, cache_k, cache_v, cache_idx_k, state_ret,
           w_cond, b_cond, ffn1_w_gate, ffn1_w_up, ffn1_w_down, ln1_g, ln1_b, w_in, w_out, ln2_g, ln2_b,
           ffn2_w_gate, ffn2_w_up, ffn2_w_down, ln3_g, ln3_b):
    global _PROGRAM
    f = lambda a: np.ascontiguousarray(np.asarray(a, dtype=np.float32))
    x_prompt, x_sample, c_prompt, c_sample = f(x_prompt), f(x_sample), f(c_prompt), f(c_sample)
    cache_k, cache_v, cache_idx_k, state_ret = f(cache_k), f(cache_v), f(cache_idx_k), f(state_ret)
    consts = _consts()
    shared = {
        "w_cond": f(w_cond)[0], "b_cond": f(b_cond)[0].reshape(72, 128),
        "f1g": f(ffn1_w_gate)[0], "f1u": f(ffn1_w_up)[0], "f1d": f(ffn1_w_down)[0],
        "ln1g": f(ln1_g)[0].reshape(1, D), "ln1b": f(ln1_b)[0].reshape(1, D),
        "w_in": f(w_in)[0], "w_out": f(w_out)[0],
        "ln2g": f(ln2_g)[0].reshape(1, D), "ln2b": f(ln2_b)[0].reshape(1, D),
        "f2g": f(ffn2_w_gate)[0], "f2u": f(ffn2_w_up)[0], "f2d": f(ffn2_w_down)[0],
        "ln3g": f(ln3_g)[0].reshape(1, D), "ln3b": f(ln3_b)[0].reshape(1, D),
    }
    for n, v in consts.items():
        shared["k_" + n] = v
    in_maps = []
    for i in range(8):
        m = dict(shared)
        m["xin"] = np.concatenate([x_prompt[i], x_sample[4 * i:4 * i + 4].reshape(128, D)], axis=0)
        m["c5"] = np.concatenate([c_prompt[i:i + 1], c_sample[4 * i:4 * i + 4]], axis=0)
        m["ck"] = cache_k[0, 4 * i:4 * i + 4].reshape(4, 2048, 512)
        m["cv"] = cache_v[0, 4 * i:4 * i + 4].reshape(4, 2048, 512)
        m["cik"] = cache_idx_k[0, 4 * i:4 * i + 4]
        m["sret"] = state_ret[0, 4 * i:4 * i + 4]
        in_maps.append(m)
    if _PROGRAM is None:
        _PROGRAM = build_program()
    res = run_bass_kernel_spmd(_PROGRAM, in_maps, core_ids=list(range(8)))
    R = res.results
    global _LAST
    _LAST = R
    y = np.stack([r["y"] for r in R])
    nk = np.stack([r["newk"] for r in R])
    nv = np.stack([r["newv"] for r in R])
    nik = np.stack([r["newik"] for r in R])
    stp = np.stack([r["stp"] for r in R])
    sts = np.stack([r["sts"] for r in R])
    y_prompt = y[:, :2048].copy()
    y_sample = y[:, 2048:].reshape(32, 32, D).copy()
    new_k_prompt = nk[:, :2048].reshape(1, 8, 2048, 8, 64).copy()
    new_v_prompt = nv[:, :2048].reshape(1, 8, 2048, 8, 64).copy()
    new_idx_k_prompt = nik[:, :2048].reshape(1, 8, 2048, 64).copy()
    state_ret_prompt = stp.reshape(1, 8, 4, 128, 128).copy()
    new_k_sample = nk[:, 2048:].reshape(1, 32, 32, 8, 64).copy()
    new_v_sample = nv[:, 2048:].reshape(1, 32, 32, 8, 64).copy()
    new_idx_k_sample = nik[:, 2048:].reshape(1, 32, 32, 64).copy()
    state_ret_sample = sts.reshape(1, 32, 4, 128, 128).copy()
    return (y_prompt, y_sample, new_k_prompt, new_v_prompt, new_idx_k_prompt, state_ret_prompt,
            new_k_sample, new_v_sample, new_idx_k_sample, state_ret_sample)
```

```python
import contextlib
import math
import numpy as np
import concourse.bass as bass
import concourse.mybir as mybir
from concourse.bass_utils import run_bass_kernel_spmd

F32 = mybir.dt.float32
BF16 = mybir.dt.bfloat16
AF = mybir.ActivationFunctionType
ALU = mybir.AluOpType

NT = 17
D = 1024
DFF = 2816
NFC = 22
DIN = 4168
ALPHA = 2.0 ** 0.25
LN_EPS = 1e-5
NEG = -1.0e30
NIT = 18
TOPK = 256
RET_G = [1.0 - 2.0 ** (-5.0 - h) for h in range(4)]
STOP_AFTER = None


class Ev:
    __slots__ = ("sem", "val", "key")

    def __init__(self, sem, val, key):
        self.sem, self.val, self.key = sem, val, key


class Res:
    __slots__ = ("name", "w", "rs")

    def __init__(self, name):
        self.name = name
        self.w = None
        self.rs = {}


class K:
    def __init__(self, nc, es):
        self.nc = nc
        self.eng = {"pe": nc.tensor, "act": nc.scalar, "dve": nc.vector, "pool": nc.gpsimd, "sp": nc.sync}
        self.sem = {}
        self.cnt = {}
        for e in ("pe", "act", "dve", "pool"):
            self.sem[e] = es.enter_context(nc.semaphore("sem_" + e))
            self.cnt[e] = 0
        self.waited = {e: {} for e in self.eng}
        self.ring = {}
        for q, depth in (("sp", 8), ("pool", 6), ("act", 4)):
            sems = [es.enter_context(nc.semaphore("dq_%s_%d" % (q, i))) for i in range(depth)]
            self.ring[q] = {"sems": sems, "k": 0, "tgt": [0] * depth}
        self.pending_pe = False

    def _wait(self, e, ev):
        if ev is None:
            return
        if e == "pe" and ev.key == "pe":
            return
        if self.waited[e].get(ev.key, 0) >= ev.val:
            return
        self.eng[e].wait_ge(ev.sem, ev.val)
        self.waited[e][ev.key] = ev.val

    def _deps(self, e, reads, writes):
        for r in reads:
            self._wait(e, r.w)
        for w in writes:
            self._wait(e, w.w)
            for ev in w.rs.values():
                self._wait(e, ev)

    def _mark(self, ev, reads, writes):
        for r in reads:
            old = r.rs.get(ev.key)
            if old is None or old.val < ev.val:
                r.rs[ev.key] = ev
        for w in writes:
            w.w = ev
            w.rs = {}

    def op(self, e, fn, reads=(), writes=(), signal=True):
        self._deps(e, reads, writes)
        ins = fn(self.eng[e])
        if signal:
            self.cnt[e] += 1
            ins.then_inc(self.sem[e], 1)
            ev = Ev(self.sem[e], self.cnt[e], e)
            if e == "pe":
                self.pending_pe = False
        else:
            assert e == "pe"
            ev = Ev(self.sem[e], self.cnt[e] + 1, e)
            self.pending_pe = True
        self._mark(ev, reads, writes)
        return ev

    def dma(self, q, out, in_, reads=(), writes=()):
        rg = self.ring[q]
        d = len(rg["sems"])
        slot = rg["k"] % d
        sem = rg["sems"][slot]
        key = "dq_%s_%d" % (q, slot)
        if rg["tgt"][slot] > 0:
            self._wait(q, Ev(sem, rg["tgt"][slot], key))
        self._deps(q, reads, writes)
        rg["tgt"][slot] += 16
        rg["k"] += 1
        self.eng[q].dma_start(out=out, in_=in_).then_inc(sem, 16)
        ev = Ev(sem, rg["tgt"][slot], key)
        self._mark(ev, reads, writes)
        return ev

    def barrier(self, engines=("pe", "act", "dve", "pool", "sp")):
        assert not self.pending_pe
        for e in engines:
            for p in ("pe", "act", "dve", "pool"):
                if self.cnt[p] > 0 and self.waited[e].get(p, 0) < self.cnt[p]:
                    self.eng[e].wait_ge(self.sem[p], self.cnt[p])
                    self.waited[e][p] = self.cnt[p]
            for q, rg in self.ring.items():
                for slot, sem in enumerate(rg["sems"]):
                    key = "dq_%s_%d" % (q, slot)
                    if rg["tgt"][slot] > 0 and self.waited[e].get(key, 0) < rg["tgt"][slot]:
                        self.eng[e].wait_ge(sem, rg["tgt"][slot])
                        self.waited[e][key] = rg["tgt"][slot]


def _consts():
    c = {}
    c["ident_f"] = np.eye(128, dtype=np.float32)
    pos = np.zeros((NT, 128), np.float32)
    for t in range(16):
        pos[t] = 128 * t + np.arange(128)
    pos[16] = 2048 + (np.arange(128) % 32)
    inv_r = (1.0 / (np.float32(10000.0) ** (np.arange(0, 128, 2, dtype=np.float32) / np.float32(128)))).astype(np.float32)
    ang = (pos[:, :, None] * inv_r[None, None, :]).astype(np.float32)
    c["rcos"] = np.cos(ang).astype(np.float32).transpose(1, 0, 2).copy()
    c["rsin"] = np.sin(ang).astype(np.float32).transpose(1, 0, 2).copy()
    inv_a = (1.0 / (np.float32(500000.0) ** (np.arange(0, 16, 2, dtype=np.float32) / np.float32(16)))).astype(np.float32)
    ang = (pos[:, :, None] * inv_a[None, None, :]).astype(np.float32)
    c["acos"] = np.cos(ang).astype(np.float32).transpose(1, 0, 2).copy()
    c["asin"] = np.sin(ang).astype(np.float32).transpose(1, 0, 2).copy()
    dec = np.zeros((128, 2, 8), np.float64)
    for kind in range(2):
        n = np.arange(128) if kind == 0 else (np.arange(128) % 32)
        for h in range(4):
            g = RET_G[h]
            dec[:, kind, h] = g ** (n + 1.0)
            dec[:, kind, 4 + h] = (g ** (-(n + 1.0))) * (128.0 ** -0.5)
    c["dec"] = dec.astype(np.float32)
    gc = np.zeros((128, 2, 4), np.float64)
    for h in range(4):
        gc[:, 0, h] = RET_G[h] ** 128.0
        gc[:, 1, h] = RET_G[h] ** 32.0
    c["gc"] = gc.astype(np.float32)
    m = np.arange(128)
    cm = (m[:, None] <= m[None, :]).astype(np.float32)
    cms = cm * ((m[:, None] // 32) == (m[None, :] // 32)).astype(np.float32)
    c["cmask"] = np.stack([cm, cms], axis=1).copy()
    rowm = np.zeros((128, 4), np.float32)
    for b in range(4):
        rowm[32 * b:32 * b + 32, b] = 1.0
    c["rowm"] = rowm
    sel = np.zeros((5, 2, 128), np.float32)
    sel[0, 0, :] = 1.0
    for b in range(4):
        sel[1 + b, 1, 32 * b:32 * b + 32] = 1.0
    c["sel"] = sel
    c["pow2"] = np.tile((2.0 ** -(np.arange(NIT + 2) + 1.0)).astype(np.float32)[None, :], (128, 1)).copy()
    return c


CONST_SHAPES = {
    "ident_f": [128, 128], "rcos": [128, NT, 64], "rsin": [128, NT, 64], "acos": [128, NT, 8], "asin": [128, NT, 8],
    "dec": [128, 2, 8], "gc": [128, 2, 4], "cmask": [128, 2, 128], "rowm": [128, 4], "sel": [5, 2, 128],
    "pow2": [128, NIT + 2],
}

IN_SHAPES = {
    "xin": [NT * 128, D], "c5": [5, D],
    "ck": [4, 2048, 512], "cv": [4, 2048, 512], "cik": [4, 2048, 64], "sret": [4, 4, 128, 128],
    "w_cond": [D, 9 * D], "b_cond": [72, 128],
    "f1g": [D, DFF], "f1u": [D, DFF], "f1d": [DFF, D], "ln1g": [1, D], "ln1b": [1, D],
    "w_in": [D, DIN], "w_out": [D, D], "ln2g": [1, D], "ln2b": [1, D],
    "f2g": [D, DFF], "f2u": [D, DFF], "f2d": [DFF, D], "ln3g": [1, D], "ln3b": [1, D],
}
OUT_SHAPES = {
    "y": [NT * 128, D], "newk": [NT * 128, 512], "newv": [NT * 128, 512], "newik": [NT * 128, 64],
    "stp": [4, 128, 128], "sts": [4, 4, 128, 128],
}


def build_program():
    nc = bass.Bass("TRN2", target_bir_lowering=False)
    A = {}
    for n, s in IN_SHAPES.items():
        A[n] = nc.dram_tensor(n, s, F32, kind="ExternalInput").ap()
    for n, s in CONST_SHAPES.items():
        A[n] = nc.dram_tensor("k_" + n, s, F32, kind="ExternalInput").ap()
    for n, s in OUT_SHAPES.items():
        A[n] = nc.dram_tensor(n, s, F32, kind="ExternalOutput").ap()
    if STOP_AFTER == "dsa":
        A["dbg"] = nc.dram_tensor("dbg", [NT * 128, 512], F32, kind="ExternalOutput").ap()
        A["dbgI"] = nc.dram_tensor("dbgI", [2, 128, 2080], F32, kind="ExternalOutput").ap()
        A["dbgM"] = nc.dram_tensor("dbgM", [2, 128, 2080], F32, kind="ExternalOutput").ap()

    with contextlib.ExitStack() as es:
        k = K(nc, es)
        _emit(nc, k, A, es)
    return nc


def _emit(nc, k, A, es0):
    uid = [0]

    def sb(es, name, shape, dt=F32):
        uid[0] += 1
        return es.enter_context(nc.sbuf_tensor("s%d_%s" % (uid[0], name), shape, dt))

    def ps(es, name, shape, dt=F32):
        uid[0] += 1
        return es.enter_context(nc.psum_tensor("p%d_%s" % (uid[0], name), shape, dt))

    def bc(ap, axis, shape):
        return ap.unsqueeze(axis).broadcast_to(shape)

    identf = sb(es0, "identf", [128, 128]); identf_r = Res("identf")
    identb = sb(es0, "identb", [128, 128], BF16); identb_r = Res("identb")
    MODT = sb(es0, "MODT", [128, 72, 5]); MODT_r = Res("MODT")
    SEL = sb(es0, "SEL", [5, 2, 128]); SEL_r = Res("SEL")
    ROWM = sb(es0, "ROWM", [128, 4]); ROWM_r = Res("ROWM")
    epsc = sb(es0, "epsc", [128, 1]); epsc_r = Res("epsc")
    NHALF = sb(es0, "NHALF", [128, 4]); NHALF_r = Res("NHALF")
    XS = nc.dram_tensor("xs_scratch", [NT * 128, D], F32).ap()
    XS_r = [Res("XS%d" % t) for t in range(NT)]
    OATS = nc.dram_tensor("oat_scratch", [NT, 128, 512], BF16).ap()
    OATS_r = [Res("OATS%d" % t) for t in range(NT)]

    k.dma("sp", identf[:], A["ident_f"], writes=[identf_r])
    k.op("act", lambda e: e.copy(out=identb[:], in_=identf[:]), reads=[identf_r], writes=[identb_r])
    k.dma("sp", SEL[:], A["sel"], writes=[SEL_r])
    k.dma("sp", ROWM[:], A["rowm"], writes=[ROWM_r])
    k.op("dve", lambda e: e.memset(epsc[:], LN_EPS), writes=[epsc_r])
    k.op("dve", lambda e: e.memset(NHALF[:], -0.5), writes=[NHALF_r])

    def seq_cols(t):
        if t < 16:
            return [(0, 0, 128)]
        return [(1 + b, 32 * b, 32 * b + 32) for b in range(4)]

    def load_stage_consts(P, gidx, lng, lnb, gbps, gbps_r, GROW, GROW_r):
        GB, GB_r, LNG, LNG_r, LNB, LNB_r = P["GB"], P["GB_r"], P["LNG"], P["LNG_r"], P["LNB"], P["LNB_r"]
        k.dma("sp", LNG[:], A[lng].partition_broadcast(128), writes=[LNG_r])
        k.dma("sp", LNB[:], A[lnb].partition_broadcast(128), writes=[LNB_r])
        jg = (2, 5, 8)[gidx]
        for half in range(2):
            for q in range(4):
                c = half * 4 + q
                k.op("pe", lambda e, c=c, q=q, half=half: e.transpose(
                    gbps[half][0:5, q * 128:(q + 1) * 128], MODT[:, jg * 8 + c, :], identf[:]),
                    reads=[MODT_r, identf_r], writes=[gbps_r[half]], signal=(q == 3))
            k.op("act", lambda e, half=half: e.copy(out=GROW[:, half * 512:(half + 1) * 512], in_=gbps[half][0:5, :]),
                 reads=[gbps_r[half]], writes=[GROW_r])
        i = 0
        for kind in range(2):
            for half in range(2):
                g = gbps[i % 2]; gr = gbps_r[i % 2]; i += 1
                k.op("pe", lambda e, g=g, kind=kind, half=half: e.matmul(
                    g[:, :], lhsT=SEL[:, kind, :], rhs=GROW[:, half * 512:(half + 1) * 512], start=True, stop=True),
                    reads=[SEL_r, GROW_r], writes=[gr])
                k.op("act", lambda e, g=g, kind=kind, half=half: e.copy(out=GB[:, kind, half * 512:(half + 1) * 512], in_=g[:, :]),
                     reads=[gr], writes=[GB_r])

    def make_xmT(t, src_fn, src_r, jsh, jsc, tp, tp_r, dst_fn, dst_r):
        for half in range(2):
            p_ = tp[half]; pr = tp_r[half]
            for q in range(4):
                kc = half * 4 + q
                k.op("pe", lambda e, kc=kc, q=q, p_=p_: e.transpose(p_[:, q, :], src_fn(kc), identf[:]),
                     reads=[src_r, identf_r], writes=[pr], signal=(q == 3))
            for q in range(4):
                kc = half * 4 + q
                for (s, c0, c1) in seq_cols(t):
                    k.op("act", lambda e, kc=kc, q=q, s=s, c0=c0, c1=c1, p_=p_: e.activation(
                        out=dst_fn(kc, c0, c1), in_=p_[:, q, c0:c1], func=AF.Identity,
                        scale=MODT[:, jsc * 8 + kc, s:s + 1], bias=MODT[:, jsh * 8 + kc, s:s + 1]),
                        reads=[pr, MODT_r], writes=[dst_r])

    def post_norm_ln(P, t, yps, yps_r, T, T_r, st, st_r, mv, mv_r):
        X, XR = P["X"], P["XR"]
        GB, GB_r, LNG, LNG_r, LNB, LNB_r = P["GB"], P["GB_r"], P["LNG"], P["LNG_r"], P["LNB"], P["LNB_r"]
        kind = 0 if t < 16 else 1
        for half in range(2):
            k.op("dve", lambda e, half=half: e.tensor_tensor(
                out=T[:, half * 512:(half + 1) * 512], in0=yps[half][:, :], in1=GB[:, kind, half * 512:(half + 1) * 512],
                op=ALU.mult), reads=[yps_r[half], GB_r], writes=[T_r])
        k.op("dve", lambda e: e.scalar_tensor_tensor(out=T[:], in0=X[:, t, :], scalar=ALPHA, in1=T[:],
                                                     op0=ALU.mult, op1=ALU.add), reads=[XR[t], T_r], writes=[T_r])
        for half in range(2):
            k.op("dve", lambda e, half=half: e.bn_stats(out=st[:, half, :], in_=T[:, half * 512:(half + 1) * 512]),
                 reads=[T_r], writes=[st_r])
        k.op("dve", lambda e: e.bn_aggr(out=mv[:, 0:2], in_=st[:].rearrange("p a b -> p (a b)")), reads=[st_r], writes=[mv_r])
        k.op("dve", lambda e: e.tensor_scalar_add(out=mv[:, 2:3], in0=mv[:, 1:2], scalar1=LN_EPS), reads=[mv_r], writes=[mv_r])
        k.op("pool", lambda e: e.tensor_tensor(out=mv[:, 2:3], in0=mv[:, 2:3], in1=NHALF[:, 0:1], op=ALU.pow),
             reads=[mv_r, NHALF_r], writes=[mv_r])
        k.op("dve", lambda e: e.tensor_scalar(out=mv[:, 3:4], in0=mv[:, 0:1], scalar1=mv[:, 2:3], scalar2=-1.0,
                                              op0=ALU.mult, op1=ALU.mult), reads=[mv_r], writes=[mv_r])
        k.op("act", lambda e: e.activation(out=T[:], in_=T[:], func=AF.Identity, scale=mv[:, 2:3], bias=mv[:, 3:4]),
             reads=[T_r, mv_r], writes=[T_r])
        k.op("pool", lambda e: e.tensor_tensor(out=T[:], in0=T[:], in1=LNG[:], op=ALU.mult), reads=[T_r, LNG_r], writes=[T_r])
        k.op("pool", lambda e: e.tensor_tensor(out=X[:, t, :], in0=T[:], in1=LNB[:], op=ALU.add),
             reads=[T_r, LNB_r], writes=[XR[t]])

    def stage_ln_tiles(es, P):
        P["GB"] = sb(es, "GB", [128, 2, D]); P["GB_r"] = Res("GB")
        P["LNG"] = sb(es, "LNG", [128, D]); P["LNG_r"] = Res("LNG")
        P["LNB"] = sb(es, "LNB", [128, D]); P["LNB_r"] = Res("LNB")

    def cond_stage():
        with contextlib.ExitStack() as es:
            c5 = sb(es, "c5", [5, D]); c5_r = Res("c5")
            sc = sb(es, "sc", [5, D], BF16); sc_r = Res("sc")
            scT = sb(es, "scT", [128, 8, 5], BF16); scT_r = Res("scT")
            bcn = sb(es, "bc", [72, 128]); bc_r = Res("bc")
            bT = sb(es, "bT", [128, 72]); bT_r = Res("bT")
            WC = [sb(es, "WC%d" % i, [128, 8, D], BF16) for i in range(2)]
            WC_r = [Res("WC%d" % i) for i in range(2)]
            tps = ps(es, "tps", [128, 8, 8], BF16); tps_r = Res("tps")
            bps = ps(es, "bps", [128, 72]); bps_r = Res("bps")
            mps = ps(es, "mps", [128, 72, 5]); mps_r = Res("mps")
            k.dma("sp", c5[:], A["c5"], writes=[c5_r])
            k.dma("sp", bcn[:], A["b_cond"], writes=[bc_r])
            wc_v = A["w_cond"].rearrange("(kc p) n -> p kc n", p=128)
            for j in range(2):
                k.dma("pool", WC[j][:], wc_v[:, :, j * D:(j + 1) * D], writes=[WC_r[j]])
            k.op("act", lambda e: e.activation(out=sc[:], in_=c5[:], func=AF.Silu), reads=[c5_r], writes=[sc_r])
            for kc in range(8):
                k.op("pe", lambda e, kc=kc: e.transpose(tps[:, kc, 0:5], sc[:, kc * 128:(kc + 1) * 128], identb[0:5, 0:5]),
                     reads=[sc_r, identb_r], writes=[tps_r])
            k.op("act", lambda e: e.copy(out=scT[:], in_=tps[:, :, 0:5]), reads=[tps_r], writes=[scT_r])
            k.op("pe", lambda e: e.transpose(bps[:], bcn[:], identf[0:72, 0:72]), reads=[bc_r, identf_r], writes=[bps_r])
            k.op("act", lambda e: e.copy(out=bT[:], in_=bps[:]), reads=[bps_r], writes=[bT_r])
            for j in range(9):
                w = WC[j % 2]; wr = WC_r[j % 2]
                for c in range(8):
                    for kc in range(8):
                        k.op("pe", lambda e, c=c, kc=kc, w=w, j=j: e.matmul(
                            mps[:, j * 8 + c, :], lhsT=w[:, kc, c * 128:(c + 1) * 128], rhs=scT[:, kc, :],
                            start=(kc == 0), stop=(kc == 7)),
                            reads=[wr, scT_r], writes=[mps_r], signal=(kc == 7))
                if j + 2 < 9:
                    k.dma("pool", w[:], wc_v[:, :, (j + 2) * D:(j + 3) * D], writes=[wr])
            k.op("dve", lambda e: e.tensor_tensor(out=MODT[:], in0=mps[:], in1=bc(bT[:], 2, [128, 72, 5]), op=ALU.add),
                 reads=[mps_r, bT_r], writes=[MODT_r])
            for j in (1, 4, 7):
                k.op("dve", lambda e, j=j: e.tensor_scalar_add(out=MODT[:, j * 8:(j + 1) * 8, :], in0=MODT[:, j * 8:(j + 1) * 8, :],
                                                               scalar1=1.0), reads=[MODT_r], writes=[MODT_r])
            for j in (2, 5, 8):
                wgt = 1.0 if j == 5 else 0.5
                k.op("dve", lambda e, j=j, wgt=wgt: e.tensor_scalar(
                    out=MODT[:, j * 8:(j + 1) * 8, :], in0=MODT[:, j * 8:(j + 1) * 8, :], scalar1=1.0, scalar2=wgt,
                    op0=ALU.add, op1=ALU.mult), reads=[MODT_r], writes=[MODT_r])
            k.barrier()

    def ffn_stage(P, wg, wu, wd, jsh, jsc, gidx, lng, lnb, final, spill):
        X, XR = P["X"], P["XR"]
        with contextlib.ExitStack() as es:
            stage_ln_tiles(es, P)
            blocks = [list(range(0, 6)), list(range(6, 12)), list(range(12, 17))]
            WD = sb(es, "WD", [128, NFC, D], BF16); WD_r = [Res("WD%d" % i) for i in range(4)]
            WG = [sb(es, "WG%d" % i, [128, 8, 256], BF16) for i in range(2)]
            WU = [sb(es, "WU%d" % i, [128, 8, 256], BF16) for i in range(2)]
            WGU_r = [Res("WGU%d" % i) for i in range(2)]
            xmT = sb(es, "xmT", [128, 8, 768], BF16); xmT_r = Res("xmT")
            H = sb(es, "H", [128, NFC, 768], BF16); H_r = [Res("H%d" % c) for c in range(NFC)]
            S = [sb(es, "S%d" % i, [128, 512]) for i in range(2)]; S_r = [Res("S%d" % i) for i in range(2)]
            T = [sb(es, "T%d" % i, [128, D]) for i in range(2)]; T_r = [Res("T%d" % i) for i in range(2)]
            st = [sb(es, "st%d" % i, [128, 2, 6]) for i in range(2)]; st_r = [Res("st%d" % i) for i in range(2)]
            mv = [sb(es, "mv%d" % i, [128, 4]) for i in range(2)]; mv_r = [Res("mv%d" % i) for i in range(2)]
            tp = [ps(es, "tp%d" % i, [128, 4, 128]) for i in range(2)]; tp_r = [Res("tp%d" % i) for i in range(2)]
            pA = [ps(es, "pA%d" % i, [128, 512]) for i in range(2)]; pA_r = [Res("pA%d" % i) for i in range(2)]
            pB = [ps(es, "pB%d" % i, [128, 512]) for i in range(2)]; pB_r = [Res("pB%d" % i) for i in range(2)]
            pY = [ps(es, "pY%d" % i, [128, 512]) for i in range(2)]; pY_r = [Res("pY%d" % i) for i in range(2)]

            load_stage_consts(P, gidx, lng, lnb, pY, pY_r, T[0][0:5, :], T_r[0])
            wd_v = A[wd].rearrange("(c p) n -> p c n", p=128)
            wdq = [(0, 6), (6, 12), (12, 17), (17, 22)]
            wg_v = A[wg].rearrange("(kc p) n -> p kc n", p=128)
            wu_v = A[wu].rearrange("(kc p) n -> p kc n", p=128)
            groups = [(g * 2, 2) for g in range(11)]
            gcount = 0
            wd_loaded = False

            def load_group(gi_, slot):
                c0, n = groups[gi_]
                k.dma("pool", WG[slot][:, :, 0:n * 128], wg_v[:, :, c0 * 128:(c0 + n) * 128], writes=[WGU_r[slot]])
                k.dma("pool", WU[slot][:, :, 0:n * 128], wu_v[:, :, c0 * 128:(c0 + n) * 128], writes=[WGU_r[slot]])

            seqg = [(bi, gi_) for bi in range(len(blocks)) for gi_ in range(len(groups))]
            load_group(seqg[0][1], 0)
            load_group(seqg[1][1], 1)
            si = 0
            mm = 0
            ti = 0
            for bi, tiles in enumerate(blocks):
                ntok = 128 * len(tiles)
                if bi == 0:
                    for li, t in enumerate(tiles):
                        make_xmT(t, lambda kc, t=t: X[:, t, kc * 128:(kc + 1) * 128], XR[t], jsh, jsc, tp, tp_r,
                                 lambda kc, c0, c1, li=li: xmT[:, kc, li * 128 + c0:li * 128 + c1], xmT_r)
                subs = [(s0, min(512, ntok - s0)) for s0 in range(0, ntok, 512)]
                for gi_ in range(len(groups)):
                    slot = gcount % 2
                    c0, n = groups[gi_]
                    for cc in range(n):
                        c = c0 + cc
                        for (s0, sn) in subs:
                            a = pA[mm % 2]; ar = pA_r[mm % 2]; b_ = pB[mm % 2]; br = pB_r[mm % 2]; mm += 1
                            for kc in range(8):
                                k.op("pe", lambda e, kc=kc, cc=cc, a=a, s0=s0, sn=sn, slot=slot: e.matmul(
                                    a[:, 0:sn], lhsT=WG[slot][:, kc, cc * 128:(cc + 1) * 128], rhs=xmT[:, kc, s0:s0 + sn],
                                    start=(kc == 0), stop=(kc == 7)),
                                    reads=[WGU_r[slot], xmT_r], writes=[ar], signal=(kc == 7))
                            for kc in range(8):
                                k.op("pe", lambda e, kc=kc, cc=cc, b_=b_, s0=s0, sn=sn, slot=slot: e.matmul(
                                    b_[:, 0:sn], lhsT=WU[slot][:, kc, cc * 128:(cc + 1) * 128], rhs=xmT[:, kc, s0:s0 + sn],
                                    start=(kc == 0), stop=(kc == 7)),
                                    reads=[WGU_r[slot], xmT_r], writes=[br], signal=(kc == 7))
                            s_ = S[si % 2]; sr = S_r[si % 2]; si += 1
                            k.op("act", lambda e, a=a, s_=s_, sn=sn: e.activation(out=s_[:, 0:sn], in_=a[:, 0:sn], func=AF.Silu),
                                 reads=[ar], writes=[sr])
                            k.op("dve", lambda e, b_=b_, s_=s_, c=c, s0=s0, sn=sn: e.tensor_tensor(
                                out=H[:, c, s0:s0 + sn], in0=s_[:, 0:sn], in1=b_[:, 0:sn], op=ALU.mult),
                                reads=[sr, br], writes=[H_r[c]])
                    gcount += 1
                    if gcount + 1 < len(seqg):
                        load_group(seqg[gcount + 1][1], slot)
                    if not wd_loaded and gcount == 2:
                        for qi, (q0, q1) in enumerate(wdq):
                            k.dma("pool", WD[:, q0:q1, :], wd_v[:, q0:q1, :], writes=[WD_r[qi]])
                        wd_loaded = True
                for li, t in enumerate(tiles):
                    for half in range(2):
                        for c in range(NFC):
                            qi = [i for i, (q0, q1) in enumerate(wdq) if q0 <= c < q1][0]
                            k.op("pe", lambda e, c=c, half=half, li=li: e.matmul(
                                pY[half][:, :], lhsT=H[:, c, li * 128:(li + 1) * 128], rhs=WD[:, c, half * 512:(half + 1) * 512],
                                start=(c == 0), stop=(c == NFC - 1)),
                                reads=[H_r[c], WD_r[qi]], writes=[pY_r[half]], signal=(c == NFC - 1))
                    if bi + 1 < len(blocks) and li < len(blocks[bi + 1]):
                        tn = blocks[bi + 1][li]
                        make_xmT(tn, lambda kc, tn=tn: X[:, tn, kc * 128:(kc + 1) * 128], XR[tn], jsh, jsc, tp, tp_r,
                                 lambda kc, c0, c1, li=li: xmT[:, kc, li * 128 + c0:li * 128 + c1], xmT_r)
                    post_norm_ln(P, t, pY, pY_r, T[ti % 2], T_r[ti % 2], st[ti % 2], st_r[ti % 2], mv[ti % 2], mv_r[ti % 2])
                    if final:
                        k.dma("sp", A["y"][t * 128:(t + 1) * 128, :], X[:, t, :], reads=[XR[t]])
                    if spill:
                        k.dma("sp", XS[t * 128:(t + 1) * 128, :], X[:, t, :], reads=[XR[t]], writes=[XS_r[t]])
                    ti += 1
            k.barrier()

    def dsa_stage():
        with contextlib.ExitStack() as es:
            C_AQ, C_AK, C_IQ, C_IK, C_IK2, C_IW, C_AV = 0, 512, 1024, 1536, 1600, 1664, 1672
            WA = sb(es, "WA", [128, 8, 2184], BF16); WA_rs = [Res("WA%d" % i) for i in range(5)]
            wi_v = A["w_in"].rearrange("(kc p) n -> p kc n", p=128)
            for (dst, src, n, gi_) in ((C_AQ, 2048, 512, 0), (C_AK, 2560, 512, 1), (C_IQ, 3584, 512, 2), (C_IK, 4096, 64, 3),
                                       (C_IK2, 4096, 64, 3), (C_IW, 4160, 8, 3), (C_AV, 3072, 512, 4)):
                k.dma("pool", WA[:, :, dst:dst + n], wi_v[:, :, src:src + n], writes=[WA_rs[gi_]])
            ACOS = sb(es, "ACOS", [128, NT, 8]); ASIN = sb(es, "ASIN", [128, NT, 8]); AC_r = Res("AC")
            k.dma("sp", ACOS[:], A["acos"], writes=[AC_r])
            k.dma("sp", ASIN[:], A["asin"], writes=[AC_r])
            POW2 = sb(es, "POW2", [128, NIT + 2]); POW2_r = Res("POW2")
            k.dma("sp", POW2[:], A["pow2"], writes=[POW2_r])
            kT = sb(es, "kT", [128, 4, 2048], BF16); kT_r = [Res("kT%d" % j) for j in range(16)]
            Vaug = sb(es, "Vaug", [128, 16, 8, 65], BF16); V_r = [Res("V%d" % j) for j in range(16)]
            ikT2 = sb(es, "ikT2", [128, 5, 2048], BF16); ik_r = [[Res("ik%d_%d" % (b, j)) for j in range(16)] for b in range(5)]
            akTn = sb(es, "akTn", [128, 4, 128], BF16); akTn_r = Res("akTn")
            ikT2n = sb(es, "ikT2n", [128, 128], BF16); ikT2n_r = Res("ikT2n")
            Vn = sb(es, "Vn", [128, 8, 65], BF16); Vn_r = Res("Vn")
            iqTz = [sb(es, "iqTz%d" % b, [128, 4, 128], BF16) for b in range(4)]; iqTz_r = Res("iqTz")
            Mnew = sb(es, "Mnew", [128, 128], BF16); Mnew_r = Res("Mnew")
            MTn = sb(es, "MTn", [128, 128], BF16); MTn_r = Res("MTn")
            CI = sb(es, "CI", [128, 16, 64]); CI_r = Res("CI")
            CIb = sb(es, "CIb", [128, 16, 2, 64], BF16); CIb_r = Res("CIb")
            KS = [sb(es, "KS%d" % i, [128, 512]) for i in range(2)]; KS_r = [Res("KS%d" % i) for i in range(2)]
            VS = [sb(es, "VS%d" % i, [128, 512]) for i in range(2)]; VS_r = [Res("VS%d" % i) for i in range(2)]
            PTz = [[sb(es, "PTz%d_%d" % (p_, r), [128, 4, 128], BF16) for r in range(2)] for p_ in range(2)]
            PTz_r = [[Res("PTz%d_%d" % (p_, r)) for r in range(2)] for p_ in range(2)]
            XT = [sb(es, "XT%d" % i, [128, D]) for i in range(2)]; XT_r = [Res("XT%d" % i) for i in range(2)]
            XM = [sb(es, "XM%d" % i, [128, 8, 128], BF16) for i in range(2)]; XM_r = [Res("XM%d" % i) for i in range(2)]
            ZA = sb(es, "ZA", [128, 26, 64]); ZA_r = Res("ZA")
            ZAb = sb(es, "ZAb", [128, 26, 64], BF16); ZAb_r = Res("ZAb")
            RT = [sb(es, "RT%d" % i, [128, 26, 8]) for i in range(4)]; RT_r = [Res("RT%d" % i) for i in range(4)]
            ZV = sb(es, "ZV", [128, 512]); ZV_r = Res("ZV")
            IW = [sb(es, "IW%d" % i, [128, 8]) for i in range(2)]; IW_r = [Res("IW%d" % i) for i in range(2)]
            DIAG = sb(es, "DIAG", [128, 8, 128], BF16); DIAG_r = Res("DIAG")
            qT = [sb(es, "qT%d" % i, [128, 4, 128], BF16) for i in range(4)]; qT_r = [Res("qT%d" % i) for i in range(4)]
            iqT = [sb(es, "iqT%d" % i, [128, 4, 128], BF16) for i in range(2)]; iqT_r = [Res("iqT%d" % i) for i in range(2)]
            R = [sb(es, "R%d" % i, [128, 512], BF16) for i in range(4)]; R_r = [Res("R%d" % i) for i in range(4)]
            I = [sb(es, "I%d" % i, [128, 2080]) for i in range(2)]; I_r = [Res("I%d" % i) for i in range(2)]
            M = [sb(es, "M%d" % i, [128, 2080], BF16) for i in range(2)]; M_r = [Res("M%d" % i) for i in range(2)]
            MT = sb(es, "MT", [128, 16, 128], BF16); MT_r = Res("MT")
            BS = sb(es, "BS", [128, 8]); BS_r = Res("BS")
            HK = sb(es, "HK", [128, NIT + 2]); HK_r = Res("HK")
            E = [[sb(es, "E%d_%d" % (p_, r), [128, 4, 128], BF16) for r in range(2)] for p_ in range(2)]
            E_r = [[Res("E%d_%d" % (p_, r)) for r in range(2)] for p_ in range(2)]
            PT = [[sb(es, "PT%d_%d" % (p_, r), [128, 4, 128], BF16) for r in range(2)] for p_ in range(2)]
            PT_r = [[Res("PT%d_%d" % (p_, r)) for r in range(2)] for p_ in range(2)]
            RD = sb(es, "RD", [128, 2, 4]); RD_r = Res("RD")
            OATT = sb(es, "OATT", [128, 4, 2, 64], BF16); OATT_r = Res("OATT")
            OT = [sb(es, "OT%d" % i, [128, 4, 128], BF16) for i in range(2)]; OT_r = [Res("OT%d" % i) for i in range(2)]
            G0 = ps(es, "G0", [128, 512]); G0_r = Res("G0")
            G1 = ps(es, "G1", [128, 512]); G1_r = Res("G1")
            TB = ps(es, "TB", [128, 2, 4, 128], BF16); TB_r = [Res("TB0")] * 2
            TF = TB[:].rearrange("p a b c -> p (a b c)").bitcast(F32).rearrange("p (a b) -> p a b", b=128); TF_r = TB_r[0]
            SB2 = ps(es, "SB2", [128, 512]); SB2_r = Res("SB2")
            L = [ps(es, "L%d" % i, [128, 4, 128]) for i in range(2)]; L_r = [Res("L%d" % i) for i in range(2)]
            O = [ps(es, "O%d" % i, [128, 4, 65]) for i in range(2)]; O_r = [Res("O%d" % i) for i in range(2)]
            TFf = TB[:].rearrange("p a b c -> p (a b c)").bitcast(F32)
            G1v = G1[:, :].rearrange("p (a b) -> p a b", b=128)
            cnt = {"tb": 0, "r": 0, "e": 0, "ks": 0, "vs": 0}

            k.op("pool", lambda e: e.memset(Vaug[:, :, :, 64:65], 1.0), writes=V_r)
            k.op("pool", lambda e: e.memset(Vn[:, :, 64:65], 1.0), writes=[Vn_r])
            for b in range(4):
                k.op("pool", lambda e, b=b: e.memset(iqTz[b][:], 0.0), writes=[iqTz_r])

            def transposes_bf(src_fn, nblk, src_rs, dst_ap, dst_rs, evac):
                hb = cnt["tb"] % 2; cnt["tb"] += 1
                for q in range(nblk):
                    k.op("pe", lambda e, q=q, hb=hb: e.transpose(TB[:, hb, q, :], src_fn(q), identb[:]),
                         reads=list(src_rs) + [identb_r], writes=[TB_r[hb]], signal=(q == nblk - 1))
                src = TB[:, hb, 0:nblk, :] if nblk > 1 else TB[:, hb, 0, :]
                if evac == "act":
                    k.op("act", lambda e: e.copy(out=dst_ap, in_=src), reads=[TB_r[hb]], writes=list(dst_rs))
                else:
                    k.op("dve", lambda e: e.tensor_copy(out=dst_ap, in_=src), reads=[TB_r[hb]], writes=list(dst_rs))

            def S1x(t):
                smp = (t == 16)
                iq_ = iqT[t % 2]; iq_r = iqT_r[t % 2]; iw_ = IW[t % 2]; iw_r = IW_r[t % 2]
                xt = XT[t % 2]; xt_r = XT_r[t % 2]
                k.dma("sp", xt[:], XS[t * 128:(t + 1) * 128, :], reads=[XS_r[t]], writes=[xt_r])
                xm = XM[t % 2]; xm_r = XM_r[t % 2]
                make_xmT(t, lambda kc: xt[:, kc * 128:(kc + 1) * 128], xt_r, 3, 4, [TF, TF], [TF_r, TF_r],
                         lambda kc, c0, c1: xm[:, kc, c0:c1], xm_r)
                yield
                for gi, (c0, n) in enumerate(((C_AQ, 512), (C_AK, 512), (C_IQ, 512), (C_IK, 136), (C_AV, 512))):
                    g, gr = TFf, TF_r
                    for kc in range(8):
                        k.op("pe", lambda e, kc=kc, g=g, c0=c0, n=n: e.matmul(
                            g[:, 0:n], lhsT=xm[:, kc, :], rhs=WA[:, kc, c0:c0 + n], start=(kc == 0), stop=(kc == 7)),
                            reads=[xm_r, WA_rs[gi]], writes=[gr], signal=(kc == 7))
                    if gi < 3:
                        k.op("act", lambda e, g=g, gi=gi: e.copy(
                            out=ZA[:, 8 * gi:8 * gi + 8, :], in_=g[:, 0:512].rearrange("p (h d) -> p h d", d=64)),
                            reads=[gr], writes=[ZA_r])
                    elif gi == 3:
                        k.op("act", lambda e, g=g: e.copy(
                            out=ZA[:, 24:26, :], in_=g[:, 0:128].rearrange("p (h d) -> p h d", d=64)),
                            reads=[gr], writes=[ZA_r])
                        k.op("act", lambda e, g=g: e.mul(out=iw_[:], in_=g[:, 128:136], mul=8.0 ** -0.5),
                             reads=[gr], writes=[iw_r])
                    else:
                        k.op("act", lambda e, g=g: e.copy(out=ZV[:], in_=g[:, 0:512]), reads=[gr], writes=[ZV_r])
                    yield
                k.dma("sp", A["newv"][t * 128:(t + 1) * 128, :], ZV[:], reads=[ZV_r])
                if not smp:
                    k.op("pool", lambda e: e.tensor_copy(out=Vaug[:, t, :, 0:64], in_=ZV[:].rearrange("p (h d) -> p h d", d=64)),
                         reads=[ZV_r], writes=[V_r[t]])
                else:
                    k.op("pool", lambda e: e.tensor_copy(out=Vn[:, :, 0:64], in_=ZV[:].rearrange("p (h d) -> p h d", d=64)),
                         reads=[ZV_r], writes=[Vn_r])
                cosb = bc(ACOS[:, t, :], 1, [128, 26, 8]); sinb = bc(ASIN[:, t, :], 1, [128, 26, 8])
                x1 = ZA[:, :, 0:8]; x2 = ZA[:, :, 8:16]
                k.op("pool", lambda e: e.tensor_tensor(out=RT[0][:], in0=x1, in1=cosb, op=ALU.mult), reads=[ZA_r, AC_r], writes=[RT_r[0]])
                k.op("pool", lambda e: e.tensor_tensor(out=RT[1][:], in0=x2, in1=sinb, op=ALU.mult), reads=[ZA_r, AC_r], writes=[RT_r[1]])
                k.op("pool", lambda e: e.tensor_tensor(out=RT[2][:], in0=x2, in1=cosb, op=ALU.mult), reads=[ZA_r, AC_r], writes=[RT_r[2]])
                k.op("pool", lambda e: e.tensor_tensor(out=RT[3][:], in0=x1, in1=sinb, op=ALU.mult), reads=[ZA_r, AC_r], writes=[RT_r[3]])
                k.op("pool", lambda e: e.tensor_tensor(out=x1, in0=RT[0][:], in1=RT[1][:], op=ALU.subtract),
                     reads=[RT_r[0], RT_r[1]], writes=[ZA_r])
                k.op("pool", lambda e: e.tensor_tensor(out=x2, in0=RT[2][:], in1=RT[3][:], op=ALU.add),
                     reads=[RT_r[2], RT_r[3]], writes=[ZA_r])
                k.dma("sp", A["newk"][t * 128:(t + 1) * 128, :], ZA[:, 8:16, :].rearrange("p h d -> p (h d)"), reads=[ZA_r])
                k.dma("sp", A["newik"][t * 128:(t + 1) * 128, :], ZA[:, 24, :], reads=[ZA_r])
                yield
                k.op("pool", lambda e: e.tensor_copy(out=ZAb[:], in_=ZA[:]), reads=[ZA_r], writes=[ZAb_r])
                yield
                zb = ZAb[:].rearrange("p h d -> p (h d)")
                q_ = qT[t % 4]
                transposes_bf(lambda q: zb[:, q * 128:(q + 1) * 128], 4, [ZAb_r], q_[:], [qT_r[t % 4]], "act")
                if not smp:
                    transposes_bf(lambda q: zb[:, 512 + q * 128:512 + (q + 1) * 128], 4, [ZAb_r],
                                  kT[:, :, t * 128:(t + 1) * 128], [kT_r[t]], "act")
                else:
                    transposes_bf(lambda q: zb[:, 512 + q * 128:512 + (q + 1) * 128], 4, [ZAb_r], akTn[:], [akTn_r], "act")
                transposes_bf(lambda q: zb[:, 1024 + q * 128:1024 + (q + 1) * 128], 4, [ZAb_r], iq_[:], [iq_r], "act")
                if not smp:
                    transposes_bf(lambda q: zb[:, 1536:1664], 1, [ZAb_r], ikT2[:, 0, t * 128:(t + 1) * 128], [ik_r[0][t]], "act")
                else:
                    transposes_bf(lambda q: zb[:, 1536:1664], 1, [ZAb_r], ikT2n[:], [ikT2n_r], "act")
                    for b in range(4):
                        k.op("pool", lambda e, b=b: e.tensor_copy(out=iqTz[b][:, :, 32 * b:32 * b + 32], in_=iq_[:, :, 32 * b:32 * b + 32]),
                             reads=[iq_r], writes=[iqTz_r])
                    for b in range(4):
                        k.dma("sp", CI[:], A["cik"][b].rearrange("(kt p) d -> p kt d", p=128), writes=[CI_r])
                        k.op("pool", lambda e: e.tensor_copy(out=CIb[:, :, 0, :], in_=CI[:]), reads=[CI_r], writes=[CIb_r])
                        k.op("pool", lambda e: e.tensor_copy(out=CIb[:, :, 1, :], in_=CI[:]), reads=[CI_r], writes=[CIb_r])
                        for k4 in range(4):
                            transposes_bf(lambda q, k4=k4: CIb[:, k4 * 4 + q, :, :].rearrange("p a d -> p (a d)"), 4, [CIb_r],
                                          ikT2[:, b + 1, k4 * 512:(k4 + 1) * 512].rearrange("p (q n) -> p q n", n=128),
                                          [ik_r[b + 1][k4 * 4 + q] for q in range(4)], "act")
                yield

            def S1s(t):
                smp = (t == 16)
                I_ = I[t % 2]; Ir = I_r[t % 2]
                iq_ = iqT[t % 2]; iq_r = iqT_r[t % 2]; iw_ = IW[t % 2]; iw_r = IW_r[t % 2]
                k.op("pool", lambda e: e.tensor_tensor(out=DIAG[:], in0=bc(identb[:], 1, [128, 8, 128]),
                                                      in1=bc(iw_[:], 2, [128, 8, 128]), op=ALU.mult),
                     reads=[identb_r, iw_r], writes=[DIAG_r])
                nk = 2080 if smp else 128 * (t + 1)
                chunks = [(c0, min(512, 2048 - c0) if smp else min(512, nk - c0)) for c0 in range(0, 2048 if smp else nk, 512)]
                if smp:
                    chunks.append((2048, 32))
                pend = None
                for (c0, n) in chunks:
                    for hp in range(4):
                        items = []
                        for h in (2 * hp, 2 * hp + 1):
                            par = h % 2
                            g, gr = ((G0, G0_r), (SB2, SB2_r))[h % 2]
                            rows = slice(64 * par, 64 * par + 64)
                            if not smp:
                                k.op("pe", lambda e, g=g, h=h, c0=c0, n=n, rows=rows: e.matmul(
                                    g[:, 0:n], lhsT=iq_[rows, h // 2, :], rhs=ikT2[rows, 0, c0:c0 + n], start=True, stop=True),
                                    reads=[iq_r] + [ik_r[0][j] for j in range(c0 // 128, (c0 + n) // 128)], writes=[gr])
                            else:
                                for b in range(4):
                                    if c0 < 2048:
                                        rhs = ikT2[rows, b + 1, c0:c0 + n]
                                        rr = [ik_r[b + 1][j] for j in range(c0 // 128, (c0 + n) // 128)]
                                    else:
                                        rhs = ikT2n[rows, 32 * b:32 * b + 32]
                                        rr = [ikT2n_r]
                                    k.op("pe", lambda e, g=g, h=h, b=b, n=n, rows=rows, rhs=rhs: e.matmul(
                                        g[:, 0:n], lhsT=iqTz[b][rows, h // 2, :], rhs=rhs, start=(b == 0), stop=(b == 3)),
                                        reads=[iqTz_r] + rr, writes=[gr], signal=(b == 3))
                            items.append((h, g, gr))
                        nxt = []
                        for (h, g, gr) in items:
                            r_ = R[cnt["r"] % 4]; rr_ = R_r[cnt["r"] % 4]; cnt["r"] += 1
                            k.op("act", lambda e, g=g, r_=r_, n=n: e.activation(out=r_[:, 0:n], in_=g[:, 0:n], func=AF.Relu, scale=0.125),
                                 reads=[gr], writes=[rr_])
                            nxt.append((h, r_, rr_))
                        if pend is not None:
                            pend()
                        yield
                        def pend(nxt=nxt, n=n, c0=c0):
                            for (h, r_, rr_) in nxt:
                                k.op("pe", lambda e, h=h, r_=r_: e.matmul(
                                    G1[:, 0:n], lhsT=DIAG[:, h, :], rhs=r_[:, 0:n], start=(h == 0), stop=(h == 7)),
                                    reads=[DIAG_r, rr_], writes=[G1_r], signal=(h == 7))
                                if h == 7:
                                    k.op("dve", lambda e: e.tensor_copy(out=I_[:, c0:c0 + n], in_=G1[:, 0:n]), reads=[G1_r], writes=[Ir])
                pend()
                yield

            def S1b(t):
                smp = (t == 16)
                nk = 2080 if smp else 128 * (t + 1)
                m_ = M[t % 2]; m_r = M_r[t % 2]
                I_ = I[t % 2]; Ir = I_r[t % 2]
                if t >= 2:
                    k.op("dve", lambda e: e.tensor_reduce(out=BS[:, 0:1], in_=I_[:, 0:nk], axis=mybir.AxisListType.X, op=ALU.max),
                         reads=[Ir], writes=[BS_r])
                    k.op("dve", lambda e: e.tensor_reduce(out=BS[:, 1:2], in_=I_[:, 0:nk], axis=mybir.AxisListType.X, op=ALU.min),
                         reads=[Ir], writes=[BS_r])
                    if not smp:
                        k.op("pool", lambda e: e.memset(I_[0:64, nk - 64:nk], NEG), reads=[BS_r], writes=[Ir])
                    k.op("dve", lambda e: e.tensor_tensor(out=BS[:, 2:3], in0=BS[:, 0:1], in1=BS[:, 1:2], op=ALU.subtract),
                         reads=[BS_r], writes=[BS_r])
                    k.op("dve", lambda e: e.tensor_scalar(out=HK[:], in0=POW2[:], scalar1=BS[:, 2:3], scalar2=None, op0=ALU.mult),
                         reads=[POW2_r, BS_r], writes=[HK_r])
                    k.op("dve", lambda e: e.tensor_tensor(out=BS[:, 3:4], in0=BS[:, 1:2], in1=HK[:, 0:1], op=ALU.add),
                         reads=[BS_r, HK_r], writes=[BS_r])
                    yield
                    for kk in range(NIT):
                        k.op("dve", lambda e: e.tensor_scalar(out=m_[:, 0:nk], in0=I_[:, 0:nk], scalar1=BS[:, 3:4], scalar2=None,
                                                              op0=ALU.is_gt, op1=ALU.add, accum_out=BS[:, 4:5]),
                             reads=[Ir, BS_r], writes=[m_r, BS_r])
                        k.op("dve", lambda e, kk=kk: e.tensor_scalar(out=BS[:, 5:6], in0=BS[:, 4:5], scalar1=TOPK - 0.5,
                                                                     scalar2=HK[:, kk:kk + 1], op0=ALU.is_gt, op1=ALU.mult),
                             reads=[BS_r, HK_r], writes=[BS_r])
                        sub = kk + 1 if kk < NIT - 1 else kk
                        dst = BS[:, 3:4] if kk < NIT - 1 else BS[:, 6:7]
                        k.op("dve", lambda e, sub=sub, dst=dst: e.scalar_tensor_tensor(
                            out=dst, in0=BS[:, 5:6], scalar=HK[:, sub:sub + 1], in1=BS[:, 3:4], op0=ALU.subtract, op1=ALU.add),
                            reads=[BS_r, HK_r], writes=[BS_r])
                        yield
                    k.op("dve", lambda e: e.tensor_scalar(out=m_[:, 0:nk], in0=I_[:, 0:nk], scalar1=BS[:, 6:7], scalar2=None,
                                                          op0=ALU.is_gt), reads=[Ir, BS_r], writes=[m_r])
                else:
                    k.op("pool", lambda e: e.memset(I_[0:64, nk - 64:nk], NEG), writes=[Ir])
                    k.op("dve", lambda e: e.tensor_scalar(out=m_[:, 0:nk], in0=I_[:, 0:nk], scalar1=-1.0e29, scalar2=None,
                                                          op0=ALU.is_gt), reads=[Ir], writes=[m_r])
                if STOP_AFTER == "dsa" and t in (3, 16):
                    sl = 0 if t == 3 else 1
                    k.dma("sp", A["dbgI"][sl, :, 0:nk], I_[:, 0:nk], reads=[Ir])
                    k.dma("pool", A["dbgM"][sl, :, 0:nk], m_[:, 0:nk], reads=[m_r])
                yield

            def attend_step(b, m_src, kt_fn, kt_rs, v_fn, v_rs, q_, q_r, first, last, par=None, lb=None, lb_r=None):
                cs = slice(0, 128) if b is None else slice(32 * b, 32 * b + 32)
                ncol = 128 if b is None else 32
                st_ = {}

                def front():
                    for i in range(4):
                        for par_ in range(2):
                            rows = slice(64 * par_, 64 * par_ + 64)
                            k.op("pe", lambda e, i=i, par_=par_, rows=rows: e.matmul(
                                L[par_][:, i, 0:ncol], lhsT=kt_fn(rows, i), rhs=q_[rows, i, cs], start=True, stop=True),
                                reads=list(kt_rs) + [q_r], writes=[L_r[par_]], signal=(i == 3))
                    ps_ = []
                    for par_ in range(2):
                        ei = cnt["e%d" % par_] % 2; cnt["e%d" % par_] += 1
                        e_ = E[par_][ei]; er = E_r[par_][ei]
                        k.op("act", lambda e, par_=par_, e_=e_: e.activation(out=e_[:, :, 0:ncol], in_=L[par_][:, :, 0:ncol],
                                                                            func=AF.Exp, scale=0.125),
                             reads=[L_r[par_]], writes=[er])
                        if b is None:
                            p_ = PT[par_][ei]; pr = PT_r[par_][ei]
                        else:
                            p_ = PTz[par_][ei]; pr = PTz_r[par_][ei]
                        k.op("pool", lambda e, e_=e_, p_=p_: e.tensor_tensor(
                            out=p_[:, :, cs], in0=e_[:, :, 0:ncol], in1=bc(m_src, 1, [128, 4, ncol]), op=ALU.mult),
                            reads=[er, MT_r, MTn_r], writes=[pr])
                        ps_.append((p_, pr))
                    st_["p"] = ps_

                def pv():
                    for par_ in range(2):
                        p_, pr = st_["p"][par_]
                        for i in range(4):
                            h = 2 * i + par_
                            k.op("pe", lambda e, i=i, h=h, par_=par_, p_=p_: e.matmul(
                                O[par_][:, i, :], lhsT=p_[:, i, :], rhs=v_fn(h), start=(first and i == 0), stop=last,
                                skip_group_check=True),
                                reads=[pr] + list(v_rs), writes=[O_r[par_]], signal=(i == 3))
                return [front, pv, None]

            def run_steps(steps):
                prev = None
                for st_ in steps:
                    st_[0]()
                    if prev is not None:
                        prev[1]()
                        if prev[2] is not None:
                            prev[2]()
                    prev = st_
                    yield
                if prev is not None:
                    prev[1]()
                    if prev[2] is not None:
                        prev[2]()
                yield

            def finalize(t):
                for par in range(2):
                    k.op("dve", lambda e, par=par: e.reciprocal(out=RD[:, par, :], in_=O[par][:, :, 64]), reads=[O_r[par]], writes=[RD_r])
                    k.op("dve", lambda e, par=par: e.tensor_tensor(
                        out=OATT[:, :, par, :], in0=O[par][:, :, 0:64], in1=bc(RD[:, par, :], 2, [128, 4, 64]), op=ALU.mult),
                        reads=[O_r[par], RD_r], writes=[OATT_r])
                of = OATT[:].rearrange("p a b d -> p (a b d)")
                o_ = OT[t % 2]; o_r = OT_r[t % 2]
                transposes_bf(lambda q: of[:, q * 128:(q + 1) * 128], 4, [OATT_r], o_[:], [o_r], "act")
                k.dma("sp", OATS[t], o_[:].rearrange("p a b -> p (a b)"), reads=[o_r], writes=[OATS_r[t]])
                if STOP_AFTER == "dsa":
                    k.dma("pool", A["dbg"][t * 128:(t + 1) * 128, :], of, reads=[OATT_r])

            def S2(t):
                m_ = M[t % 2]; m_r = M_r[t % 2]
                q_ = qT[t % 4]; q_r = qT_r[t % 4]
                if t < 16:
                    for j0 in range(0, t + 1, 4):
                        nb = min(4, t + 1 - j0)
                        transposes_bf(lambda q, j0=j0: m_[:, (j0 + q) * 128:(j0 + q + 1) * 128], nb, [m_r],
                                      MT[:, j0:j0 + nb, :] if nb > 1 else MT[:, j0, :], [MT_r], "act")
                    yield
                    steps = []
                    for j in range(t + 1):
                        steps.append(attend_step(None, MT[:, j, :], lambda rows, i, j=j: kT[rows, i, j * 128:(j + 1) * 128], [kT_r[j]],
                                                 lambda h, j=j: Vaug[:, j, h, :], [V_r[j]], q_, q_r, (j == 0), (j == t)))
                    yield from run_steps(steps)
                    finalize(t)
                    yield
                    return
                for j0 in range(0, 16, 4):
                    transposes_bf(lambda q, j0=j0: m_[:, (j0 + q) * 128:(j0 + q + 1) * 128], 4, [m_r], MT[:, j0:j0 + 4, :], [MT_r], "act")
                for b in range(4):
                    k.op("pool", lambda e, b=b: e.tensor_scalar(out=Mnew[:, 32 * b:32 * b + 32], in0=m_[:, 2048:2080],
                                                                scalar1=ROWM[:, b:b + 1], scalar2=None, op0=ALU.mult),
                         reads=[m_r, ROWM_r], writes=[Mnew_r])
                transposes_bf(lambda q: Mnew[:], 1, [Mnew_r], MTn[:], [MTn_r], "act")
                def prep(b, kt):
                    ks = KS[cnt["ks"] % 2]; ksr = KS_r[cnt["ks"] % 2]; cnt["ks"] += 1
                    k.dma("sp", ks[:], A["ck"][b, kt * 128:(kt + 1) * 128, :], writes=[ksr])
                    for q in range(4):
                        k.op("pe", lambda e, q=q, ks=ks: e.transpose(TF[:, q, :], ks[:, q * 128:(q + 1) * 128], identf[:]),
                             reads=[ksr, identf_r], writes=[TF_r], signal=(q == 3))
                    k.op("dve", lambda e: e.tensor_copy(out=kT[:, :, kt * 128:(kt + 1) * 128], in_=TF[:]), reads=[TF_r], writes=[kT_r[kt]])
                    vs = VS[cnt["vs"] % 2]; vsr = VS_r[cnt["vs"] % 2]; cnt["vs"] += 1
                    k.dma("sp", vs[:], A["cv"][b, kt * 128:(kt + 1) * 128, :], writes=[vsr])
                    k.op("act", lambda e: e.copy(out=Vaug[:, kt, :, 0:64], in_=vs[:].rearrange("p (h d) -> p h d", d=64)),
                         reads=[vsr], writes=[V_r[kt]])

                for kt in range(16):
                    prep(0, kt)
                    yield
                for b in range(4):
                    for par in range(2):
                        for r in range(2):
                            k.op("pool", lambda e, par=par, r=r: e.memset(PTz[par][r][:], 0.0), writes=[PTz_r[par][r]])
                    steps = []
                    for j in range(17):
                        if j < 16:
                            stp = attend_step(b, MT[:, j, 32 * b:32 * b + 32],
                                              lambda rows, i, j=j: kT[rows, i, j * 128:(j + 1) * 128], [kT_r[j]],
                                              lambda h, j=j: Vaug[:, j, h, :], [V_r[j]], q_, q_r, (b == 0 and j == 0), False)
                            if b < 3:
                                stp[2] = (lambda b=b, j=j: prep(b + 1, j))
                        else:
                            stp = attend_step(b, MTn[:, 32 * b:32 * b + 32], lambda rows, i: akTn[rows, i, :], [akTn_r],
                                              lambda h: Vn[:, h, :], [Vn_r], q_, q_r, False, (b == 3))
                        steps.append(stp)
                    yield from run_steps(steps)
                finalize(t)
                yield

            def interleave(g1, n1, g2, n2):
                a1 = a2 = 0
                d1 = d2 = False
                while not (d1 and d2):
                    take1 = (not d1) and (d2 or a1 * n2 <= a2 * n1)
                    if take1:
                        try:
                            next(g1); a1 += 1
                        except StopIteration:
                            d1 = True
                    else:
                        try:
                            next(g2); a2 += 1
                        except StopIteration:
                            d2 = True

            cnt["e0"] = 0; cnt["e1"] = 0

            def interleave_n(gens):
                gens = [[g, n, 0, False] for (g, n) in gens]
                while any(not x[3] for x in gens):
                    live = [x for x in gens if not x[3]]
                    x = min(live, key=lambda x: x[2] / float(x[1]))
                    try:
                        next(x[0]); x[2] += 1
                    except StopIteration:
                        x[3] = True

            def n_s1s(t):
                return 2 + 4 * (5 if t == 16 else (t + 4) // 4)

            for r in range(NT + 3):
                gens = []
                if r < NT:
                    gens.append((S1x(r), 14))
                if 0 <= r - 1 < NT:
                    gens.append((S1s(r - 1), n_s1s(r - 1)))
                if 0 <= r - 2 < NT:
                    gens.append((S1b(r - 2), NIT + 3))
                if 0 <= r - 3 < NT:
                    t2 = r - 3
                    gens.append((S2(t2), (t2 + 5) if t2 < 16 else 90))
                interleave_n(gens)
            k.barrier()

    def ro_stage(P):
        X, XR = P["X"], P["XR"]
        with contextlib.ExitStack() as es:
            stage_ln_tiles(es, P)
            WR = sb(es, "WR", [128, 8, 2048], BF16); WR_r = [Res("WR%d" % i) for i in range(4)]
            WO = sb(es, "WO", [128, 8, D], BF16); WO_r = Res("WO")
            wi_v = A["w_in"].rearrange("(kc p) n -> p kc n", p=128)
            for i in range(4):
                k.dma("pool", WR[:, :, i * 512:(i + 1) * 512], wi_v[:, :, i * 512:(i + 1) * 512], writes=[WR_r[i]])
            k.dma("pool", WO[:], A["w_out"].rearrange("(kc p) n -> p kc n", p=128), writes=[WO_r])
            RCS = [sb(es, "RCS%d" % i, [128, 2, 64]) for i in range(2)]; RCS_r = [Res("RCS%d" % i) for i in range(2)]
            DEC = sb(es, "DEC", [128, 2, 8]); GC = sb(es, "GC", [128, 2, 4]); CMASK = sb(es, "CMASK", [128, 2, 128]); CN_r = Res("CN")
            k.dma("sp", DEC[:], A["dec"], writes=[CN_r])
            k.dma("sp", GC[:], A["gc"], writes=[CN_r])
            k.dma("sp", CMASK[:], A["cmask"], writes=[CN_r])
            S32 = sb(es, "S32", [128, 4, 128]); S32_r = Res("S32")
            Sbf = sb(es, "Sbf", [128, 4, 128], BF16); Sbf_r = Res("Sbf")
            S0 = [sb(es, "S0_%d" % b, [128, 4, 128]) for b in range(4)]; S0_r = [Res("S0_%d" % b) for b in range(4)]
            S0b = [sb(es, "S0b_%d" % b, [128, 4, 128], BF16) for b in range(4)]; S0b_r = [Res("S0b_%d" % b) for b in range(4)]
            QTz = [sb(es, "QTz%d" % b, [128, 4, 128], BF16) for b in range(4)]; QTz_r = Res("QTz")
            KMb = sb(es, "KMb", [128, 4, 128], BF16); KMb_r = Res("KMb")
            XM = [sb(es, "XMr%d" % i, [128, 8, 128], BF16) for i in range(2)]; XM_r = [Res("XMr%d" % i) for i in range(2)]
            ZR = sb(es, "ZR", [128, 8, 128]); ZR_r = Res("ZR")
            ZRb = [sb(es, "ZRb%d" % i, [128, 8, 128], BF16) for i in range(2)]; ZRb_r = [Res("ZRb%d" % i) for i in range(2)]
            RT = [sb(es, "RTr%d" % i, [128, 8, 64]) for i in range(4)]; RT_r = [Res("RTr%d" % i) for i in range(4)]
            Vb = [sb(es, "Vb%d" % i, [128, 4, 128], BF16) for i in range(2)]; Vb_r = [Res("Vb%d" % i) for i in range(2)]
            SG = [sb(es, "SG%d" % i, [128, 512]) for i in range(2)]; SG_r = [Res("SG%d" % i) for i in range(2)]
            QKT = [sb(es, "QKT%d" % i, [128, 8, 128], BF16) for i in range(2)]; QKT_r = [Res("QKT%d" % i) for i in range(2)]
            ST = sb(es, "ST", [128, 4, 128], BF16); ST_r = Res("ST")
            hst = sb(es, "hst", [128, 4, 6]); hmv = sb(es, "hmv", [128, 4, 2]); hrs = sb(es, "hrs", [128, 4]); hs_r = Res("hs")
            ON = sb(es, "ON", [128, 4, 128]); ON_r = Res("ON")
            ORb = [sb(es, "ORb%d" % i, [128, 512], BF16) for i in range(2)]; ORb_r = [Res("ORb%d" % i) for i in range(2)]
            ORT = [sb(es, "ORT%d" % i, [128, 4, 128], BF16) for i in range(2)]; ORT_r = [Res("ORT%d" % i) for i in range(2)]
            OTl = [sb(es, "OTl%d" % i, [128, 4, 128], BF16) for i in range(2)]; OTl_r = [Res("OTl%d" % i) for i in range(2)]
            T = [sb(es, "Tr%d" % i, [128, D]) for i in range(2)]; T_r = [Res("Tr%d" % i) for i in range(2)]
            st = [sb(es, "str%d" % i, [128, 2, 6]) for i in range(2)]; st_r = [Res("str%d" % i) for i in range(2)]
            mv = [sb(es, "mvr%d" % i, [128, 4]) for i in range(2)]; mv_r = [Res("mvr%d" % i) for i in range(2)]
            G = [ps(es, "Gr%d" % i, [128, 512]) for i in range(2)]; G_r = [Res("Gr%d" % i) for i in range(2)]
            TF = ps(es, "TFr", [128, 4, 128]); TF_r = Res("TFr")
            TBa = ps(es, "TBr0", [128, 1, 4, 128], BF16); TBb = ps(es, "TBr1", [128, 1, 4, 128], BF16)
            TBs = [TBa, TBb]; TB_r = [Res("TBr0"), Res("TBr1")]
            SC = ps(es, "SC", [128, 4, 128]); SC_r = Res("SC")
            OP = ps(es, "OP", [128, 4, 128]); OP_r = Res("OP")
            UP = ps(es, "UP", [128, 4, 128]); UP_r = Res("UP")
            cnt = {"tb": 0, "g": 0}

            load_stage_consts(P, 1, "ln2g", "ln2b", G, G_r, T[0][0:5, :], T_r[0])
            for b in range(4):
                k.dma("sp", S0[b][:], A["sret"][b].rearrange("h k v -> k h v"), writes=[S0_r[b]])
                k.op("pool", lambda e, b=b: e.tensor_copy(out=S0b[b][:], in_=S0[b][:]), reads=[S0_r[b]], writes=[S0b_r[b]])
                k.op("pool", lambda e, b=b: e.memset(QTz[b][:], 0.0), writes=[QTz_r])

            def transposes_bf(src_fn, nblk, src_rs, dst_ap, dst_rs, evac):
                hb = cnt["tb"] % 2; cnt["tb"] += 1
                for q in range(nblk):
                    k.op("pe", lambda e, q=q, hb=hb: e.transpose(TBs[hb][:, 0, q, :], src_fn(q), identb[:]),
                         reads=list(src_rs) + [identb_r], writes=[TB_r[hb]], signal=(q == nblk - 1))
                if evac == "act":
                    k.op("act", lambda e: e.copy(out=dst_ap, in_=TBs[hb][:, 0, 0:nblk, :]), reads=[TB_r[hb]], writes=list(dst_rs))
                else:
                    k.op("dve", lambda e: e.tensor_copy(out=dst_ap, in_=TBs[hb][:, 0, 0:nblk, :]), reads=[TB_r[hb]], writes=list(dst_rs))

            def RA1(t):
                xm = XM[t % 2]; xm_r = XM_r[t % 2]
                rcs = RCS[t % 2]; rcs_r = RCS_r[t % 2]
                k.dma("sp", rcs[:, 0, :], A["rcos"][:, t, :], writes=[rcs_r])
                k.dma("sp", rcs[:, 1, :], A["rsin"][:, t, :], writes=[rcs_r])
                make_xmT(t, lambda kc, t=t: X[:, t, kc * 128:(kc + 1) * 128], XR[t], 3, 4, [TF, TF], [TF_r, TF_r],
                         lambda kc, c0, c1: xm[:, kc, c0:c1], xm_r)

            def RA(t):
                smp = (t == 16); kind = 1 if smp else 0
                xm = XM[t % 2]; xm_r = XM_r[t % 2]
                rcs = RCS[t % 2]; rcs_r = RCS_r[t % 2]
                zrb = ZRb[t % 2]; zrb_r = ZRb_r[t % 2]; vb = Vb[t % 2]; vb_r = Vb_r[t % 2]
                sg = SG[t % 2]; sg_r = SG_r[t % 2]; qkt = QKT[t % 2]; qkt_r = QKT_r[t % 2]
                for gi in range(4):
                    if gi % 2 == 0:
                        g = TF[:].rearrange("p a b -> p (a b)"); gr = TF_r
                    else:
                        g = UP[:].rearrange("p a b -> p (a b)"); gr = UP_r
                    for kc in range(8):
                        k.op("pe", lambda e, kc=kc, g=g, gi=gi: e.matmul(
                            g[:, :], lhsT=xm[:, kc, :], rhs=WR[:, kc, gi * 512:(gi + 1) * 512], start=(kc == 0), stop=(kc == 7)),
                            reads=[xm_r, WR_r[gi]], writes=[gr], signal=(kc == 7))
                    gv = g[:, :].rearrange("p (h d) -> p h d", d=128)
                    if gi < 2:
                        k.op("act", lambda e, gv=gv, gi=gi: e.copy(out=ZR[:, 4 * gi:4 * gi + 4, :], in_=gv), reads=[gr], writes=[ZR_r])
                    elif gi == 2:
                        k.op("act", lambda e, gv=gv: e.copy(out=vb[:], in_=gv), reads=[gr], writes=[vb_r])
                    else:
                        k.op("act", lambda e, g=g: e.activation(out=sg[:], in_=g[:, :], func=AF.Silu), reads=[gr], writes=[sg_r])
                    yield
                cosb = bc(rcs[:, 0, :], 1, [128, 8, 64]); sinb = bc(rcs[:, 1, :], 1, [128, 8, 64])
                x1 = ZR[:, :, 0:64]; x2 = ZR[:, :, 64:128]
                k.op("dve", lambda e: e.tensor_tensor(out=RT[0][:], in0=x1, in1=cosb, op=ALU.mult), reads=[ZR_r, rcs_r], writes=[RT_r[0]])
                k.op("dve", lambda e: e.tensor_tensor(out=RT[1][:], in0=x2, in1=sinb, op=ALU.mult), reads=[ZR_r, rcs_r], writes=[RT_r[1]])
                k.op("pool", lambda e: e.tensor_tensor(out=RT[2][:], in0=x2, in1=cosb, op=ALU.mult), reads=[ZR_r, rcs_r], writes=[RT_r[2]])
                k.op("pool", lambda e: e.tensor_tensor(out=RT[3][:], in0=x1, in1=sinb, op=ALU.mult), reads=[ZR_r, rcs_r], writes=[RT_r[3]])
                k.op("dve", lambda e: e.tensor_tensor(out=x1, in0=RT[0][:], in1=RT[1][:], op=ALU.subtract),
                     reads=[RT_r[0], RT_r[1]], writes=[ZR_r])
                k.op("pool", lambda e: e.tensor_tensor(out=x2, in0=RT[2][:], in1=RT[3][:], op=ALU.add),
                     reads=[RT_r[2], RT_r[3]], writes=[ZR_r])
                k.op("dve", lambda e: e.tensor_tensor(out=zrb[:], in0=ZR[:], in1=bc(DEC[:, kind, :], 2, [128, 8, 128]), op=ALU.mult),
                     reads=[ZR_r, CN_r], writes=[zrb_r])
                yield
                transposes_bf(lambda q: zrb[:, q, :], 4, [zrb_r], qkt[:, 0:4, :], [qkt_r], "act")
                yield
                transposes_bf(lambda q: zrb[:, 4 + q, :], 4, [zrb_r], qkt[:, 4:8, :], [qkt_r], "act")
                yield

            def RB(t):
                smp = (t == 16); kind = 1 if smp else 0
                zrb = ZRb[t % 2]; zrb_r = ZRb_r[t % 2]; vb = Vb[t % 2]; vb_r = Vb_r[t % 2]
                sg = SG[t % 2]; sg_r = SG_r[t % 2]; qkt = QKT[t % 2]; qkt_r = QKT_r[t % 2]
                for h in range(4):
                    k.op("pe", lambda e, h=h: e.matmul(SC[:, h, :], lhsT=qkt[:, 4 + h, :], rhs=qkt[:, h, :], start=True, stop=True),
                         reads=[qkt_r], writes=[SC_r], signal=(h == 3))
                k.op("dve", lambda e: e.tensor_tensor(out=ST[:], in0=SC[:], in1=bc(CMASK[:, kind, :], 1, [128, 4, 128]), op=ALU.mult),
                     reads=[SC_r, CN_r], writes=[ST_r])
                if smp:
                    for b in range(4):
                        k.op("pool", lambda e, b=b: e.tensor_copy(out=QTz[b][:, :, 32 * b:32 * b + 32], in_=qkt[:, 0:4, 32 * b:32 * b + 32]),
                             reads=[qkt_r], writes=[QTz_r])
                yield
                for h in range(4):
                    cross = smp or t > 0
                    k.op("pe", lambda e, h=h, cross=cross: e.matmul(OP[:, h, :], lhsT=ST[:, h, :], rhs=vb[:, h, :], start=True, stop=(not cross)),
                         reads=[ST_r, vb_r], writes=[OP_r], signal=(not cross and h == 3))
                    if smp:
                        for b in range(4):
                            k.op("pe", lambda e, h=h, b=b: e.matmul(OP[:, h, :], lhsT=QTz[b][:, h, :], rhs=S0b[b][:, h, :],
                                                                    start=False, stop=(b == 3)),
                                 reads=[QTz_r, S0b_r[b]], writes=[OP_r], signal=(b == 3 and h == 3))
                    elif t > 0:
                        k.op("pe", lambda e, h=h: e.matmul(OP[:, h, :], lhsT=qkt[:, h, :], rhs=Sbf[:, h, :], start=False, stop=True),
                             reads=[qkt_r, Sbf_r], writes=[OP_r], signal=(h == 3))
                yield
                gcb = bc(GC[:, kind, :], 2, [128, 4, 128])
                if not smp:
                    for h in range(4):
                        k.op("pe", lambda e, h=h: e.matmul(UP[:, h, :], lhsT=zrb[:, 4 + h, :], rhs=vb[:, h, :], start=True, stop=True),
                             reads=[zrb_r, vb_r], writes=[UP_r], signal=(h == 3))
                    if t == 0:
                        k.op("dve", lambda e: e.tensor_tensor(out=S32[:], in0=UP[:], in1=gcb, op=ALU.mult), reads=[UP_r, CN_r], writes=[S32_r])
                    else:
                        k.op("dve", lambda e: e.tensor_tensor(out=S32[:], in0=S32[:], in1=UP[:], op=ALU.add), reads=[UP_r, S32_r], writes=[S32_r])
                        k.op("dve", lambda e: e.tensor_tensor(out=S32[:], in0=S32[:], in1=gcb, op=ALU.mult), reads=[S32_r, CN_r], writes=[S32_r])
                    if t < 15:
                        k.op("pool", lambda e: e.tensor_copy(out=Sbf[:], in_=S32[:]), reads=[S32_r], writes=[Sbf_r])
                    else:
                        k.dma("sp", A["stp"].rearrange("h k v -> k h v"), S32[:], reads=[S32_r])
                else:
                    for b in range(4):
                        k.op("pool", lambda e, b=b: e.tensor_scalar(out=KMb[:], in0=zrb[:, 4:8, :], scalar1=ROWM[:, b:b + 1], scalar2=None,
                                                                    op0=ALU.mult), reads=[zrb_r, ROWM_r], writes=[KMb_r])
                        for h in range(4):
                            k.op("pe", lambda e, h=h: e.matmul(UP[:, h, :], lhsT=KMb[:, h, :], rhs=vb[:, h, :], start=True, stop=True),
                                 reads=[KMb_r, vb_r], writes=[UP_r], signal=(h == 3))
                        k.op("dve", lambda e, b=b: e.tensor_tensor(out=S0[b][:], in0=S0[b][:], in1=UP[:], op=ALU.add),
                             reads=[UP_r, S0_r[b], S0b_r[b]], writes=[S0_r[b]])
                        k.op("dve", lambda e, b=b: e.tensor_tensor(out=S0[b][:], in0=S0[b][:], in1=gcb, op=ALU.mult),
                             reads=[S0_r[b], CN_r], writes=[S0_r[b]])
                        k.dma("sp", A["sts"][b].rearrange("h k v -> k h v"), S0[b][:], reads=[S0_r[b]])
                yield
                for h in range(4):
                    k.op("dve", lambda e, h=h: e.bn_stats(out=hst[:, h, :], in_=OP[:, h, :]), reads=[OP_r], writes=[hs_r])
                for h in range(4):
                    k.op("dve", lambda e, h=h: e.bn_aggr(out=hmv[:, h, :], in_=hst[:, h, :]), reads=[hs_r], writes=[hs_r])
                k.op("dve", lambda e: e.tensor_scalar_add(out=hrs[:], in0=hmv[:, :, 1], scalar1=LN_EPS), reads=[hs_r], writes=[hs_r])
                k.op("pool", lambda e: e.tensor_tensor(out=hrs[:], in0=hrs[:], in1=NHALF[:, 0:4], op=ALU.pow),
                     reads=[hs_r, NHALF_r], writes=[hs_r])
                k.op("dve", lambda e: e.tensor_tensor(out=ON[:], in0=OP[:], in1=bc(hmv[:, :, 0], 2, [128, 4, 128]), op=ALU.subtract),
                     reads=[OP_r, hs_r], writes=[ON_r])
                yield
                k.op("pool", lambda e: e.tensor_tensor(out=ON[:], in0=ON[:], in1=bc(hrs[:], 2, [128, 4, 128]), op=ALU.mult),
                     reads=[ON_r, hs_r], writes=[ON_r])
                orb = ORb[t % 2]; orb_r = ORb_r[t % 2]
                k.op("pool", lambda e: e.tensor_tensor(out=orb[:], in0=ON[:].rearrange("p h d -> p (h d)"), in1=sg[:], op=ALU.mult),
                     reads=[ON_r, sg_r], writes=[orb_r])
                yield

            def RC(t):
                ort = ORT[t % 2]; ort_r = ORT_r[t % 2]
                ot = OTl[t % 2]; ot_r = OTl_r[t % 2]
                k.dma("sp", ot[:].rearrange("p a b -> p (a b)"), OATS[t], reads=[OATS_r[t]], writes=[ot_r])
                orb = ORb[t % 2]; orb_r = ORb_r[t % 2]
                transposes_bf(lambda q: orb[:, q * 128:(q + 1) * 128], 4, [orb_r], ort[:], [ort_r], "act")
                yield
                ys = []
                for half in range(2):
                    g = G[half]; gr = G_r[half]
                    ys.append((g, gr))
                    for c in range(8):
                        lhs = ort[:, c, :] if c < 4 else ot[:, c - 4, :]
                        k.op("pe", lambda e, c=c, g=g, half=half, lhs=lhs: e.matmul(
                            g[:, :], lhsT=lhs, rhs=WO[:, c, half * 512:(half + 1) * 512], start=(c == 0), stop=(c == 7)),
                            reads=[ort_r, ot_r, WO_r], writes=[gr], signal=(c == 7))
                    yield
                post_norm_ln(P, t, [ys[0][0], ys[1][0]], [ys[0][1], ys[1][1]], T[t % 2], T_r[t % 2], st[t % 2], st_r[t % 2],
                             mv[t % 2], mv_r[t % 2])
                if STOP_AFTER == "ro":
                    k.dma("sp", A["y"][t * 128:(t + 1) * 128, :], X[:, t, :], reads=[XR[t]])
                yield

            def interleave_n(gens):
                gens = [[g, n, 0, False] for (g, n) in gens]
                while any(not x[3] for x in gens):
                    live = [x for x in gens if not x[3]]
                    x = min(live, key=lambda x: x[2] / float(x[1]))
                    try:
                        next(x[0]); x[2] += 1
                    except StopIteration:
                        x[3] = True

            for t in range(min(3, NT)):
                P["reload"](t)
            RA1(0)
            for r in range(NT + 2):
                gens = []
                if r + 3 < NT:
                    P["reload"](r + 3)
                if r + 1 < NT:
                    RA1(r + 1)
                if r < NT:
                    gens.append((RA(r), 8))
                if 0 <= r - 1 < NT:
                    gens.append((RB(r - 1), 6))
                if 0 <= r - 2 < NT:
                    gens.append((RC(r - 2), 5))
                interleave_n(gens)
            k.barrier()

    P = {}
    with contextlib.ExitStack() as esA:
        X = sb(esA, "X", [128, NT, D])
        P["X"] = X
        P["XR"] = [Res("X%d" % t) for t in range(NT)]
        xin_v = A["xin"].rearrange("(t p) d -> p t d", p=128)
        for t in range(NT):
            k.dma("sp", X[:, t, :], xin_v[:, t, :], writes=[P["XR"][t]])
        cond_stage()
        ffn_stage(P, "f1g", "f1u", "f1d", 0, 1, 0, "ln1g", "ln1b", final=(STOP_AFTER == "ffn1"), spill=True)
        k.barrier()
    if STOP_AFTER == "ffn1":
        return
    dsa_stage()
    if STOP_AFTER == "dsa":
        return
    with contextlib.ExitStack() as esC:
        X = sb(esC, "X2", [128, NT, D])
        P["X"] = X
        P["XR"] = [Res("X2_%d" % t) for t in range(NT)]
        P["reload"] = lambda t: k.dma("sp", P["X"][:, t, :], XS[t * 128:(t + 1) * 128, :], reads=[XS_r[t]], writes=[P["XR"][t]])
        ro_stage(P)
        if STOP_AFTER != "ro":
            ffn_stage(P, "f2g", "f2u", "f2d", 6, 7, 2, "ln3g", "ln3b", final=True, spill=False)
        k.barrier()


_PROGRAM = None
_LAST = None


def kernel(x_prompt, x_sample, c_prompt, c_sample, cache_k, cache_v, cache_idx_k, state_ret,
           w_cond, b_cond, ffn1_w_gate, ffn1_w_up, ffn1_w_down, ln1_g, ln1_b, w_in, w_out, ln2_g, ln2_b,
           ffn2_w_gate, ffn2_w_up, ffn2_w_down, ln3_g, ln3_b):
    global _PROGRAM
    f = lambda a: np.ascontiguousarray(np.asarray(a, dtype=np.float32))
    x_prompt, x_sample, c_prompt, c_sample = f(x_prompt), f(x_sample), f(c_prompt), f(c_sample)
    cache_k, cache_v, cache_idx_k, state_ret = f(cache_k), f(cache_v), f(cache_idx_k), f(state_ret)
    consts = _consts()
    shared = {
        "w_cond": f(w_cond)[0], "b_cond": f(b_cond)[0].reshape(72, 128),
        "f1g": f(ffn1_w_gate)[0], "f1u": f(ffn1_w_up)[0], "f1d": f(ffn1_w_down)[0],
        "ln1g": f(ln1_g)[0].reshape(1, D), "ln1b": f(ln1_b)[0].reshape(1, D),
        "w_in": f(w_in)[0], "w_out": f(w_out)[0],
        "ln2g": f(ln2_g)[0].reshape(1, D), "ln2b": f(ln2_b)[0].reshape(1, D),
        "f2g": f(ffn2_w_gate)[0], "f2u": f(ffn2_w_up)[0], "f2d": f(ffn2_w_down)[0],
        "ln3g": f(ln3_g)[0].reshape(1, D), "ln3b": f(ln3_b)[0].reshape(1, D),
    }
    for n, v in consts.items():
        shared["k_" + n] = v
    in_maps = []
    for i in range(8):
        m = dict(shared)
        m["xin"] = np.concatenate([x_prompt[i], x_sample[4 * i:4 * i + 4].reshape(128, D)], axis=0)
        m["c5"] = np.concatenate([c_prompt[i:i + 1], c_sample[4 * i:4 * i + 4]], axis=0)
        m["ck"] = cache_k[0, 4 * i:4 * i + 4].reshape(4, 2048, 512)
        m["cv"] = cache_v[0, 4 * i:4 * i + 4].reshape(4, 2048, 512)
        m["cik"] = cache_idx_k[0, 4 * i:4 * i + 4]
        m["sret"] = state_ret[0, 4 * i:4 * i + 4]
        in_maps.append(m)
    if _PROGRAM is None:
        _PROGRAM = build_program()
    res = run_bass_kernel_spmd(_PROGRAM, in_maps, core_ids=list(range(8)))
    R = res.results
    global _LAST
    _LAST = R
    y = np.stack([r["y"] for r in R])
    nk = np.stack([r["newk"] for r in R])
    nv = np.stack([r["newv"] for r in R])
    nik = np.stack([r["newik"] for r in R])
    stp = np.stack([r["stp"] for r in R])
    sts = np.stack([r["sts"] for r in R])
    y_prompt = y[:, :2048].copy()
    y_sample = y[:, 2048:].reshape(32, 32, D).copy()
    new_k_prompt = nk[:, :2048].reshape(1, 8, 2048, 8, 64).copy()
    new_v_prompt = nv[:, :2048].reshape(1, 8, 2048, 8, 64).copy()
    new_idx_k_prompt = nik[:, :2048].reshape(1, 8, 2048, 64).copy()
    state_ret_prompt = stp.reshape(1, 8, 4, 128, 128).copy()
    new_k_sample = nk[:, 2048:].reshape(1, 32, 32, 8, 64).copy()
    new_v_sample = nv[:, 2048:].reshape(1, 32, 32, 8, 64).copy()
    new_idx_k_sample = nik[:, 2048:].reshape(1, 32, 32, 64).copy()
    state_ret_sample = sts.reshape(1, 32, 4, 128, 128).copy()
    return (y_prompt, y_sample, new_k_prompt, new_v_prompt, new_idx_k_prompt, state_ret_prompt,
            new_k_sample, new_v_sample, new_idx_k_sample, state_ret_sample)
```
